# Optimizing a Trainium2 kernel written in Bass

```python
import jax, jax.numpy as jnp
from jax import lax
import numpy as np

D_MODEL = 1024
BATCH = 16
SEQ = 2048
DEPTH = 2

N_META = 16
HEAD_DIM = 64
N_Q_HEADS = D_MODEL // HEAD_DIM
N_KV_HEADS = N_Q_HEADS // 4
GQA_GROUP = N_Q_HEADS // N_KV_HEADS
WINDOW = 128
BLOCK = 128
POOL_WINDOWS = (2, 4, 8, 16)
N_POOL_GROUPS = len(POOL_WINDOWS)
POOL_GROUP_DIM = D_MODEL // N_POOL_GROUPS
D_FF = ((8 * D_MODEL // 3 + 255) // 256) * 256
N_MIXERS = 2
N_ATTN_LAYERS = (DEPTH + 1) // N_MIXERS
N_POOL_LAYERS = DEPTH // N_MIXERS
QKV_DIM = (N_Q_HEADS + 2 * N_KV_HEADS) * HEAD_DIM
RMS_EPS = 1e-6
NEG_INF = -1e30

kernel_name = 'hybrid_window_gqa_multiscale_pool_macaron'


def rms_norm(x, g):
    xf = x.astype(jnp.float32)
    y = xf * lax.rsqrt(jnp.mean(xf * xf, axis=-1, keepdims=True) + RMS_EPS)
    return (y * g.astype(jnp.float32)).astype(x.dtype)


def swiglu(x, w_gu, w_down):
    g, u = jnp.split(x @ w_gu, 2, axis=-1)
    return (jax.nn.silu(g) * u) @ w_down


def alibi_slopes(n):
    return jnp.exp2(-8.0 * jnp.arange(1, n + 1, dtype=jnp.float32) / n)


def windowed_gqa_attention(h, w_qkv, q_gain, k_gain, sink, w_o):
    B, L, _ = h.shape
    S = L - N_META
    nb = S // BLOCK
    f32 = jnp.float32
    q, k, v = jnp.split(h @ w_qkv, [N_Q_HEADS * HEAD_DIM, (N_Q_HEADS + N_KV_HEADS) * HEAD_DIM], axis=-1)
    q = rms_norm(q.reshape(B, L, N_KV_HEADS, GQA_GROUP, HEAD_DIM), q_gain) * (HEAD_DIM ** -0.5)
    k = rms_norm(k.reshape(B, L, N_KV_HEADS, HEAD_DIM), k_gain)
    v = v.reshape(B, L, N_KV_HEADS, HEAD_DIM)
    qm, qr = q[:, :N_META], q[:, N_META:]
    km, kr = k[:, :N_META], k[:, N_META:]
    vm, vr = v[:, :N_META], v[:, N_META:]
    slopes = alibi_slopes(N_Q_HEADS).reshape(N_KV_HEADS, GQA_GROUP)
    sink = sink.astype(f32).reshape(N_KV_HEADS, GQA_GROUP)

    qb = qr.reshape(B, nb, BLOCK, N_KV_HEADS, GQA_GROUP, HEAD_DIM)

    def band(t):
        tp = jnp.pad(t, ((0, 0), (BLOCK, BLOCK), (0, 0), (0, 0))).reshape(B, nb + 2, BLOCK, N_KV_HEADS, HEAD_DIM)
        return jnp.concatenate([tp[:, :-2], tp[:, 1:-1], tp[:, 2:]], axis=2)

    kb, vb = band(kr), band(vr)
    s_band = jnp.einsum('bnqkgd,bnckd->bnkgqc', qb, kb).astype(f32)
    s_meta = jnp.einsum('bnqkgd,bmkd->bnkgqm', qb, km).astype(f32)
    a = jnp.arange(BLOCK)
    c = jnp.arange(3 * BLOCK)
    blk = jnp.arange(nb)
    rel = c[None, :] - BLOCK - a[:, None]
    key_idx = blk[:, None] * BLOCK - BLOCK + c[None, :]
    valid = (jnp.abs(rel) <= WINDOW)[None] & ((key_idx >= 0) & (key_idx < S))[:, None, :]
    band_bias = -slopes[:, :, None, None] * jnp.abs(rel).astype(f32)
    s_band = jnp.where(valid[None, :, None, None], s_band + band_bias[None, None], NEG_INF)
    t_glob = N_META + blk[:, None] * BLOCK + a[None, :]
    meta_dist = (t_glob[:, :, None] - jnp.arange(N_META)[None, None, :]).astype(f32)
    s_meta = s_meta - slopes[:, :, None, None] * meta_dist[:, None, None]
    sink_col = jnp.broadcast_to(sink[:, :, None, None], s_band.shape[:-1] + (1,))
    p = jax.nn.softmax(jnp.concatenate([s_meta, s_band, sink_col], axis=-1), axis=-1)[..., :-1].astype(h.dtype)
    o_r = (jnp.einsum('bnkgqm,bmkd->bnqkgd', p[..., :N_META], vm)
           + jnp.einsum('bnkgqc,bnckd->bnqkgd', p[..., N_META:], vb))
    o_r = o_r.reshape(B, S, N_Q_HEADS * HEAD_DIM)

    k_mq = jnp.concatenate([km, kr[:, :BLOCK]], axis=1)
    v_mq = jnp.concatenate([vm, vr[:, :BLOCK]], axis=1)
    s_mq = jnp.einsum('bpkgd,bskd->bkgps', qm, k_mq).astype(f32)
    dist = jnp.abs(jnp.arange(N_META)[:, None] - jnp.arange(N_META + BLOCK)[None, :])
    s_mq = jnp.where(dist <= WINDOW, s_mq - slopes[:, :, None, None] * dist.astype(f32), NEG_INF)
    sink_mq = jnp.broadcast_to(sink[:, :, None, None], s_mq.shape[:-1] + (1,))
    p_mq = jax.nn.softmax(jnp.concatenate([s_mq, sink_mq], axis=-1), axis=-1)[..., :-1].astype(h.dtype)
    o_m = jnp.einsum('bkgps,bskd->bpkgd', p_mq, v_mq).reshape(B, N_META, N_Q_HEADS * HEAD_DIM)

    return jnp.concatenate([o_m, o_r], axis=1) @ w_o


def multiscale_pool_mixer(h, w_in, w_group, scale, w_out):
    B, L, _ = h.shape
    f32 = jnp.float32
    uf = (h @ w_in).astype(f32).reshape(B, L, N_POOL_GROUPS, POOL_GROUP_DIM)
    cs = jnp.pad(jnp.cumsum(uf, axis=1), ((0, 0), (1, 0), (0, 0), (0, 0)))
    t = jnp.arange(L)
    half = jnp.array(POOL_WINDOWS, dtype=jnp.int32) // 2
    lo = jnp.clip(t[:, None] - half[None, :], 0, L)
    hi = jnp.clip(t[:, None] + half[None, :], 0, L)
    g_idx = jnp.arange(N_POOL_GROUPS)[None, :]
    win_sum = cs[:, hi, g_idx] - cs[:, lo, g_idx]
    mean = win_sum / (hi - lo).astype(f32)[None, :, :, None]
    pooled = (mean - uf).astype(h.dtype)
    mixed = jnp.einsum('blgc,gcd->blgd', pooled, w_group).reshape(B, L, D_MODEL)
    return (mixed * scale) @ w_out


def setup_inputs(seed: int = 0) -> dict:
    key = jax.random.key(seed)
    ks = jax.random.split(key, 16)
    nrm = jax.random.normal
    f32 = jnp.float32
    return {
        'x': nrm(ks[0], (BATCH, SEQ, D_MODEL), f32),
        'meta_tokens': nrm(ks[1], (N_META, D_MODEL), f32),
        'ffn_norm': 1.0 + 0.02 * nrm(ks[2], (DEPTH, 2, D_MODEL), f32),
        'w_gate_up': nrm(ks[3], (DEPTH, 2, D_MODEL, 2 * D_FF), f32) * D_MODEL ** -0.5,
        'w_down': nrm(ks[4], (DEPTH, 2, D_FF, D_MODEL), f32) * D_FF ** -0.5,
        'mixer_norm': 1.0 + 0.02 * nrm(ks[5], (DEPTH, D_MODEL), f32),
        'w_qkv': nrm(ks[6], (N_ATTN_LAYERS, D_MODEL, QKV_DIM), f32) * D_MODEL ** -0.5,
        'q_norm': 1.0 + 0.02 * nrm(ks[7], (N_ATTN_LAYERS, HEAD_DIM), f32),
        'k_norm': 1.0 + 0.02 * nrm(ks[8], (N_ATTN_LAYERS, HEAD_DIM), f32),
        'sink_logit': nrm(ks[9], (N_ATTN_LAYERS, N_Q_HEADS), f32),
        'w_o': nrm(ks[10], (N_ATTN_LAYERS, N_Q_HEADS * HEAD_DIM, D_MODEL), f32) * (N_Q_HEADS * HEAD_DIM) ** -0.5,
        'w_pool_in': nrm(ks[11], (N_POOL_LAYERS, D_MODEL, D_MODEL), f32) * D_MODEL ** -0.5,
        'w_pool_group': nrm(ks[12], (N_POOL_LAYERS, N_POOL_GROUPS, POOL_GROUP_DIM, POOL_GROUP_DIM), f32) * POOL_GROUP_DIM ** -0.5,
        'pool_scale': 1.0 + 0.02 * nrm(ks[13], (N_POOL_LAYERS, D_MODEL), f32),
        'w_pool_out': nrm(ks[14], (N_POOL_LAYERS, D_MODEL, D_MODEL), f32) * D_MODEL ** -0.5,
    }


def reference(x, meta_tokens, ffn_norm, w_gate_up, w_down, mixer_norm, w_qkv, q_norm, k_norm,
              sink_logit, w_o, w_pool_in, w_pool_group, pool_scale, w_pool_out):
    B = x.shape[0]
    meta = jnp.broadcast_to(meta_tokens[None].astype(x.dtype), (B, N_META, D_MODEL))
    h = jnp.concatenate([meta, x], axis=1)
    for i in range(DEPTH):
        h = h + 0.5 * swiglu(rms_norm(h, ffn_norm[i, 0]), w_gate_up[i, 0], w_down[i, 0])
        hn = rms_norm(h, mixer_norm[i])
        j = i // N_MIXERS
        if i % N_MIXERS == 0:
            h = h + windowed_gqa_attention(hn, w_qkv[j], q_norm[j], k_norm[j], sink_logit[j], w_o[j])
        else:
            h = h + multiscale_pool_mixer(hn, w_pool_in[j], w_pool_group[j], pool_scale[j], w_pool_out[j])
        h = h + 0.5 * swiglu(rms_norm(h, ffn_norm[i, 1]), w_gate_up[i, 1], w_down[i, 1])
    return h[:, N_META:]
```

```python
import numpy as np
from contextlib import ExitStack
import concourse.bass as bass
import concourse.mybir as mybir
from concourse.bass_utils import run_bass_kernel_spmd

F32 = mybir.dt.float32
BF16 = mybir.dt.bfloat16
AF = mybir.ActivationFunctionType
ALU = mybir.AluOpType

D = 1024
SEQ = 2048
NM = 16
L = SEQ + NM
KC = 8
NJ = 22
EPS = 1e-6
NEG = -30000.0
TT = [(0, 16)] + [(16 + 512 * i, 512) for i in range(4)]
SLABS = [(0, 4), (4, 4), (8, 4), (12, 4), (16, 4), (20, 2)]
NSLOT = 3
ATT_SUB = 9
STAGE = 6


class Buf:
    __slots__ = ("name", "w", "r")

    def __init__(self, name):
        self.name = name
        self.w = None
        self.r = {}


class Eng:
    def __init__(self, name, sem):
        self.name = name
        self.sem = sem
        self.count = 0
        self.known = {}
        self.prog = []


class Ctx:
    def __init__(self, nc, sems):
        self.nc = nc
        self.E = {k: Eng(k, v) for k, v in sems.items()}
        self.semobj = dict(sems)
        self.dma_cnt = {}
        self.ninst = 0
        self.nwaits = 0

    def add_sem(self, key, handle):
        self.semobj[key] = handle
        self.dma_cnt[key] = 0

    def _waits(self, e, reads, writes):
        need = {}
        for b in reads:
            if b.w is not None and need.get(b.w[0], 0) < b.w[1]:
                need[b.w[0]] = b.w[1]
        for b in writes:
            if b.w is not None and need.get(b.w[0], 0) < b.w[1]:
                need[b.w[0]] = b.w[1]
            for s, v in b.r.items():
                if need.get(s, 0) < v:
                    need[s] = v
        out = []
        for s, v in need.items():
            if e.name == "pe" and s == "pe":
                continue
            if e.known.get(s, 0) < v:
                e.known[s] = v
                out.append((s, v))
        return out

    def _mark(self, tok, reads, writes):
        for b in reads:
            if b.r.get(tok[0], 0) < tok[1]:
                b.r[tok[0]] = tok[1]
        for b in writes:
            b.w = tok
            b.r = {}

    def op(self, eng, fn, reads=(), writes=()):
        e = self.E[eng]
        waits = self._waits(e, reads, writes)
        e.count += 1
        tok = (eng, e.count)
        semobj = self.semobj
        mysem = e.sem

        def run(h):
            for s, v in waits:
                h.wait_ge(semobj[s], v)
            fn(h).then_inc(mysem, 1)

        e.prog.append(run)
        self.ninst += 1
        self.nwaits += len(waits)
        self._mark(tok, reads, writes)

    def dma_group(self, queue, semkey, items, reads=(), writes=()):
        e = self.E[queue]
        waits = self._waits(e, reads, writes)
        self.dma_cnt[semkey] += 16 * len(items)
        tok = (semkey, self.dma_cnt[semkey])
        semobj = self.semobj

        def run(h):
            for s, v in waits:
                h.wait_ge(semobj[s], v)
            for o, i in items:
                h.dma_start(out=o, in_=i).then_inc(semobj[semkey], 16)

        e.prog.append(run)
        self.ninst += len(items)
        self._mark(tok, reads, writes)

    def wait_all(self, eng, bufs):
        e = self.E[eng]
        waits = self._waits(e, bufs, bufs)
        semobj = self.semobj

        def run(h):
            for s, v in waits:
                h.wait_ge(semobj[s], v)

        e.prog.append(run)

    def replay(self, block):
        E = self.E

        @block.sync
        def _(h):
            for f in E["sp"].prog:
                f(h)

        @block.tensor
        def _(h):
            for f in E["pe"].prog:
                f(h)

        @block.scalar
        def _(h):
            for f in E["act"].prog:
                f(h)

        @block.vector
        def _(h):
            for f in E["dve"].prog:
                f(h)

        @block.gpsimd
        def _(h):
            for f in E["pool"].prog:
                f(h)


def build_program(stage=6, phases=None, nffn=4, nj=NJ):
    nc = bass.Bass("TRN2", target_bir_lowering=False)

    def din(name, shape):
        return nc.dram_tensor(name, list(shape), F32, kind="ExternalInput").ap()

    xT = din("xT", [2, 128, KC, SEQ])
    metaT = din("metaT", [2, 128, KC, NM])
    wg = din("wg", [nffn, 128, nj, KC, 128])
    wu = din("wu", [nffn, 128, nj, KC, 128])
    wd = din("wd", [nffn, 128, nj, D])
    cst = din("cst", [128, 288])
    wq = din("wq", [128, KC, D])
    wk2 = din("wk2", [128, KC, 4, 128])
    wv = din("wv", [128, KC, 256])
    wo = din("wo", [128, KC, D])
    wpi = din("wpi", [128, KC, D])
    wpg = din("wpg", [128, 4, 2, 256])
    wpo = din("wpo", [128, KC, D])
    bband = din("bband", [4, 128, 3, 4, 128])
    bmeta = din("bmeta", [4, 16, 4, 128])
    bm0 = din("bm0", [4, 128, 4, 16])
    bmm = din("bmm", [4, 16, 4, 16])
    outT = nc.dram_tensor("outT", [2, 128, KC, SEQ], F32, kind="ExternalOutput").ap()

    slopes = [2.0 ** (-8.0 * (h + 1) / 16.0) for h in range(16)]

    with ExitStack() as st:
        ent = st.enter_context
        TOTAL_F32 = 209408 // 4
        sb = ent(nc.sbuf_tensor("sb", [128, TOTAL_F32], F32))
        ps = ent(nc.psum_tensor("ps", [128, 8, 512], F32))
        sems = {k: ent(nc.semaphore("s_" + k)) for k in ["pe", "act", "dve", "pool", "sp"]}
        c = Ctx(nc, sems)
        for k in ["ws0", "ws1", "ws2", "mw", "mq0", "mq1", "xin", "out", "tb", "cst"]:
            c.add_sem(k, ent(nc.semaphore("d_" + k)))
        block = ent(nc.Block())

        def carve(off, nbytes, dt=F32):
            a = sb[:, off // 4:(off + nbytes) // 4]
            if dt == BF16:
                a = a.bitcast(BF16)
            return a

        OFF_H, OFF_XN, OFF_C, OFF_NS, OFF_A = 0, 66048, 99072, 101120, 109312
        H = carve(OFF_H, 66048).rearrange("p (k t) -> p k t", k=KC)
        XN = carve(OFF_XN, 33024, BF16).rearrange("p (k t) -> p k t", k=KC)
        CST = carve(OFF_C, 288 * 4)
        gains = CST[:, 0:56].rearrange("p (g k) -> p g k", k=KC)
        qkg = CST[:, 56:58]
        sink_sb = CST[:, 58:74]
        es3 = CST[:, 74:90].rearrange("p (h o) -> p h o", o=1)
        invc = CST[:, 224:288].rearrange("p (g e) -> p g e", e=16)
        ones_bf = carve(OFF_C + 1280, 256, BF16)
        bd_ones = carve(OFF_C + 1536, 256, BF16)
        sq = [carve(OFF_NS + 1024 * i, 1024, BF16) for i in range(2)]
        rs = [carve(OFF_NS + 2048 + 2048 * i, 2048) for i in range(2)]
        sm = carve(OFF_NS + 6144, 2048)

        BH = [[Buf("H%d_%d" % (k, t)) for t in range(5)] for k in range(KC)]
        BXN = [[Buf("XN%d_%d" % (k, t)) for t in range(5)] for k in range(KC)]
        Bsq = [Buf("sq0"), Buf("sq1")]
        Brs = [Buf("rs0"), Buf("rs1")]
        Bsm = Buf("sm")
        BPS = [Buf("ps%d" % i) for i in range(8)]
        Bcst = Buf("cst")
        allH = [b for row in BH for b in row]
        allXN = [b for row in BXN for b in row]

        arena_live = []

        def new_mode(names):
            fence = {}
            for b in arena_live:
                if b.w is not None and fence.get(b.w[0], 0) < b.w[1]:
                    fence[b.w[0]] = b.w[1]
                for s, v in b.r.items():
                    if fence.get(s, 0) < v:
                        fence[s] = v
            del arena_live[:]
            out = {}
            for n in names:
                b = Buf(n)
                b.r = dict(fence)
                arena_live.append(b)
                out[n] = b
            return out

        rot = {"r3": 0, "gu": 0, "dn": 0, "pv": 0, "pt": 0, "tmp": 0, "rd": 0, "rs": 0}

        def nxt(key, n):
            v = rot[key]
            rot[key] = (v + 1) % n
            return v

        c.dma_group("sp", "cst", [(CST, cst)], writes=[Bcst])
        c.op("dve", lambda h: h.memset(ones_bf, 1.0), writes=[Bcst], reads=[Bcst])
        c.op("dve", lambda h: h.memset(bd_ones, 0.0), writes=[Bcst], reads=[Bcst])
        c.op("dve", lambda h: h.memset(bd_ones[0:64, 0:64], 1.0), writes=[Bcst], reads=[Bcst])
        c.op("dve", lambda h: h.memset(bd_ones[64:128, 64:128], 1.0), writes=[Bcst], reads=[Bcst])
        c.op("act", lambda h: h.activation(out=CST[:, 74:90], in_=sink_sb, func=AF.Exp), reads=[Bcst], writes=[Bcst])

        def rmsnorm(gi):
            for ti, (t0, n) in enumerate(TT):
                _norm_tile(gi, ti, t0, n)

        def _norm_tile(gi, ti, t0, n):
            for kc in range(KC):
                s_i = kc % 2
                c.op("act", lambda h, kc=kc, s_i=s_i: h.activation(out=sq[s_i][:, :n], in_=H[:, kc, t0:t0 + n], func=AF.Square),
                     reads=[BH[kc][ti]], writes=[Bsq[s_i]])
                c.op("pe", lambda h, kc=kc, s_i=s_i: h.matmul(ps[:, 7, :n], lhsT=ones_bf, rhs=sq[s_i][:, :n], start=(kc == 0), stop=(kc == KC - 1)),
                     reads=[Bsq[s_i], Bcst], writes=[BPS[7]])
            r_i = nxt("rs", 2)
            r = rs[r_i]
            c.op("dve", lambda h: h.tensor_scalar(out=r[:, :n], in0=ps[:, 7, :n], scalar1=1.0 / D, scalar2=EPS, op0=ALU.mult, op1=ALU.add),
                 reads=[BPS[7]], writes=[Brs[r_i]])
            c.op("act", lambda h: h.activation(out=r[:, :n], in_=r[:, :n], func=AF.Sqrt), reads=[Brs[r_i]], writes=[Brs[r_i]])
            c.op("dve", lambda h: h.reciprocal(out=r[:, :n], in_=r[:, :n]), reads=[Brs[r_i]], writes=[Brs[r_i]])
            for kc in range(KC):
                c.op("dve", lambda h, kc=kc: h.scalar_tensor_tensor(out=XN[:, kc, t0:t0 + n], in0=H[:, kc, t0:t0 + n], scalar=gains[:, gi, kc:kc + 1],
                                                                in1=r[:, :n], op0=ALU.mult, op1=ALU.mult),
                     reads=[BH[kc][ti], Brs[r_i], Bcst], writes=[BXN[kc][ti]])

        def ffn(n_ffn):
            names = ["slot%d" % i for i in range(NSLOT)] + ["act%d_%d" % (a, j) for a in range(2) for j in range(4)] + ["sg0", "sg1"]
            B = new_mode(names)
            slot_g, slot_u, slot_d = [], [], []
            for s in range(NSLOT):
                base = OFF_A + s * 24576
                slot_g.append(carve(base, 8192, BF16).rearrange("p (j k c) -> p j k c", j=4, k=KC))
                slot_u.append(carve(base + 8192, 8192, BF16).rearrange("p (j k c) -> p j k c", j=4, k=KC))
                slot_d.append(carve(base + 16384, 8192, BF16).rearrange("p (j m) -> p j m", j=4))
            actb = [carve(OFF_A + 73728 + 4096 * a, 4096, BF16).rearrange("p (j n) -> p j n", j=4) for a in range(2)]
            sgb = [carve(OFF_A + 81920 + 2048 * a, 2048) for a in range(2)]

            def load_slab(si):
                j0, S = SLABS[si]
                s = si % NSLOT
                c.dma_group("pool", "ws%d" % s,
                            [(slot_g[s][:, 0:S], wg[n_ffn, :, j0:j0 + S]),
                             (slot_u[s][:, 0:S], wu[n_ffn, :, j0:j0 + S]),
                             (slot_d[s][:, 0:S], wd[n_ffn, :, j0:j0 + S])],
                            writes=[B["slot%d" % s]])

            for si in range(NSLOT):
                load_slab(si)
            rmsnorm(n_ffn)
            steps = [(si, ti) for si in range(len(SLABS)) for ti in range(5)]

            def gu(idx):
                si, ti = steps[idx]
                j0, S = SLABS[si]
                s = si % NSLOT
                t0, n = TT[ti]
                ab = idx % 2
                for j in range(S):
                    p = nxt("gu", 2)

                    def mm(h, j=j, p=p):
                        for kc in range(KC):
                            h.matmul(ps[:, 2 * p, :n], lhsT=slot_g[s][:, j, kc, :], rhs=XN[:, kc, t0:t0 + n], start=(kc == 0), stop=(kc == KC - 1))
                        for kc in range(KC):
                            ins = h.matmul(ps[:, 2 * p + 1, :n], lhsT=slot_u[s][:, j, kc, :], rhs=XN[:, kc, t0:t0 + n], start=(kc == 0), stop=(kc == KC - 1))
                        return ins
                    c.op("pe", mm, reads=[B["slot%d" % s]] + [BXN[kc][ti] for kc in range(KC)], writes=[BPS[2 * p], BPS[2 * p + 1]])
                    c.op("act", lambda h, p=p: h.activation(out=sgb[p][:, :n], in_=ps[:, 2 * p, :n], func=AF.Silu),
                         reads=[BPS[2 * p]], writes=[B["sg%d" % p]])
                    c.op("dve", lambda h, p=p, j=j: h.tensor_tensor(out=actb[ab][:, j, :n], in0=sgb[p][:, :n], in1=ps[:, 2 * p + 1, :n], op=ALU.mult),
                         reads=[B["sg%d" % p], BPS[2 * p + 1]], writes=[B["act%d_%d" % (ab, j)]])

            def down(idx):
                si, ti = steps[idx]
                j0, S = SLABS[si]
                s = si % NSLOT
                t0, n = TT[ti]
                ab = idx % 2
                for m in range(KC):
                    bk = 4 + nxt("dn", 3)

                    def mm(h, m=m, bk=bk):
                        for j in range(S):
                            ins = h.matmul(ps[:, bk, :n], lhsT=slot_d[s][:, j, m * 128:(m + 1) * 128], rhs=actb[ab][:, j, :n], start=(j == 0), stop=(j == S - 1))
                        return ins
                    c.op("pe", mm, reads=[B["slot%d" % s]] + [B["act%d_%d" % (ab, j)] for j in range(S)], writes=[BPS[bk]])
                    c.op("dve", lambda h, m=m, bk=bk: h.scalar_tensor_tensor(out=H[:, m, t0:t0 + n], in0=ps[:, bk, :n], scalar=0.5, in1=H[:, m, t0:t0 + n],
                                                                         op0=ALU.mult, op1=ALU.add),
                         reads=[BPS[bk], BH[m][ti]], writes=[BH[m][ti]])
                if ti == 4 and si + NSLOT < len(SLABS):
                    load_slab(si + NSLOT)

            for idx in range(len(steps)):
                gu(idx)
                if idx > 0:
                    down(idx - 1)
            down(len(steps) - 1)

        def attention(gi):
            names = (["wq0", "wq1", "wk0", "wk1", "wv", "wo", "kT", "bias"] + ["qT%d" % i for i in range(2)] + ["V%d" % i for i in range(17)]
                     + ["PT%d" % i for i in range(8)] + ["tmp0", "tmp1", "rd0", "rd1"] + ["oT%d_%d" % (cc, t) for cc in range(2) for t in range(5)])
            B = new_mode(names)
            A = OFF_A
            wq_sb = [carve(A + 4096 * i, 4096, BF16).rearrange("p (k c) -> p k c", k=KC) for i in range(2)]
            wk_sb = [carve(A + 8192 + 2048 * i, 2048, BF16).rearrange("p (k c) -> p k c", k=KC) for i in range(2)]
            wv_sb = carve(A + 12288, 4096, BF16).rearrange("p (k c) -> p k c", k=KC)
            wo_sb = carve(A + 16384, 16384, BF16).rearrange("p (k c) -> p k c", k=KC)
            qT = carve(A + 32768, 8256, BF16).rearrange("p (c t) -> p c t", c=2)
            kTa = carve(A + 41024, 4128, BF16)
            kTb = carve(A + 95904, 4128, BF16)
            Vd = carve(A + 45152, 17408, BF16).rearrange("p (b k d) -> p b k d", b=17, k=4)
            Bband = carve(A + 62560, 6144).rearrange("p (r x) -> p r x", r=3)
            Bmeta = carve(A + 68704, 2048)
            Bm0 = carve(A + 70752, 256)
            Bmm = carve(A + 71008, 256)
            PT = [carve(A + 71264 + 1024 * i, 1024, BF16) for i in range(8)]
            tmp = [carve(A + 79456 + 2048 * i, 2048) for i in range(2)]
            rd = [carve(A + 83552 + 2048 * i, 2048) for i in range(2)]
            oT = carve(A + 87648, 8256, BF16).rearrange("p (c t) -> p c t", c=2)

            c.dma_group("pool", "mw", [(wv_sb, wv), (wo_sb, wo)], writes=[B["wv"], B["wo"]])
            c.op("dve", lambda h: h.memset(kTa[64:128, :], 0.0), writes=[B["kT"]])
            c.op("dve", lambda h: h.memset(kTb[0:64, :], 0.0), writes=[B["kT"]])

            def load_k(k):
                i = k % 2
                c.dma_group("pool", "mq%d" % i, [(wq_sb[i], wq[:, :, k * 256:(k + 1) * 256]), (wk_sb[i], wk2[:, :, k, :])],
                            writes=[B["wq%d" % i], B["wk%d" % i]])
            load_k(0)
            load_k(1)
            rmsnorm(gi)
            def vproj(blk):
                ks, nk = (0, 16) if blk == 0 else (16 + 128 * (blk - 1), 128)
                ti = 0 if blk == 0 else 1 + (blk - 1) // 4
                bk = nxt("r3", 3)

                def mm(h):
                    for kc in range(KC):
                        ins = h.matmul(ps[:nk, bk, 0:256], lhsT=XN[:, kc, ks:ks + nk], rhs=wv_sb[:, kc, :], start=(kc == 0), stop=(kc == KC - 1))
                    return ins
                c.op("pe", mm, reads=[B["wv"]] + [BXN[kc][ti] for kc in range(KC)], writes=[BPS[bk]])
                src = ps[:nk, bk, 0:256].rearrange("p (k d) -> p k d", k=4)
                c.op("act", lambda h: h.activation(out=Vd[:nk, blk, :, 0:64], in_=src, func=AF.Copy),
                     reads=[BPS[bk]], writes=[B["V%d" % blk]])
                c.op("dve", lambda h: h.tensor_copy(out=Vd[:nk, blk, :, 64:128], in_=src),
                     reads=[BPS[bk]], writes=[B["V%d" % blk]])
            for blk in range(17):
                vproj(blk)
            if ATT_SUB <= 1:
                return

            def proj(k, wi, ti, t0, n, which):
                bk = nxt("r3", 3)
                if which < 2:
                    def mm(h):
                        for kc in range(KC):
                            ins = h.matmul(ps[:, bk, :n], lhsT=wq_sb[wi][:, kc, which * 128:(which + 1) * 128], rhs=XN[:, kc, t0:t0 + n],
                                           start=(kc == 0), stop=(kc == KC - 1))
                        return ins
                    wb = B["wq%d" % wi]
                else:
                    def mm(h):
                        for kc in range(KC):
                            ins = h.matmul(ps[:, bk, :n], lhsT=wk_sb[wi][:, kc, :], rhs=XN[:, kc, t0:t0 + n], start=(kc == 0), stop=(kc == KC - 1))
                        return ins
                    wb = B["wk%d" % wi]
                c.op("pe", mm, reads=[wb] + [BXN[kc][ti] for kc in range(KC)], writes=[BPS[bk]])
                c.op("act", lambda h: h.activation(out=sq[0][:, :n], in_=ps[:, bk, :n], func=AF.Square), reads=[BPS[bk]], writes=[Bsq[0]])
                c.op("pe", lambda h: h.matmul(ps[:, 7, :n], lhsT=bd_ones, rhs=sq[0][:, :n], start=True, stop=True), reads=[Bsq[0], Bcst], writes=[BPS[7]])
                r_i = nxt("rs", 2)
                r = rs[r_i]
                if which < 2:
                    c.op("dve", lambda h: h.tensor_scalar(out=r[:, :n], in0=ps[:, 7, :n], scalar1=64.0 * EPS, scalar2=None, op0=ALU.add),
                         reads=[BPS[7]], writes=[Brs[r_i]])
                else:
                    c.op("dve", lambda h: h.tensor_scalar(out=r[:, :n], in0=ps[:, 7, :n], scalar1=1.0 / 64.0, scalar2=EPS, op0=ALU.mult, op1=ALU.add),
                         reads=[BPS[7]], writes=[Brs[r_i]])
                c.op("act", lambda h: h.activation(out=r[:, :n], in_=r[:, :n], func=AF.Sqrt), reads=[Brs[r_i]], writes=[Brs[r_i]])
                c.op("dve", lambda h: h.reciprocal(out=r[:, :n], in_=r[:, :n]), reads=[Brs[r_i]], writes=[Brs[r_i]])
                if which < 2:
                    c.op("dve", lambda h: h.scalar_tensor_tensor(out=qT[:, which, t0:t0 + n], in0=ps[:, bk, :n], scalar=qkg[:, 0:1],
                                                             in1=r[:, :n], op0=ALU.mult, op1=ALU.mult),
                         reads=[BPS[bk], Brs[r_i], Bcst], writes=[B["qT%d" % which]])
                else:
                    c.op("dve", lambda h: h.scalar_tensor_tensor(out=kTa[0:64, t0:t0 + n], in0=ps[0:64, bk, :n], scalar=qkg[0:64, 1:2],
                                                             in1=r[0:64, :n], op0=ALU.mult, op1=ALU.mult),
                         reads=[BPS[bk], Brs[r_i], Bcst], writes=[B["kT"]])
                    c.op("dve", lambda h: h.scalar_tensor_tensor(out=kTb[64:128, t0:t0 + n], in0=ps[64:128, bk, :n], scalar=qkg[64:128, 1:2],
                                                             in1=r[64:128, :n], op0=ALU.mult, op1=ALU.mult),
                         reads=[BPS[bk], Brs[r_i], Bcst], writes=[B["kT"]])

            def keygroup(k, qs, nq, W, bA, bB, blk, ks, nk, bias_ap, cq, first, last):
                bk = nxt("r3", 3)

                def mm(h):
                    for g in range(4):
                        cc, hh = g // 2, g % 2
                        ins = h.matmul(ps[:nk, bk, g * nq:(g + 1) * nq], lhsT=(kTa if hh == 0 else kTb)[:, ks:ks + nk],
                                       rhs=qT[:, cc, qs:qs + nq], start=True, stop=True)
                    return ins
                c.op("pe", mm, reads=[B["kT"], B["qT0"], B["qT1"]], writes=[BPS[bk]])
                tb = nxt("tmp", 2)
                if cq is None:
                    c.op("dve", lambda h: h.tensor_tensor(out=tmp[tb][:nk, :W], in0=ps[:nk, bk, :W], in1=bias_ap, op=ALU.add),
                         reads=[BPS[bk], B["bias"]], writes=[B["tmp%d" % tb]])
                else:
                    for g in range(4):
                        cval = -slopes[4 * k + g] * 128.0 * cq
                        c.op("dve", lambda h, g=g, cval=cval: h.scalar_tensor_tensor(
                            out=tmp[tb][:nk, g * nq:(g + 1) * nq], in0=ps[:nk, bk, g * nq:(g + 1) * nq], scalar=cval,
                            in1=bias_ap[:, g * nq:(g + 1) * nq], op0=ALU.add, op1=ALU.add),
                            reads=[BPS[bk], B["bias"]], writes=[B["tmp%d" % tb]])
                pt = nxt("pt", 8)
                c.op("act", lambda h: h.activation(out=PT[pt][:nk, :W], in_=tmp[tb][:nk, :W], func=AF.Exp),
                     reads=[B["tmp%d" % tb]], writes=[B["PT%d" % pt]])

                def mm2(h):
                    h.matmul(ps[:, bA, :W], lhsT=Vd[:nk, blk, k, :], rhs=PT[pt][:nk, :W], start=first, stop=last)
                    return h.matmul(ps[:, bB, :W], lhsT=ones_bf[:nk, :], rhs=PT[pt][:nk, :W], start=first, stop=last)
                c.op("pe", mm2, reads=[B["V%d" % blk], B["PT%d" % pt], Bcst], writes=[BPS[bA], BPS[bB]])

            def qgroup(k, qb, qs, nq):
                W = 4 * nq
                if qb < 0:
                    kgs = [(0, 0, 16, Bmm[0:16, 0:W], None), (1, 16, 128, Bm0[:, 0:W], None)]
                else:
                    kgs = [(0, 0, 16, Bmeta[0:16, :], qb)]
                    for r_ in (-1, 0, 1):
                        if 0 <= qb + r_ <= 15:
                            kgs.append((qb + r_ + 1, 16 + 128 * (qb + r_), 128, Bband[:, r_ + 1, :], None))
                pv = nxt("pv", 2)
                bA, bB = 3 + 2 * pv, 4 + 2 * pv
                oti = 0 if qb < 0 else 1 + qb // 4
                for gi_, (blk, ks, nk, bias_ap, cq) in enumerate(kgs):
                    keygroup(k, qs, nq, W, bA, bB, blk, ks, nk, bias_ap, cq, gi_ == 0, gi_ == len(kgs) - 1)
                ri = nxt("rd", 2)
                c.op("dve", lambda h: h.tensor_tensor(out=rd[ri][:, :W].rearrange("p (g n) -> p g n", g=4),
                                                      in0=ps[:, bB, :W].rearrange("p (g n) -> p g n", g=4),
                                                      in1=es3[:, 4 * k:4 * k + 4, :].to_broadcast([128, 4, nq]), op=ALU.add),
                     reads=[BPS[bB], Bcst], writes=[B["rd%d" % ri]])
                c.op("dve", lambda h: h.reciprocal(out=rd[ri][:, :W], in_=rd[ri][:, :W]), reads=[B["rd%d" % ri]], writes=[B["rd%d" % ri]])
                for hh in range(2):
                    lo = 64 * hh
                    c.op("dve", lambda h, hh=hh, lo=lo: h.tensor_tensor(
                        out=oT[lo:lo + 64, :, qs:qs + nq],
                        in0=ps[lo:lo + 64, bA, :W].rearrange("p (c h n) -> p c h n", c=2, h=2)[:, :, hh, :],
                        in1=rd[ri][lo:lo + 64, :W].rearrange("p (c h n) -> p c h n", c=2, h=2)[:, :, hh, :], op=ALU.mult),
                        reads=[BPS[bA], B["rd%d" % ri]], writes=[B["oT0_%d" % oti], B["oT1_%d" % oti]])

            def oproj(k, ti, t0, n):
                for m in range(KC):
                    bk = nxt("r3", 3)

                    def mm(h, m=m, bk=bk):
                        for cc in range(2):
                            ins = h.matmul(ps[:, bk, :n], lhsT=wo_sb[:, 2 * k + cc, m * 128:(m + 1) * 128], rhs=oT[:, cc, t0:t0 + n], start=(cc == 0), stop=(cc == 1))
                        return ins
                    c.op("pe", mm, reads=[B["wo"], B["oT0_%d" % ti], B["oT1_%d" % ti]], writes=[BPS[bk]])
                    c.op("dve", lambda h, m=m, bk=bk: h.tensor_tensor(out=H[:, m, t0:t0 + n], in0=ps[:, bk, :n], in1=H[:, m, t0:t0 + n], op=ALU.add),
                         reads=[BPS[bk], BH[m][ti]], writes=[BH[m][ti]])

            def att_k(k):
                wi = k % 2
                c.dma_group("sp", "tb", [(Bband.rearrange("p r x -> p (r x)"), bband[k].rearrange("p r g a -> p (r g a)")),
                                         (Bmeta[0:16, :], bmeta[k].rearrange("p g a -> p (g a)")),
                                         (Bm0, bm0[k].rearrange("p g a -> p (g a)")),
                                         (Bmm[0:16, :], bmm[k].rearrange("p g a -> p (g a)"))], writes=[B["bias"]])
                for ti, (t0, n) in enumerate(TT):
                    for which in range(3):
                        proj(k, wi, ti, t0, n, which)
                if k + 2 < 4:
                    load_k(k + 2)
                if ATT_SUB <= 2:
                    return
                for (qb, qs, nq) in [(-1, 0, 16)] + [(qb, 16 + 128 * qb, 128) for qb in range(16)]:
                    qgroup(k, qb, qs, nq)
                    if ATT_SUB <= 3:
                        return
                if ATT_SUB <= 4:
                    return
                for ti, (t0, n) in enumerate(TT):
                    oproj(k, ti, t0, n)

            for k in range(4):
                att_k(k)

        def poolmix(gi):
            B = new_mode(["wpi", "wpg", "wpo", "U", "T0", "T1"] + ["pl%d" % i for i in range(KC)])
            A = OFF_A
            wpi_sb = carve(A, 16384, BF16).rearrange("p (k c) -> p k c", k=KC)
            wpg_sb = carve(A + 16384, 4096, BF16).rearrange("p (g k c) -> p g k c", g=4, k=2)
            wpo_sb = carve(A + 20480, 16384, BF16).rearrange("p (k c) -> p k c", k=KC)
            pooled = carve(A + 36864, 33024, BF16).rearrange("p (k t) -> p k t", k=KC)
            U = carve(A + 69888, 8320)
            T = [carve(A + 78208, 8320), carve(A + 86528, 8320)]
            mixed = carve(A + 78208, 8192, BF16).rearrange("p (k n) -> p k n", k=KC)
            BT = [B["T0"], B["T1"]]
            c.dma_group("pool", "mw", [(wpi_sb, wpi), (wpg_sb, wpg), (wpo_sb, wpo)], writes=[B["wpi"], B["wpg"], B["wpo"]])
            rmsnorm(gi)
            c.op("dve", lambda h: h.memset(U[:, 0:8], 0.0), writes=[B["U"]])
            c.op("dve", lambda h: h.memset(U[:, 2072:2080], 0.0), writes=[B["U"]])
            def chunk(ch):
                g = ch // 2
                hw = 1 << g
                for ti, (t0, n) in enumerate(TT):
                    bk = nxt("r3", 3)

                    def mm(h, bk=bk, t0=t0, n=n):
                        for kc in range(KC):
                            ins = h.matmul(ps[:, bk, :n], lhsT=wpi_sb[:, kc, ch * 128:(ch + 1) * 128], rhs=XN[:, kc, t0:t0 + n], start=(kc == 0), stop=(kc == KC - 1))
                        return ins
                    c.op("pe", mm, reads=[B["wpi"]] + [BXN[kc][ti] for kc in range(KC)], writes=[BPS[bk]])
                    c.op("act", lambda h, bk=bk, t0=t0, n=n: h.activation(out=U[:, 8 + t0:8 + t0 + n], in_=ps[:, bk, :n], func=AF.Copy),
                         reads=[BPS[bk]], writes=[B["U"]])
                src, bsrc = U, B["U"]
                s_ = 1
                lvl = 0
                while s_ <= hw:
                    ln = 2080 - 2 * s_ + 1
                    dst, bdst = T[lvl % 2], BT[lvl % 2]
                    c.op("dve", lambda h, src=src, dst=dst, s_=s_, ln=ln: h.tensor_tensor(out=dst[:, 0:ln], in0=src[:, 0:ln], in1=src[:, s_:s_ + ln], op=ALU.add),
                         reads=[bsrc], writes=[bdst])
                    src, bsrc = dst, bdst
                    s_ *= 2
                    lvl += 1
                w_ = 2 * hw
                c.op("dve", lambda h: h.scalar_tensor_tensor(out=pooled[:, ch, :], in0=src[:, 8 - hw:8 - hw + L], scalar=1.0 / w_, in1=U[:, 8:8 + L],
                                                         op0=ALU.mult, op1=ALU.subtract),
                     reads=[bsrc, B["U"]], writes=[B["pl%d" % ch]])
                c.op("dve", lambda h: h.tensor_tensor(out=sm[:, 0:hw], in0=src[:, 8 - hw:8], in1=invc[:, g, 0:hw], op=ALU.mult),
                     reads=[bsrc, Bcst], writes=[Bsm])
                c.op("dve", lambda h: h.tensor_tensor(out=pooled[:, ch, 0:hw], in0=sm[:, 0:hw], in1=U[:, 8:8 + hw], op=ALU.subtract),
                     reads=[Bsm, B["U"]], writes=[B["pl%d" % ch]])
                if hw > 1:
                    nr = hw - 1
                    tb_ = L - hw + 1
                    c.op("dve", lambda h: h.tensor_tensor(out=sm[:, 16:16 + nr], in0=src[:, 8 + tb_ - hw:8 + tb_ - hw + nr],
                                                          in1=invc[:, g, 8:8 + nr], op=ALU.mult),
                         reads=[bsrc, Bcst], writes=[Bsm])
                    c.op("dve", lambda h: h.tensor_tensor(out=pooled[:, ch, tb_:tb_ + nr], in0=sm[:, 16:16 + nr], in1=U[:, 8 + tb_:8 + tb_ + nr], op=ALU.subtract),
                         reads=[Bsm, B["U"]], writes=[B["pl%d" % ch]])

            def mixtile(ti, t0, n):
                for g in range(4):
                    for mm_ in range(2):
                        bk = nxt("r3", 3)
                        oc = 2 * g + mm_

                        def mm(h, g=g, mm_=mm_, bk=bk):
                            for kk in range(2):
                                ins = h.matmul(ps[:, bk, :n], lhsT=wpg_sb[:, g, kk, mm_ * 128:(mm_ + 1) * 128], rhs=pooled[:, 2 * g + kk, t0:t0 + n], start=(kk == 0), stop=(kk == 1))
                            return ins
                        c.op("pe", mm, reads=[B["wpg"], B["pl%d" % (2 * g)], B["pl%d" % (2 * g + 1)]], writes=[BPS[bk]])
                        c.op("dve", lambda h, oc=oc, bk=bk: h.tensor_scalar(out=mixed[:, oc, :n], in0=ps[:, bk, :n], scalar1=gains[:, 6, oc:oc + 1], scalar2=None, op0=ALU.mult),
                             reads=[BPS[bk], Bcst], writes=[B["T0"]])
                for m in range(KC):
                    bk = nxt("r3", 3)

                    def mm(h, m=m, bk=bk):
                        for cc in range(KC):
                            ins = h.matmul(ps[:, bk, :n], lhsT=wpo_sb[:, cc, m * 128:(m + 1) * 128], rhs=mixed[:, cc, :n], start=(cc == 0), stop=(cc == KC - 1))
                        return ins
                    c.op("pe", mm, reads=[B["wpo"], B["T0"]], writes=[BPS[bk]])
                    c.op("dve", lambda h, m=m, bk=bk: h.tensor_tensor(out=H[:, m, t0:t0 + n], in0=ps[:, bk, :n], in1=H[:, m, t0:t0 + n], op=ALU.add),
                         reads=[BPS[bk], BH[m][ti]], writes=[BH[m][ti]])

            for ch in range(KC):
                chunk(ch)
            for ti, (t0, n) in enumerate(TT):
                mixtile(ti, t0, n)

        phases_sel = phases
        for seq in range(2):
            c.dma_group("sp", "xin", [(H[:, 2 * i:2 * i + 2, NM:L], xT[seq, :, 2 * i:2 * i + 2, :]) for i in range(4)] + [(H[:, :, 0:NM], metaT[seq])], writes=allH)
            phase_fns = [lambda: ffn(0), lambda: attention(4), lambda: ffn(1), lambda: ffn(2), lambda: poolmix(5), lambda: ffn(3)]
            for pi in (phases_sel if phases_sel is not None else range(min(stage, 6))):
                phase_fns[pi]()
            c.dma_group("sp", "out", [(outT[seq, :, 2 * i:2 * i + 2, :], H[:, 2 * i:2 * i + 2, NM:L]) for i in range(4)], reads=allH)
        c.wait_all("sp", allH)
        c.replay(block)
    return nc


def _const_tables():
    slopes = np.array([2.0 ** (-8.0 * (h + 1) / 16.0) for h in range(16)], dtype=np.float64).reshape(4, 4)
    a = np.arange(128)
    bband = np.zeros((4, 128, 3, 4, 128), np.float32)
    for r in range(3):
        rel = (r - 1) * 128 + a[:, None] - a[None, :]
        valid = np.abs(rel) <= 128
        for k in range(4):
            for g in range(4):
                bband[k, :, r, g, :] = np.where(valid, -slopes[k, g] * np.abs(rel), NEG)
    m = np.arange(16)
    bmeta = np.zeros((4, 16, 4, 128), np.float32)
    bmm = np.zeros((4, 16, 4, 16), np.float32)
    bm0 = np.zeros((4, 128, 4, 16), np.float32)
    for k in range(4):
        for g in range(4):
            bmeta[k, :, g, :] = -slopes[k, g] * (16 + a[None, :] - m[:, None])
            bmm[k, :, g, :] = -slopes[k, g] * np.abs(m[None, :] - m[:, None])
            dist = 16 + a[:, None] - m[None, :]
            bm0[k, :, g, :] = np.where(dist <= 128, -slopes[k, g] * dist, NEG)
    invc = np.ones((4, 16), np.float32)
    for g in range(4):
        hw = 1 << g
        for t in range(hw):
            invc[g, t] = 1.0 / (t + hw)
        for i in range(hw - 1):
            invc[g, 8 + i] = 1.0 / (2 * hw - 1 - i)
    return bband, bmeta, bm0, bmm, invc


def _fm(v):
    return np.ascontiguousarray(v.reshape(KC, 128).T)


def _wfm(w):
    return np.ascontiguousarray(w.reshape(KC, 128, -1).transpose(1, 0, 2))


_CACHE = {}


def kernel(x, meta_tokens, ffn_norm, w_gate_up, w_down, mixer_norm, w_qkv, q_norm, k_norm,
           sink_logit, w_o, w_pool_in, w_pool_group, pool_scale, w_pool_out):
    f = np.float32
    x = np.asarray(x, f)
    bband, bmeta, bm0, bmm, invc = _const_tables()
    wg = np.empty((4, 128, NJ, KC, 128), f)
    wu = np.empty((4, 128, NJ, KC, 128), f)
    wd = np.empty((4, 128, NJ, D), f)
    for i in range(2):
        for ff in range(2):
            n = 2 * i + ff
            wgu = np.asarray(w_gate_up[i, ff], f)
            wg[n] = wgu[:, :2816].reshape(KC, 128, NJ, 128).transpose(1, 2, 0, 3)
            wu[n] = wgu[:, 2816:].reshape(KC, 128, NJ, 128).transpose(1, 2, 0, 3)
            wd[n] = np.asarray(w_down[i, ff], f).reshape(NJ, 128, D).transpose(1, 0, 2)
    cst = np.zeros((128, 288), f)
    gl = [ffn_norm[0, 0], ffn_norm[0, 1], ffn_norm[1, 0], ffn_norm[1, 1], mixer_norm[0], mixer_norm[1], pool_scale[0]]
    for gi, v in enumerate(gl):
        cst[:, gi * 8:(gi + 1) * 8] = _fm(np.asarray(v, f))
    cst[:, 56] = np.tile(np.asarray(q_norm[0], f), 2)
    cst[:, 57] = np.tile(np.asarray(k_norm[0], f), 2)
    cst[:, 58:74] = np.asarray(sink_logit[0], f)[None, :]
    cst[:, 224:288] = invc.reshape(1, 64)
    wqkv = np.asarray(w_qkv[0], f)
    wq = _wfm(wqkv[:, :1024])
    wk = wqkv[:, 1024:1280].reshape(KC, 128, 4, 64).transpose(1, 0, 2, 3)
    wk2 = np.ascontiguousarray(np.concatenate([wk, wk], axis=3))
    wv = _wfm(wqkv[:, 1280:1536])
    wo = _wfm(np.asarray(w_o[0], f))
    wpi = _wfm(np.asarray(w_pool_in[0], f))
    wpg = np.ascontiguousarray(np.asarray(w_pool_group[0], f).reshape(4, 2, 128, 256).transpose(2, 0, 1, 3))
    wpo = _wfm(np.asarray(w_pool_out[0], f))
    metaT1 = np.ascontiguousarray(np.asarray(meta_tokens, f).T.reshape(KC, 128, NM).transpose(1, 0, 2))
    metaT = np.ascontiguousarray(np.stack([metaT1, metaT1]))
    shared = dict(metaT=metaT, wg=wg, wu=wu, wd=wd, cst=cst, wq=wq, wk2=wk2, wv=wv, wo=wo, wpi=wpi, wpg=wpg, wpo=wpo,
                  bband=bband, bmeta=bmeta, bm0=bm0, bmm=bmm)
    in_maps = []
    for core in range(8):
        xs = x[2 * core:2 * core + 2]
        xTc = np.ascontiguousarray(xs.reshape(2, SEQ, KC, 128).transpose(0, 3, 2, 1))
        d = dict(shared)
        d["xT"] = xTc
        in_maps.append(d)
    if STAGE not in _CACHE:
        _CACHE[STAGE] = build_program(STAGE)
    nc = _CACHE[STAGE]
    res = run_bass_kernel_spmd(nc, in_maps, core_ids=list(range(8)))
    out = np.empty((16, SEQ, D), f)
    for core in range(8):
        o = res.results[core]["outT"]
        out[2 * core:2 * core + 2] = o.transpose(0, 3, 2, 1).reshape(2, SEQ, D)
    return out
```

```python
import numpy as np
from contextlib import ExitStack
import concourse.bass as bass
import concourse.mybir as mybir
from concourse.bass_utils import run_bass_kernel_spmd

F32 = mybir.dt.float32
BF16 = mybir.dt.bfloat16
AF = mybir.ActivationFunctionType
ALU = mybir.AluOpType

D = 1024
SEQ = 2048
NM = 16
L = SEQ + NM
KC = 8
NJ = 22
EPS = 1e-6
NEG = -30000.0
TT = [(0, 16)] + [(16 + 512 * i, 512) for i in range(4)]
SLABS = [(0, 4), (4, 4), (8, 4), (12, 4), (16, 4), (20, 2)]
NSLOT = 3
ATT_SUB = 9
STAGE = 6


class Buf:
    __slots__ = ("name", "w", "r", "lo", "hi", "ov", "grp")

    def __init__(self, name, lo=None, hi=None, grp=None):
        self.name = name
        self.w = None
        self.r = {}
        self.lo = lo
        self.hi = hi
        self.ov = []
        self.grp = grp or name


class Eng:
    def __init__(self, name, sem):
        self.name = name
        self.sem = sem
        self.count = 0
        self.known = {}
        self.prog = []


class Ctx:
    def __init__(self, nc, sems):
        self.nc = nc
        self.E = {k: Eng(k, v) for k, v in sems.items()}
        self.semobj = dict(sems)
        self.dma_cnt = {}
        self.ninst = 0
        self.nwaits = 0
        self.ranged = {}

    def rbuf(self, name, lo, size, grp=None):
        key = (name, lo, size)
        if key in self.ranged:
            return self.ranged[key]
        b = Buf(name, lo, lo + size, grp)
        for o in self.ranged.values():
            if o.lo < b.hi and b.lo < o.hi and o.grp != b.grp:
                o.ov.append(b)
                b.ov.append(o)
        self.ranged[key] = b
        return b

    def add_sem(self, key, handle):
        self.semobj[key] = handle
        self.dma_cnt[key] = 0

    def _waits(self, e, reads, writes):
        need = {}
        for b in reads:
            if b.w is not None and need.get(b.w[0], 0) < b.w[1]:
                need[b.w[0]] = b.w[1]
        for b in writes:
            if b.w is not None and need.get(b.w[0], 0) < b.w[1]:
                need[b.w[0]] = b.w[1]
            for s, v in b.r.items():
                if need.get(s, 0) < v:
                    need[s] = v
            for o in b.ov:
                if o.w is not None and need.get(o.w[0], 0) < o.w[1]:
                    need[o.w[0]] = o.w[1]
                for s, v in o.r.items():
                    if need.get(s, 0) < v:
                        need[s] = v
        out = []
        for s, v in need.items():
            if e.name == "pe" and s == "pe":
                continue
            if e.known.get(s, 0) < v:
                e.known[s] = v
                out.append((s, v))
        return out

    def _mark(self, tok, reads, writes):
        for b in reads:
            if b.r.get(tok[0], 0) < tok[1]:
                b.r[tok[0]] = tok[1]
        for b in writes:
            b.w = tok
            b.r = {}

    def op(self, eng, fn, reads=(), writes=()):
        e = self.E[eng]
        waits = self._waits(e, reads, writes)
        e.count += 1
        tok = (eng, e.count)
        semobj = self.semobj
        mysem = e.sem

        def run(h):
            for s, v in waits:
                h.wait_ge(semobj[s], v)
            fn(h).then_inc(mysem, 1)

        e.prog.append(run)
        self.ninst += 1
        self.nwaits += len(waits)
        self._mark(tok, reads, writes)

    def dma_group(self, queue, semkey, items, reads=(), writes=()):
        e = self.E[queue]
        waits = self._waits(e, reads, writes)
        self.dma_cnt[semkey] += 16 * len(items)
        tok = (semkey, self.dma_cnt[semkey])
        semobj = self.semobj

        def run(h):
            for s, v in waits:
                h.wait_ge(semobj[s], v)
            for o, i in items:
                h.dma_start(out=o, in_=i).then_inc(semobj[semkey], 16)

        e.prog.append(run)
        self.ninst += len(items)
        self._mark(tok, reads, writes)

    def wait_all(self, eng, bufs):
        e = self.E[eng]
        waits = self._waits(e, bufs, bufs)
        semobj = self.semobj

        def run(h):
            for s, v in waits:
                h.wait_ge(semobj[s], v)

        e.prog.append(run)

    def replay(self, block):
        E = self.E

        @block.sync
        def _(h):
            for f in E["sp"].prog:
                f(h)

        @block.tensor
        def _(h):
            for f in E["pe"].prog:
                f(h)

        @block.scalar
        def _(h):
            for f in E["act"].prog:
                f(h)

        @block.vector
        def _(h):
            for f in E["dve"].prog:
                f(h)

        @block.gpsimd
        def _(h):
            for f in E["pool"].prog:
                f(h)


def build_program(stage=6, phases=None, nffn=4, nj=NJ):
    nc = bass.Bass("TRN2", target_bir_lowering=False)

    def din(name, shape):
        return nc.dram_tensor(name, list(shape), F32, kind="ExternalInput").ap()

    xT = din("xT", [2, 128, KC, SEQ])
    metaT = din("metaT", [2, 128, KC, NM])
    wg = din("wg", [nffn, 128, nj, KC, 128])
    wu = din("wu", [nffn, 128, nj, KC, 128])
    wd = din("wd", [nffn, 128, nj, D])
    cst = din("cst", [128, 288])
    cbias_d = din("cbias", [128, 256])
    wq = din("wq", [128, KC, D])
    wk2 = din("wk2", [128, KC, 4, 128])
    wv = din("wv", [128, KC, 256])
    wo = din("wo", [128, KC, D])
    wpi = din("wpi", [128, KC, D])
    wpg = din("wpg", [128, 4, 2, 256])
    wpo = din("wpo", [128, KC, D])
    bband = din("bband", [4, 128, 3, 4, 128])
    bmeta = din("bmeta", [4, 16, 4, 128])
    bm0 = din("bm0", [4, 128, 4, 16])
    bmm = din("bmm", [4, 16, 4, 16])
    outT = nc.dram_tensor("outT", [2, 128, KC, SEQ], F32, kind="ExternalOutput").ap()

    slopes = [2.0 ** (-8.0 * (h + 1) / 16.0) for h in range(16)]

    with ExitStack() as st:
        ent = st.enter_context
        TOTAL_F32 = 212480 // 4
        sb = ent(nc.sbuf_tensor("sb", [128, TOTAL_F32], F32))
        ps = ent(nc.psum_tensor("ps", [128, 8, 512], F32))
        sems = {k: ent(nc.semaphore("s_" + k)) for k in ["pe", "act", "dve", "pool", "sp"]}
        c = Ctx(nc, sems)
        for k in ["ws0", "ws1", "ws2", "mw", "mo", "mq0", "mq1", "xin", "out", "tb", "cst"]:
            c.add_sem(k, ent(nc.semaphore("d_" + k)))
        block = ent(nc.Block())

        def carve(off, nbytes, dt=F32):
            a = sb[:, off // 4:(off + nbytes) // 4]
            if dt == BF16:
                a = a.bitcast(BF16)
            return a

        OFF_H, OFF_XN, OFF_C, OFF_NS, OFF_A = 0, 66048, 99072, 101120, 109312
        H = carve(OFF_H, 66048).rearrange("p (k t) -> p k t", k=KC)
        XN = carve(OFF_XN, 33024, BF16).rearrange("p (k t) -> p k t", k=KC)
        CST = carve(OFF_C, 288 * 4)
        gains = CST[:, 0:56].rearrange("p (g k) -> p g k", k=KC)
        qkg = CST[:, 56:58]
        sink_sb = CST[:, 58:74]
        es3 = CST[:, 74:90].rearrange("p (h o) -> p h o", o=1)
        invc = CST[:, 224:288].rearrange("p (g e) -> p g e", e=16)
        ones_bf = carve(OFF_C + 1280, 256, BF16)
        bd_ones = carve(OFF_C + 1536, 256, BF16)
        sq = [carve(OFF_NS + 1024 * i, 1024, BF16) for i in range(2)]
        rs = [carve(OFF_NS + 2048 + 2048 * i, 2048) for i in range(2)]
        sm = carve(OFF_NS + 6144, 2048)
        cbias = sm[:, 256:512].rearrange("p (h q) -> p h q", q=16)

        BH = [[Buf("H%d_%d" % (k, t)) for t in range(5)] for k in range(KC)]
        BXN = [[Buf("XN%d_%d" % (k, t)) for t in range(5)] for k in range(KC)]
        Bsq = [Buf("sq0"), Buf("sq1")]
        Brs = [Buf("rs0"), Buf("rs1")]
        Bsm = Buf("sm")
        BPS = [Buf("ps%d" % i) for i in range(8)]
        Bcst = Buf("cst")
        allH = [b for row in BH for b in row]
        allXN = [b for row in BXN for b in row]

        pending = {"gi": None}

        rot = {"r4": 0, "r3": 0, "gu": 0, "dn": 0, "pv": 0, "pt": 0, "tmp": 0, "rd": 0, "rs": 0}

        def nxt(key, n):
            v = rot[key]
            rot[key] = (v + 1) % n
            return v

        c.dma_group("sp", "cst", [(CST, cst), (sm[:, 256:512], cbias_d)], writes=[Bcst])
        c.op("dve", lambda h: h.memset(ones_bf, 1.0), writes=[Bcst], reads=[Bcst])
        c.op("dve", lambda h: h.memset(bd_ones, 0.0), writes=[Bcst], reads=[Bcst])
        c.op("dve", lambda h: h.memset(bd_ones[0:64, 0:64], 1.0), writes=[Bcst], reads=[Bcst])
        c.op("dve", lambda h: h.memset(bd_ones[64:128, 64:128], 1.0), writes=[Bcst], reads=[Bcst])
        c.op("act", lambda h: h.activation(out=CST[:, 74:90], in_=sink_sb, func=AF.Exp), reads=[Bcst], writes=[Bcst])

        def rmsnorm(gi):
            if pending["gi"] == gi:
                pending["gi"] = None
                return
            for ti, (t0, n) in enumerate(TT):
                _norm_tile(gi, ti, t0, n)

        def _norm_tile(gi, ti, t0, n):
            for kc in range(KC):
                s_i = kc % 2
                c.op("act", lambda h, kc=kc, s_i=s_i: h.activation(out=sq[s_i][:, :n], in_=H[:, kc, t0:t0 + n], func=AF.Square),
                     reads=[BH[kc][ti]], writes=[Bsq[s_i]])
                c.op("pe", lambda h, kc=kc, s_i=s_i: h.matmul(ps[:, 7, :n], lhsT=ones_bf, rhs=sq[s_i][:, :n], start=(kc == 0), stop=(kc == KC - 1)),
                     reads=[Bsq[s_i], Bcst], writes=[BPS[7]])
            r_i = nxt("rs", 2)
            r = rs[r_i]
            c.op("act", lambda h: h.activation(out=r[:, :n], in_=ps[:, 7, :n], func=AF.Ln, scale=1.0 / D, bias=CST[:, 90:91]),
                 reads=[BPS[7], Bcst], writes=[Brs[r_i]])
            c.op("act", lambda h: h.activation(out=r[:, :n], in_=r[:, :n], func=AF.Exp, scale=-0.5), reads=[Brs[r_i]], writes=[Brs[r_i]])
            for kc in range(KC):
                c.op("dve", lambda h, kc=kc: h.scalar_tensor_tensor(out=XN[:, kc, t0:t0 + n], in0=H[:, kc, t0:t0 + n], scalar=gains[:, gi, kc:kc + 1],
                                                                in1=r[:, :n], op0=ALU.mult, op1=ALU.mult),
                     reads=[BH[kc][ti], Brs[r_i], Bcst], writes=[BXN[kc][ti]])

        def ffn(n_ffn, next_gi=None):
            B = {}
            for s_ in range(NSLOT):
                B["slot%d" % s_] = c.rbuf("ffn.slot%d" % s_, s_ * 24576, 24576)
            for a_ in range(2):
                for j_ in range(4):
                    B["act%d_%d" % (a_, j_)] = c.rbuf("ffn.act%d_%d" % (a_, j_), 73728 + 4096 * a_ + 1024 * j_, 1024)
                B["sg%d" % a_] = c.rbuf("ffn.sg%d" % a_, 81920 + 2048 * a_, 2048)
            slot_g, slot_u, slot_d = [], [], []
            for s in range(NSLOT):
                base = OFF_A + s * 24576
                slot_g.append(carve(base, 8192, BF16).rearrange("p (j k c) -> p j k c", j=4, k=KC))
                slot_u.append(carve(base + 8192, 8192, BF16).rearrange("p (j k c) -> p j k c", j=4, k=KC))
                slot_d.append(carve(base + 16384, 8192, BF16).rearrange("p (j m) -> p j m", j=4))
            actb = [carve(OFF_A + 73728 + 4096 * a, 4096, BF16).rearrange("p (j n) -> p j n", j=4) for a in range(2)]
            sgb = [carve(OFF_A + 81920 + 2048 * a, 2048) for a in range(2)]

            def load_slab(si):
                j0, S = SLABS[si]
                s = si % NSLOT
                c.dma_group("pool", "ws%d" % s,
                            [(slot_g[s][:, 0:S], wg[n_ffn, :, j0:j0 + S]),
                             (slot_u[s][:, 0:S], wu[n_ffn, :, j0:j0 + S]),
                             (slot_d[s][:, 0:S], wd[n_ffn, :, j0:j0 + S])],
                            writes=[B["slot%d" % s]])

            for si in range(NSLOT):
                load_slab(si)
            rmsnorm(n_ffn)
            steps = [(si, ti) for si in range(len(SLABS)) for ti in range(5)]

            def gu(idx):
                si, ti = steps[idx]
                j0, S = SLABS[si]
                s = si % NSLOT
                t0, n = TT[ti]
                ab = idx % 2
                for j in range(S):
                    p = nxt("gu", 2)

                    def mm(h, j=j, p=p):
                        for kc in range(KC):
                            h.matmul(ps[:, 2 * p, :n], lhsT=slot_g[s][:, j, kc, :], rhs=XN[:, kc, t0:t0 + n], start=(kc == 0), stop=(kc == KC - 1))
                        for kc in range(KC):
                            ins = h.matmul(ps[:, 2 * p + 1, :n], lhsT=slot_u[s][:, j, kc, :], rhs=XN[:, kc, t0:t0 + n], start=(kc == 0), stop=(kc == KC - 1))
                        return ins
                    c.op("pe", mm, reads=[B["slot%d" % s]] + [BXN[kc][ti] for kc in range(KC)], writes=[BPS[2 * p], BPS[2 * p + 1]])
                    c.op("act", lambda h, p=p: h.activation(out=sgb[p][:, :n], in_=ps[:, 2 * p, :n], func=AF.Silu),
                         reads=[BPS[2 * p]], writes=[B["sg%d" % p]])
                    c.op("dve", lambda h, p=p, j=j: h.tensor_tensor(out=actb[ab][:, j, :n], in0=sgb[p][:, :n], in1=ps[:, 2 * p + 1, :n], op=ALU.mult),
                         reads=[B["sg%d" % p], BPS[2 * p + 1]], writes=[B["act%d_%d" % (ab, j)]])

            def down(idx):
                si, ti = steps[idx]
                j0, S = SLABS[si]
                s = si % NSLOT
                t0, n = TT[ti]
                ab = idx % 2
                for m in range(KC):
                    bk = 4 + nxt("dn", 3)

                    def mm(h, m=m, bk=bk):
                        for j in range(S):
                            ins = h.matmul(ps[:, bk, :n], lhsT=slot_d[s][:, j, m * 128:(m + 1) * 128], rhs=actb[ab][:, j, :n], start=(j == 0), stop=(j == S - 1))
                        return ins
                    c.op("pe", mm, reads=[B["slot%d" % s]] + [B["act%d_%d" % (ab, j)] for j in range(S)], writes=[BPS[bk]])
                    c.op("dve", lambda h, m=m, bk=bk: h.scalar_tensor_tensor(out=H[:, m, t0:t0 + n], in0=ps[:, bk, :n], scalar=0.5, in1=H[:, m, t0:t0 + n],
                                                                         op0=ALU.mult, op1=ALU.add),
                         reads=[BPS[bk], BH[m][ti]], writes=[BH[m][ti]])
                if ti == 4 and si + NSLOT < len(SLABS):
                    load_slab(si + NSLOT)
                if next_gi is not None and si == len(SLABS) - 1:
                    _norm_tile(next_gi, ti, t0, n)

            for idx in range(len(steps)):
                gu(idx)
                if idx > 0:
                    down(idx - 1)
            down(len(steps) - 1)
            pending["gi"] = next_gi

        def attention(gi, next_gi=None):
            A = OFF_A
            B = {}

            def rb(name, off, size, grp=None):
                B[name] = c.rbuf("att." + name, off, size, grp and "att." + grp)
            Vd = carve(A + 0, 17408, BF16).rearrange("p (b k d) -> p b k d", b=17, k=4)
            for i in range(17):
                rb("V%d" % i, 1024 * i, 1024)
            kTa = carve(A + 17408, 4128, BF16)
            rb("kTa", 17408, 4128)
            PT_off = [21536, 22560] + [45664 + 1024 * i for i in range(5)] + [100000]
            PT = [carve(A + o, 1024, BF16) for o in PT_off]
            for i, o in enumerate(PT_off):
                rb("PT%d" % i, o, 1024)
            qT = carve(A + 24576, 8256, BF16).rearrange("p (c t) -> p c t", c=2)
            rb("qT0", 24576, 4128)
            rb("qT1", 24576 + 4128, 4128)
            kTb = carve(A + 32832, 4128, BF16)
            rb("kTb", 32832, 4128)
            Bband = carve(A + 36960, 6144).rearrange("p (r x) -> p r x", r=3)
            Bmeta = carve(A + 43104, 2048)
            Bm0 = carve(A + 45152, 256)
            Bmm = carve(A + 45408, 256)
            rb("bias", 36960, 8704)
            tmp = [carve(A + 50784 + 2048 * i, 2048) for i in range(2)]
            rd = [carve(A + 54880 + 2048 * i, 2048) for i in range(2)] + [carve(A + 101024, 2048)]
            rb("rd2", 101024, 2048)
            wq_sb = [carve(A + 58976 + 4096 * i, 4096, BF16).rearrange("p (k c) -> p k c", k=KC) for i in range(2)]
            wk_sb = [carve(A + 67168 + 2048 * i, 2048, BF16).rearrange("p (k c) -> p k c", k=KC) for i in range(2)]
            for i in range(2):
                rb("tmp%d" % i, 50784 + 2048 * i, 2048)
                rb("rd%d" % i, 54880 + 2048 * i, 2048)
                rb("wq%d" % i, 58976 + 4096 * i, 4096)
                rb("wk%d" % i, 67168 + 2048 * i, 2048)
            oT = carve(A + 71264, 8256, BF16).rearrange("p (c t) -> p c t", c=2)
            for cc_ in range(2):
                for t_ in range(5):
                    rb("oT%d_%d" % (cc_, t_), 71264, 8256, grp="oT")
            wo_sb = carve(A + 79520, 16384, BF16).rearrange("p (k c) -> p k c", k=KC)
            rb("wo", 79520, 16384)
            wv_sb = carve(A + 95904, 4096, BF16).rearrange("p (k c) -> p k c", k=KC)
            rb("wv", 95904, 4096)
            NPT = len(PT)

            c.dma_group("pool", "mw", [(wv_sb, wv)], writes=[B["wv"]])
            c.op("dve", lambda h: h.memset(Vd[:, :, :, 64:128], 1.0), writes=[B["V%d" % i] for i in range(17)])
            c.op("dve", lambda h: h.memset(kTa[64:128, :], 0.0), writes=[B["kTa"]])
            c.op("dve", lambda h: h.memset(kTb[0:64, :], 0.0), writes=[B["kTb"]])

            def load_k(k):
                i = k % 2
                c.dma_group("pool", "mq%d" % i, [(wq_sb[i], wq[:, :, k * 256:(k + 1) * 256]), (wk_sb[i], wk2[:, :, k, :])],
                            writes=[B["wq%d" % i], B["wk%d" % i]])
            load_k(0)
            load_k(1)
            c.dma_group("pool", "mo", [(wo_sb, wo)], writes=[B["wo"]])
            rmsnorm(gi)
            def vproj(blk):
                ks, nk = (0, 16) if blk == 0 else (16 + 128 * (blk - 1), 128)
                ti = 0 if blk == 0 else 1 + (blk - 1) // 4
                bk = nxt("r3", 3)

                def mm(h):
                    for kc in range(KC):
                        ins = h.matmul(ps[:nk, bk, 0:256], lhsT=XN[:, kc, ks:ks + nk], rhs=wv_sb[:, kc, :], start=(kc == 0), stop=(kc == KC - 1))
                    return ins
                c.op("pe", mm, reads=[B["wv"]] + [BXN[kc][ti] for kc in range(KC)], writes=[BPS[bk]])
                src = ps[:nk, bk, 0:256].rearrange("p (k d) -> p k d", k=4)
                c.op("act", lambda h: h.activation(out=Vd[:nk, blk, :, 0:64], in_=src, func=AF.Copy),
                     reads=[BPS[bk]], writes=[B["V%d" % blk]])
            for blk in range(17):
                vproj(blk)
            if ATT_SUB <= 1:
                return

            def proj(k, wi, ti, t0, n, which):
                bk = nxt("r3", 3)
                if which < 2:
                    def mm(h):
                        for kc in range(KC):
                            ins = h.matmul(ps[:, bk, :n], lhsT=wq_sb[wi][:, kc, which * 128:(which + 1) * 128], rhs=XN[:, kc, t0:t0 + n],
                                           start=(kc == 0), stop=(kc == KC - 1))
                        return ins
                    wb = B["wq%d" % wi]
                else:
                    def mm(h):
                        for kc in range(KC):
                            ins = h.matmul(ps[:, bk, :n], lhsT=wk_sb[wi][:, kc, :], rhs=XN[:, kc, t0:t0 + n], start=(kc == 0), stop=(kc == KC - 1))
                        return ins
                    wb = B["wk%d" % wi]
                c.op("pe", mm, reads=[wb] + [BXN[kc][ti] for kc in range(KC)], writes=[BPS[bk]])
                c.op("act", lambda h: h.activation(out=sq[0][:, :n], in_=ps[:, bk, :n], func=AF.Square), reads=[BPS[bk]], writes=[Bsq[0]])
                c.op("pe", lambda h: h.matmul(ps[:, 7, :n], lhsT=bd_ones, rhs=sq[0][:, :n], start=True, stop=True), reads=[Bsq[0], Bcst], writes=[BPS[7]])
                r_i = nxt("rs", 2)
                r = rs[r_i]
                if which < 2:
                    c.op("act", lambda h: h.activation(out=r[:, :n], in_=ps[:, 7, :n], func=AF.Ln, bias=CST[:, 91:92]),
                         reads=[BPS[7], Bcst], writes=[Brs[r_i]])
                else:
                    c.op("act", lambda h: h.activation(out=r[:, :n], in_=ps[:, 7, :n], func=AF.Ln, scale=1.0 / 64.0, bias=CST[:, 90:91]),
                         reads=[BPS[7], Bcst], writes=[Brs[r_i]])
                c.op("act", lambda h: h.activation(out=r[:, :n], in_=r[:, :n], func=AF.Exp, scale=-0.5), reads=[Brs[r_i]], writes=[Brs[r_i]])
                if which < 2:
                    c.op("dve", lambda h: h.scalar_tensor_tensor(out=qT[:, which, t0:t0 + n], in0=ps[:, bk, :n], scalar=qkg[:, 0:1],
                                                             in1=r[:, :n], op0=ALU.mult, op1=ALU.mult),
                         reads=[BPS[bk], Brs[r_i], Bcst], writes=[B["qT%d" % which]])
                else:
                    c.op("dve", lambda h: h.scalar_tensor_tensor(out=kTa[0:64, t0:t0 + n], in0=ps[0:64, bk, :n], scalar=qkg[0:64, 1:2],
                                                             in1=r[0:64, :n], op0=ALU.mult, op1=ALU.mult),
                         reads=[BPS[bk], Brs[r_i], Bcst], writes=[B["kTa"]])
                    c.op("dve", lambda h: h.scalar_tensor_tensor(out=kTb[64:128, t0:t0 + n], in0=ps[64:128, bk, :n], scalar=qkg[64:128, 1:2],
                                                             in1=r[64:128, :n], op0=ALU.mult, op1=ALU.mult),
                         reads=[BPS[bk], Brs[r_i], Bcst], writes=[B["kTb"]])

            def kg_scores(k, qs, nq, W, blk, ks, nk, bias_ap, cq):
                bk = nxt("r4", 4)

                def mm(h):
                    for g in range(4):
                        cc, hh = g // 2, g % 2
                        ins = h.matmul(ps[:nk, bk, g * nq:(g + 1) * nq], lhsT=(kTa if hh == 0 else kTb)[:, ks:ks + nk],
                                       rhs=qT[:, cc, qs:qs + nq], start=True, stop=True)
                    return ins
                c.op("pe", mm, reads=[B["kTa"], B["kTb"], B["qT0"], B["qT1"]], writes=[BPS[bk]])
                tb = nxt("tmp", 2)
                if cq is None:
                    c.op("dve", lambda h: h.tensor_tensor(out=tmp[tb][:nk, :W], in0=ps[:nk, bk, :W], in1=bias_ap, op=ALU.add),
                         reads=[BPS[bk], B["bias"]], writes=[B["tmp%d" % tb]])
                else:
                    for g in range(4):
                        cval = -slopes[4 * k + g] * 128.0 * cq
                        c.op("dve", lambda h, g=g, cval=cval: h.scalar_tensor_tensor(
                            out=tmp[tb][:nk, g * nq:(g + 1) * nq], in0=ps[:nk, bk, g * nq:(g + 1) * nq], scalar=cval,
                            in1=bias_ap[:, g * nq:(g + 1) * nq], op0=ALU.add, op1=ALU.add),
                            reads=[BPS[bk], B["bias"]], writes=[B["tmp%d" % tb]])
                pt = nxt("pt", NPT)
                c.op("act", lambda h: h.activation(out=PT[pt][:nk, :W], in_=tmp[tb][:nk, :W], func=AF.Exp),
                     reads=[B["tmp%d" % tb]], writes=[B["PT%d" % pt]])
                return (blk, nk, pt)

            def qgroup_scores(k, qb, qs, nq):
                W = 4 * nq
                if qb < 0:
                    kgs = [(0, 0, 16, Bmm[0:16, 0:W], None), (1, 16, 128, Bm0[:, 0:W], None)]
                else:
                    kgs = [(0, 0, 16, Bmeta[0:16, :], qb)]
                    for r_ in (-1, 0, 1):
                        if 0 <= qb + r_ <= 15:
                            kgs.append((qb + r_ + 1, 16 + 128 * (qb + r_), 128, Bband[:, r_ + 1, :], None))
                parts = [kg_scores(k, qs, nq, W, blk, ks, nk, bias_ap, cq) for (blk, ks, nk, bias_ap, cq) in kgs]
                return (k, qb, qs, nq, parts)

            def qgroup_pv(state):
                k, qb, qs, nq, parts = state
                W = 4 * nq
                bA = 4 + nxt("pv", 3)
                oti = 0 if qb < 0 else 1 + qb // 4

                def mm2(h):
                    for i, (blk, nk, pt) in enumerate(parts):
                        ins = h.matmul(ps[:, bA, :W], lhsT=Vd[:nk, blk, k, :], rhs=PT[pt][:nk, :W], start=(i == 0), stop=(i == len(parts) - 1))
                    return ins
                c.op("pe", mm2, reads=[B["V%d" % blk] for (blk, nk, pt) in parts] + [B["PT%d" % pt] for (blk, nk, pt) in parts],
                     writes=[BPS[bA]])
                ri = nxt("rd", 3)
                c.op("dve", lambda h: h.tensor_tensor(out=rd[ri][64:128, :W].rearrange("p (g n) -> p g n", g=4),
                                                      in0=ps[64:128, bA, :W].rearrange("p (g n) -> p g n", g=4),
                                                      in1=es3[64:128, 4 * k:4 * k + 4, :].to_broadcast([64, 4, nq]), op=ALU.add),
                     reads=[BPS[bA], Bcst], writes=[B["rd%d" % ri]])
                c.op("act", lambda h: h.activation(out=rd[ri][64:128, :W], in_=rd[ri][64:128, :W], func=AF.Ln), reads=[B["rd%d" % ri]], writes=[B["rd%d" % ri]])
                c.op("act", lambda h: h.activation(out=rd[ri][64:128, :W], in_=rd[ri][64:128, :W], func=AF.Exp, scale=-1.0), reads=[B["rd%d" % ri]], writes=[B["rd%d" % ri]])
                return (qs, nq, W, bA, ri, oti)

            def qgroup_fin(fin):
                qs, nq, W, bA, ri, oti = fin
                for hh in range(2):
                    lo = 64 * hh
                    c.op("dve", lambda h, hh=hh, lo=lo: h.tensor_tensor(
                        out=oT[lo:lo + 64, :, qs:qs + nq],
                        in0=ps[0:64, bA, :W].rearrange("p (c h n) -> p c h n", c=2, h=2)[:, :, hh, :],
                        in1=rd[ri][64:128, :W].rearrange("p (c h n) -> p c h n", c=2, h=2)[:, :, hh, :], op=ALU.mult),
                        reads=[BPS[bA], B["rd%d" % ri]], writes=[B["oT0_%d" % oti], B["oT1_%d" % oti]])

            def oproj(k, ti, t0, n):
                for m in range(KC):
                    bk = nxt("r3", 3)

                    def mm(h, m=m, bk=bk):
                        for cc in range(2):
                            ins = h.matmul(ps[:, bk, :n], lhsT=wo_sb[:, 2 * k + cc, m * 128:(m + 1) * 128], rhs=oT[:, cc, t0:t0 + n], start=(cc == 0), stop=(cc == 1))
                        return ins
                    c.op("pe", mm, reads=[B["wo"], B["oT0_%d" % ti], B["oT1_%d" % ti]], writes=[BPS[bk]])
                    c.op("dve", lambda h, m=m, bk=bk: h.tensor_tensor(out=H[:, m, t0:t0 + n], in0=ps[:, bk, :n], in1=H[:, m, t0:t0 + n], op=ALU.add),
                         reads=[BPS[bk], BH[m][ti]], writes=[BH[m][ti]])
                if k == 3 and next_gi is not None:
                    _norm_tile(next_gi, ti, t0, n)

            def att_k(k):
                wi = k % 2
                c.dma_group("sp", "tb", [(Bband.rearrange("p r x -> p (r x)"), bband[k].rearrange("p r g a -> p (r g a)")),
                                         (Bmeta[0:16, :], bmeta[k].rearrange("p g a -> p (g a)")),
                                         (Bm0, bm0[k].rearrange("p g a -> p (g a)")),
                                         (Bmm[0:16, :], bmm[k].rearrange("p g a -> p (g a)"))], writes=[B["bias"]])
                for ti, (t0, n) in enumerate(TT):
                    for which in range(3):
                        proj(k, wi, ti, t0, n, which)
                if k + 2 < 4:
                    load_k(k + 2)
                if ATT_SUB <= 2:
                    return
                prev = None
                fin = None
                for (qb, qs, nq) in [(-1, 0, 16)] + [(qb, 16 + 128 * qb, 128) for qb in range(16)]:
                    stt = qgroup_scores(k, qb, qs, nq)
                    nfin = qgroup_pv(prev) if prev is not None else None
                    if fin is not None:
                        qgroup_fin(fin)
                    fin = nfin
                    prev = stt
                nfin = qgroup_pv(prev)
                if fin is not None:
                    qgroup_fin(fin)
                qgroup_fin(nfin)
                if ATT_SUB <= 4:
                    return
                for ti, (t0, n) in enumerate(TT):
                    oproj(k, ti, t0, n)

            for k in range(4):
                att_k(k)
            if ATT_SUB >= 9:
                pending["gi"] = next_gi

        def poolmix(gi, next_gi=None):
            A = OFF_A
            B = {}
            wpi_sb = carve(A, 16384, BF16).rearrange("p (k c) -> p k c", k=KC)
            B["wpi"] = c.rbuf("pool.wpi", 0, 16384)
            U = carve(A + 16384, 8320)
            B["U"] = c.rbuf("pool.U", 16384, 8320)
            T = [carve(A + 86528, 8320), carve(A + 24704, 8320)]
            B["T0"] = c.rbuf("pool.T0", 86528, 8320)
            B["T1"] = c.rbuf("pool.T1", 24704, 8320)
            wpg_sb = carve(A + 33024, 4096, BF16).rearrange("p (g k c) -> p g k c", g=4, k=2)
            B["wpg"] = c.rbuf("pool.wpg", 33024, 4096)
            wpo_sb = carve(A + 37120, 16384, BF16).rearrange("p (k c) -> p k c", k=KC)
            B["wpo"] = c.rbuf("pool.wpo", 37120, 16384)
            pooled = carve(A + 53504, 33024, BF16).rearrange("p (k t) -> p k t", k=KC)
            for i in range(KC):
                B["pl%d" % i] = c.rbuf("pool.pl%d" % i, 53504 + 4128 * i, 4128)
            mixed = carve(A + 86528, 8192, BF16).rearrange("p (k n) -> p k n", k=KC)
            BT = [B["T0"], B["T1"]]
            c.dma_group("pool", "mw", [(wpi_sb, wpi)], writes=[B["wpi"]])
            c.dma_group("pool", "mo", [(wpg_sb, wpg), (wpo_sb, wpo)], writes=[B["wpg"], B["wpo"]])
            rmsnorm(gi)
            c.op("dve", lambda h: h.memset(U[:, 0:8], 0.0), writes=[B["U"]])
            c.op("dve", lambda h: h.memset(U[:, 2072:2080], 0.0), writes=[B["U"]])
            def chunk(ch):
                g = ch // 2
                hw = 1 << g
                for ti, (t0, n) in enumerate(TT):
                    bk = nxt("r3", 3)

                    def mm(h, bk=bk, t0=t0, n=n):
                        for kc in range(KC):
                            ins = h.matmul(ps[:, bk, :n], lhsT=wpi_sb[:, kc, ch * 128:(ch + 1) * 128], rhs=XN[:, kc, t0:t0 + n], start=(kc == 0), stop=(kc == KC - 1))
                        return ins
                    c.op("pe", mm, reads=[B["wpi"]] + [BXN[kc][ti] for kc in range(KC)], writes=[BPS[bk]])
                    c.op("act", lambda h, bk=bk, t0=t0, n=n: h.activation(out=U[:, 8 + t0:8 + t0 + n], in_=ps[:, bk, :n], func=AF.Copy),
                         reads=[BPS[bk]], writes=[B["U"]])
                src, bsrc = U, B["U"]
                s_ = 1
                lvl = 0
                while s_ <= hw:
                    ln = 2080 - 2 * s_ + 1
                    dst, bdst = T[lvl % 2], BT[lvl % 2]
                    c.op("dve", lambda h, src=src, dst=dst, s_=s_, ln=ln: h.tensor_tensor(out=dst[:, 0:ln], in0=src[:, 0:ln], in1=src[:, s_:s_ + ln], op=ALU.add),
                         reads=[bsrc], writes=[bdst])
                    src, bsrc = dst, bdst
                    s_ *= 2
                    lvl += 1
                w_ = 2 * hw
                c.op("dve", lambda h: h.scalar_tensor_tensor(out=pooled[:, ch, :], in0=src[:, 8 - hw:8 - hw + L], scalar=1.0 / w_, in1=U[:, 8:8 + L],
                                                         op0=ALU.mult, op1=ALU.subtract),
                     reads=[bsrc, B["U"]], writes=[B["pl%d" % ch]])
                c.op("dve", lambda h: h.tensor_tensor(out=sm[:, 0:hw], in0=src[:, 8 - hw:8], in1=invc[:, g, 0:hw], op=ALU.mult),
                     reads=[bsrc, Bcst], writes=[Bsm])
                c.op("dve", lambda h: h.tensor_tensor(out=pooled[:, ch, 0:hw], in0=sm[:, 0:hw], in1=U[:, 8:8 + hw], op=ALU.subtract),
                     reads=[Bsm, B["U"]], writes=[B["pl%d" % ch]])
                if hw > 1:
                    nr = hw - 1
                    tb_ = L - hw + 1
                    c.op("dve", lambda h: h.tensor_tensor(out=sm[:, 16:16 + nr], in0=src[:, 8 + tb_ - hw:8 + tb_ - hw + nr],
                                                          in1=invc[:, g, 8:8 + nr], op=ALU.mult),
                         reads=[bsrc, Bcst], writes=[Bsm])
                    c.op("dve", lambda h: h.tensor_tensor(out=pooled[:, ch, tb_:tb_ + nr], in0=sm[:, 16:16 + nr], in1=U[:, 8 + tb_:8 + tb_ + nr], op=ALU.subtract),
                         reads=[Bsm, B["U"]], writes=[B["pl%d" % ch]])

            def mixtile(ti, t0, n):
                for g in range(4):
                    for mm_ in range(2):
                        bk = nxt("r3", 3)
                        oc = 2 * g + mm_

                        def mm(h, g=g, mm_=mm_, bk=bk):
                            for kk in range(2):
                                ins = h.matmul(ps[:, bk, :n], lhsT=wpg_sb[:, g, kk, mm_ * 128:(mm_ + 1) * 128], rhs=pooled[:, 2 * g + kk, t0:t0 + n], start=(kk == 0), stop=(kk == 1))
                            return ins
                        c.op("pe", mm, reads=[B["wpg"], B["pl%d" % (2 * g)], B["pl%d" % (2 * g + 1)]], writes=[BPS[bk]])
                        c.op("dve", lambda h, oc=oc, bk=bk: h.tensor_scalar(out=mixed[:, oc, :n], in0=ps[:, bk, :n], scalar1=gains[:, 6, oc:oc + 1], scalar2=None, op0=ALU.mult),
                             reads=[BPS[bk], Bcst], writes=[B["T0"]])
                for m in range(KC):
                    bk = nxt("r3", 3)

                    def mm(h, m=m, bk=bk):
                        for cc in range(KC):
                            ins = h.matmul(ps[:, bk, :n], lhsT=wpo_sb[:, cc, m * 128:(m + 1) * 128], rhs=mixed[:, cc, :n], start=(cc == 0), stop=(cc == KC - 1))
                        return ins
                    c.op("pe", mm, reads=[B["wpo"], B["T0"]], writes=[BPS[bk]])
                    c.op("dve", lambda h, m=m, bk=bk: h.tensor_tensor(out=H[:, m, t0:t0 + n], in0=ps[:, bk, :n], in1=H[:, m, t0:t0 + n], op=ALU.add),
                         reads=[BPS[bk], BH[m][ti]], writes=[BH[m][ti]])
                if next_gi is not None:
                    _norm_tile(next_gi, ti, t0, n)

            for ch in range(KC):
                chunk(ch)
            for ti, (t0, n) in enumerate(TT):
                mixtile(ti, t0, n)
            pending["gi"] = next_gi

        phases_sel = phases
        for seq in range(2):
            c.dma_group("sp", "xin", [(H[:, 2 * i:2 * i + 2, NM:L], xT[seq, :, 2 * i:2 * i + 2, :]) for i in range(4)] + [(H[:, :, 0:NM], metaT[seq])], writes=allH)
            plist = [("ffn", 0, 0), ("att", 4, 4), ("ffn", 1, 1), ("ffn", 2, 2), ("pool", 5, 5), ("ffn", 3, 3)]
            sel = list(phases_sel if phases_sel is not None else range(min(stage, 6)))
            pending["gi"] = None
            for ii, pi in enumerate(sel):
                kind, arg, g_i = plist[pi]
                nx = plist[sel[ii + 1]][2] if ii + 1 < len(sel) else None
                if kind == "ffn":
                    ffn(arg, nx)
                elif kind == "att":
                    attention(arg, nx)
                else:
                    poolmix(arg, nx)
            c.dma_group("sp", "out", [(outT[seq, :, 2 * i:2 * i + 2, :], H[:, 2 * i:2 * i + 2, NM:L]) for i in range(4)], reads=allH)
        c.wait_all("sp", allH)
        c.replay(block)
    return nc


def _const_tables():
    slopes = np.array([2.0 ** (-8.0 * (h + 1) / 16.0) for h in range(16)], dtype=np.float64).reshape(4, 4)
    a = np.arange(128)
    bband = np.zeros((4, 128, 3, 4, 128), np.float32)
    for r in range(3):
        rel = (r - 1) * 128 + a[:, None] - a[None, :]
        valid = np.abs(rel) <= 128
        for k in range(4):
            for g in range(4):
                bband[k, :, r, g, :] = np.where(valid, -slopes[k, g] * np.abs(rel), NEG)
    m = np.arange(16)
    bmeta = np.zeros((4, 16, 4, 128), np.float32)
    bmm = np.zeros((4, 16, 4, 16), np.float32)
    bm0 = np.zeros((4, 128, 4, 16), np.float32)
    for k in range(4):
        for g in range(4):
            bmeta[k, :, g, :] = -slopes[k, g] * (16 + a[None, :] - m[:, None])
            bmm[k, :, g, :] = -slopes[k, g] * np.abs(m[None, :] - m[:, None])
            dist = 16 + a[:, None] - m[None, :]
            bm0[k, :, g, :] = np.where(dist <= 128, -slopes[k, g] * dist, NEG)
    invc = np.ones((4, 16), np.float32)
    for g in range(4):
        hw = 1 << g
        for t in range(hw):
            invc[g, t] = 1.0 / (t + hw)
        for i in range(hw - 1):
            invc[g, 8 + i] = 1.0 / (2 * hw - 1 - i)
    cb = np.zeros((128, 16, 16), np.float32)
    cb[:] = (-slopes.reshape(16, 1) * 128.0 * np.arange(16)[None, :])[None]
    return bband, bmeta, bm0, bmm, invc, cb.reshape(128, 256)


def _fm(v):
    return np.ascontiguousarray(v.reshape(KC, 128).T)


def _wfm(w):
    return np.ascontiguousarray(w.reshape(KC, 128, -1).transpose(1, 0, 2))


_CACHE = {}


def kernel(x, meta_tokens, ffn_norm, w_gate_up, w_down, mixer_norm, w_qkv, q_norm, k_norm,
           sink_logit, w_o, w_pool_in, w_pool_group, pool_scale, w_pool_out):
    f = np.float32
    x = np.asarray(x, f)
    bband, bmeta, bm0, bmm, invc, cbias = _const_tables()
    wg = np.empty((4, 128, NJ, KC, 128), f)
    wu = np.empty((4, 128, NJ, KC, 128), f)
    wd = np.empty((4, 128, NJ, D), f)
    for i in range(2):
        for ff in range(2):
            n = 2 * i + ff
            wgu = np.asarray(w_gate_up[i, ff], f)
            wg[n] = wgu[:, :2816].reshape(KC, 128, NJ, 128).transpose(1, 2, 0, 3)
            wu[n] = wgu[:, 2816:].reshape(KC, 128, NJ, 128).transpose(1, 2, 0, 3)
            wd[n] = np.asarray(w_down[i, ff], f).reshape(NJ, 128, D).transpose(1, 0, 2)
    cst = np.zeros((128, 288), f)
    gl = [ffn_norm[0, 0], ffn_norm[0, 1], ffn_norm[1, 0], ffn_norm[1, 1], mixer_norm[0], mixer_norm[1], pool_scale[0]]
    for gi, v in enumerate(gl):
        cst[:, gi * 8:(gi + 1) * 8] = _fm(np.asarray(v, f))
    cst[:, 56] = np.tile(np.asarray(q_norm[0], f), 2)
    cst[:, 57] = np.tile(np.asarray(k_norm[0], f), 2)
    cst[:, 58:74] = np.asarray(sink_logit[0], f)[None, :]
    cst[:, 90] = EPS
    cst[:, 91] = 64.0 * EPS
    cst[:, 224:288] = invc.reshape(1, 64)
    wqkv = np.asarray(w_qkv[0], f)
    wq = _wfm(wqkv[:, :1024])
    wk = wqkv[:, 1024:1280].reshape(KC, 128, 4, 64).transpose(1, 0, 2, 3)
    wk2 = np.ascontiguousarray(np.concatenate([wk, wk], axis=3))
    wv = _wfm(wqkv[:, 1280:1536])
    wo = _wfm(np.asarray(w_o[0], f))
    wpi = _wfm(np.asarray(w_pool_in[0], f))
    wpg = np.ascontiguousarray(np.asarray(w_pool_group[0], f).reshape(4, 2, 128, 256).transpose(2, 0, 1, 3))
    wpo = _wfm(np.asarray(w_pool_out[0], f))
    metaT1 = np.ascontiguousarray(np.asarray(meta_tokens, f).T.reshape(KC, 128, NM).transpose(1, 0, 2))
    metaT = np.ascontiguousarray(np.stack([metaT1, metaT1]))
    shared = dict(metaT=metaT, wg=wg, wu=wu, wd=wd, cst=cst, wq=wq, wk2=wk2, wv=wv, wo=wo, wpi=wpi, wpg=wpg, wpo=wpo,
                  bband=bband, bmeta=bmeta, bm0=bm0, bmm=bmm, cbias=cbias)
    in_maps = []
    for core in range(8):
        xs = x[2 * core:2 * core + 2]
        xTc = np.ascontiguousarray(xs.reshape(2, SEQ, KC, 128).transpose(0, 3, 2, 1))
        d = dict(shared)
        d["xT"] = xTc
        in_maps.append(d)
    if STAGE not in _CACHE:
        _CACHE[STAGE] = build_program(STAGE)
    nc = _CACHE[STAGE]
    res = run_bass_kernel_spmd(nc, in_maps, core_ids=list(range(8)))
    out = np.empty((16, SEQ, D), f)
    for core in range(8):
        o = res.results[core]["outT"]
        out[2 * core:2 * core + 2] = o.transpose(0, 3, 2, 1).reshape(2, SEQ, D)
    return out
```

```python
import numpy as np
from contextlib import ExitStack
import concourse.bass as bass
import concourse.mybir as mybir
from concourse.bass_utils import run_bass_kernel_spmd

F32 = mybir.dt.float32
BF16 = mybir.dt.bfloat16
AF = mybir.ActivationFunctionType
ALU = mybir.AluOpType

D = 1024
SEQ = 2048
NM = 16
L = SEQ + NM
KC = 8
NJ = 22
EPS = 1e-6
NEG = -30000.0
TT = [(0, 16)] + [(16 + 512 * i, 512) for i in range(4)]
SLABS = [(0, 4), (4, 4), (8, 4), (12, 4), (16, 4), (20, 2)]
NSLOT = 3
ATT_SUB = 9
STAGE = 6


class Buf:
    __slots__ = ("name", "w", "r", "lo", "hi", "ov", "grp")

    def __init__(self, name, lo=None, hi=None, grp=None):
        self.name = name
        self.w = None
        self.r = {}
        self.lo = lo
        self.hi = hi
        self.ov = []
        self.grp = grp or name


class Eng:
    def __init__(self, name, sem):
        self.name = name
        self.sem = sem
        self.count = 0
        self.known = {}
        self.prog = []


class Ctx:
    def __init__(self, nc, sems):
        self.nc = nc
        self.E = {k: Eng(k, v) for k, v in sems.items()}
        self.semobj = dict(sems)
        self.dma_cnt = {}
        self.ninst = 0
        self.nwaits = 0
        self.ranged = {}

    def rbuf(self, name, lo, size, grp=None):
        key = (name, lo, size)
        if key in self.ranged:
            return self.ranged[key]
        b = Buf(name, lo, lo + size, grp)
        for o in self.ranged.values():
            if o.lo < b.hi and b.lo < o.hi and o.grp != b.grp:
                o.ov.append(b)
                b.ov.append(o)
        self.ranged[key] = b
        return b

    def add_sem(self, key, handle):
        self.semobj[key] = handle
        self.dma_cnt[key] = 0

    def _waits(self, e, reads, writes):
        need = {}
        for b in reads:
            if b.w is not None and need.get(b.w[0], 0) < b.w[1]:
                need[b.w[0]] = b.w[1]
        for b in writes:
            if b.w is not None and need.get(b.w[0], 0) < b.w[1]:
                need[b.w[0]] = b.w[1]
            for s, v in b.r.items():
                if need.get(s, 0) < v:
                    need[s] = v
            for o in b.ov:
                if o.w is not None and need.get(o.w[0], 0) < o.w[1]:
                    need[o.w[0]] = o.w[1]
                for s, v in o.r.items():
                    if need.get(s, 0) < v:
                        need[s] = v
        out = []
        for s, v in need.items():
            if e.name == "pe" and s == "pe":
                continue
            if e.known.get(s, 0) < v:
                e.known[s] = v
                out.append((s, v))
        return out

    def _mark(self, tok, reads, writes):
        for b in reads:
            if b.r.get(tok[0], 0) < tok[1]:
                b.r[tok[0]] = tok[1]
        for b in writes:
            b.w = tok
            b.r = {}

    def op(self, eng, fn, reads=(), writes=()):
        e = self.E[eng]
        waits = self._waits(e, reads, writes)
        e.count += 1
        tok = (eng, e.count)
        semobj = self.semobj
        mysem = e.sem

        def run(h):
            for s, v in waits:
                h.wait_ge(semobj[s], v)
            fn(h).then_inc(mysem, 1)

        e.prog.append(run)
        self.ninst += 1
        self.nwaits += len(waits)
        self._mark(tok, reads, writes)

    def dma_group(self, queue, semkey, items, reads=(), writes=()):
        e = self.E[queue]
        waits = self._waits(e, reads, writes)
        self.dma_cnt[semkey] += 16 * len(items)
        tok = (semkey, self.dma_cnt[semkey])
        semobj = self.semobj

        def run(h):
            for s, v in waits:
                h.wait_ge(semobj[s], v)
            for o, i in items:
                h.dma_start(out=o, in_=i).then_inc(semobj[semkey], 16)

        e.prog.append(run)
        self.ninst += len(items)
        self._mark(tok, reads, writes)

    def wait_all(self, eng, bufs):
        e = self.E[eng]
        waits = self._waits(e, bufs, bufs)
        semobj = self.semobj

        def run(h):
            for s, v in waits:
                h.wait_ge(semobj[s], v)

        e.prog.append(run)

    def replay(self, block):
        E = self.E

        @block.sync
        def _(h):
            for f in E["sp"].prog:
                f(h)

        @block.tensor
        def _(h):
            for f in E["pe"].prog:
                f(h)

        @block.scalar
        def _(h):
            for f in E["act"].prog:
                f(h)

        @block.vector
        def _(h):
            for f in E["dve"].prog:
                f(h)

        @block.gpsimd
        def _(h):
            for f in E["pool"].prog:
                f(h)


def build_program(stage=6, phases=None, nffn=4, nj=NJ):
    nc = bass.Bass("TRN2", target_bir_lowering=False)

    def din(name, shape):
        return nc.dram_tensor(name, list(shape), F32, kind="ExternalInput").ap()

    xT = din("xT", [2, 128, KC, SEQ])
    metaT = din("metaT", [2, 128, KC, NM])
    wg = din("wg", [nffn, 128, nj, KC, 128])
    wu = din("wu", [nffn, 128, nj, KC, 128])
    wd = din("wd", [nffn, 128, nj, D])
    cst = din("cst", [128, 288])
    cbias_d = din("cbias", [128, 256])
    wq = din("wq", [128, KC, D])
    wk2 = din("wk2", [128, KC, 4, 128])
    wv = din("wv", [128, KC, 256])
    wo = din("wo", [128, KC, D])
    wpi = din("wpi", [128, KC, D])
    wpg = din("wpg", [128, 4, 2, 256])
    wpo = din("wpo", [128, KC, D])
    bband = din("bband", [4, 128, 3, 4, 128])
    bmeta = din("bmeta", [4, 16, 4, 128])
    bm0 = din("bm0", [4, 128, 4, 16])
    bmm = din("bmm", [4, 16, 4, 16])
    outT = nc.dram_tensor("outT", [2, 128, KC, SEQ], F32, kind="ExternalOutput").ap()

    slopes = [2.0 ** (-8.0 * (h + 1) / 16.0) for h in range(16)]

    with ExitStack() as st:
        ent = st.enter_context
        TOTAL_F32 = 212480 // 4
        sb = ent(nc.sbuf_tensor("sb", [128, TOTAL_F32], F32))
        ps = ent(nc.psum_tensor("ps", [128, 8, 512], F32))
        sems = {k: ent(nc.semaphore("s_" + k)) for k in ["pe", "act", "dve", "pool", "sp"]}
        c = Ctx(nc, sems)
        for k in ["ws0", "ws1", "ws2", "mw", "mo", "mq0", "mq1", "tb", "cst"] + ["xin%d" % i for i in range(5)] + ["out%d" % i for i in range(5)]:
            c.add_sem(k, ent(nc.semaphore("d_" + k)))
        block = ent(nc.Block())

        def carve(off, nbytes, dt=F32):
            a = sb[:, off // 4:(off + nbytes) // 4]
            if dt == BF16:
                a = a.bitcast(BF16)
            return a

        OFF_H, OFF_XN, OFF_C, OFF_NS, OFF_A = 0, 66048, 99072, 101120, 109312
        H = carve(OFF_H, 66048).rearrange("p (k t) -> p k t", k=KC)
        XN = carve(OFF_XN, 33024, BF16).rearrange("p (k t) -> p k t", k=KC)
        CST = carve(OFF_C, 288 * 4)
        gains = CST[:, 0:56].rearrange("p (g k) -> p g k", k=KC)
        qkg = CST[:, 56:58]
        sink_sb = CST[:, 58:74]
        es3 = CST[:, 74:90].rearrange("p (h o) -> p h o", o=1)
        invc = CST[:, 224:288].rearrange("p (g e) -> p g e", e=16)
        ones_bf = carve(OFF_C + 1280, 256, BF16)
        bd_ones = carve(OFF_C + 1536, 256, BF16)
        sq = [carve(OFF_NS + 1024 * i, 1024, BF16) for i in range(2)]
        rs = [carve(OFF_NS + 2048 + 2048 * i, 2048) for i in range(2)]
        sm = carve(OFF_NS + 6144, 2048)
        cbias = sm[:, 256:512].rearrange("p (h q) -> p h q", q=16)

        BH = [[Buf("H%d_%d" % (k, t)) for t in range(5)] for k in range(KC)]
        BXN = [[Buf("XN%d_%d" % (k, t)) for t in range(5)] for k in range(KC)]
        Bsq = [Buf("sq0"), Buf("sq1")]
        Brs = [Buf("rs0"), Buf("rs1")]
        Bsm = Buf("sm")
        BPS = [Buf("ps%d" % i) for i in range(8)]
        Bcst = Buf("cst")
        allH = [b for row in BH for b in row]
        allXN = [b for row in BXN for b in row]

        pending = {"gi": None}

        rot = {"sqp": 0, "r4": 0, "r3": 0, "gu": 0, "dn": 0, "pv": 0, "pt": 0, "tmp": 0, "rd": 0, "rs": 0}

        def nxt(key, n):
            v = rot[key]
            rot[key] = (v + 1) % n
            return v

        c.dma_group("sp", "cst", [(CST, cst), (sm[:, 256:512], cbias_d)], writes=[Bcst])
        c.op("dve", lambda h: h.memset(ones_bf, 1.0), writes=[Bcst], reads=[Bcst])
        c.op("dve", lambda h: h.memset(bd_ones, 0.0), writes=[Bcst], reads=[Bcst])
        c.op("dve", lambda h: h.memset(bd_ones[0:64, 0:64], 1.0), writes=[Bcst], reads=[Bcst])
        c.op("dve", lambda h: h.memset(bd_ones[64:128, 64:128], 1.0), writes=[Bcst], reads=[Bcst])
        c.op("act", lambda h: h.activation(out=CST[:, 74:90], in_=sink_sb, func=AF.Exp), reads=[Bcst], writes=[Bcst])

        def rmsnorm(gi):
            if pending["gi"] == gi:
                pending["gi"] = None
                return
            for ti, (t0, n) in enumerate(TT):
                _norm_tile(gi, ti, t0, n)

        def _norm_tile(gi, ti, t0, n):
            for kc in range(KC):
                s_i = kc % 2
                c.op("act", lambda h, kc=kc, s_i=s_i: h.activation(out=sq[s_i][:, :n], in_=H[:, kc, t0:t0 + n], func=AF.Square),
                     reads=[BH[kc][ti]], writes=[Bsq[s_i]])
                c.op("pe", lambda h, kc=kc, s_i=s_i: h.matmul(ps[:, 7, :n], lhsT=ones_bf, rhs=sq[s_i][:, :n], start=(kc == 0), stop=(kc == KC - 1)),
                     reads=[Bsq[s_i], Bcst], writes=[BPS[7]])
            r_i = nxt("rs", 2)
            r = rs[r_i]
            c.op("act", lambda h: h.activation(out=r[:, :n], in_=ps[:, 7, :n], func=AF.Ln, scale=1.0 / D, bias=CST[:, 90:91]),
                 reads=[BPS[7], Bcst], writes=[Brs[r_i]])
            c.op("act", lambda h: h.activation(out=r[:, :n], in_=r[:, :n], func=AF.Exp, scale=-0.5), reads=[Brs[r_i]], writes=[Brs[r_i]])
            for kc in range(KC):
                c.op("dve", lambda h, kc=kc: h.scalar_tensor_tensor(out=XN[:, kc, t0:t0 + n], in0=H[:, kc, t0:t0 + n], scalar=gains[:, gi, kc:kc + 1],
                                                                in1=r[:, :n], op0=ALU.mult, op1=ALU.mult),
                     reads=[BH[kc][ti], Brs[r_i], Bcst], writes=[BXN[kc][ti]])

        def ffn(n_ffn, next_gi=None, tail=None):
            B = {}
            for s_ in range(NSLOT):
                B["slot%d" % s_] = c.rbuf("ffn.slot%d" % s_, s_ * 24576, 24576)
            for a_ in range(2):
                for j_ in range(4):
                    B["act%d_%d" % (a_, j_)] = c.rbuf("ffn.act%d_%d" % (a_, j_), 73728 + 4096 * a_ + 1024 * j_, 1024)
                B["sg%d" % a_] = c.rbuf("ffn.sg%d" % a_, 81920 + 2048 * a_, 2048)
            slot_g, slot_u, slot_d = [], [], []
            for s in range(NSLOT):
                base = OFF_A + s * 24576
                slot_g.append(carve(base, 8192, BF16).rearrange("p (j k c) -> p j k c", j=4, k=KC))
                slot_u.append(carve(base + 8192, 8192, BF16).rearrange("p (j k c) -> p j k c", j=4, k=KC))
                slot_d.append(carve(base + 16384, 8192, BF16).rearrange("p (j m) -> p j m", j=4))
            actb = [carve(OFF_A + 73728 + 4096 * a, 4096, BF16).rearrange("p (j n) -> p j n", j=4) for a in range(2)]
            sgb = [carve(OFF_A + 81920 + 2048 * a, 2048) for a in range(2)]

            def load_slab(si):
                j0, S = SLABS[si]
                s = si % NSLOT
                c.dma_group("pool", "ws%d" % s,
                            [(slot_g[s][:, 0:S], wg[n_ffn, :, j0:j0 + S]),
                             (slot_u[s][:, 0:S], wu[n_ffn, :, j0:j0 + S]),
                             (slot_d[s][:, 0:S], wd[n_ffn, :, j0:j0 + S])],
                            writes=[B["slot%d" % s]])

            for si in range(NSLOT):
                load_slab(si)
            rmsnorm(n_ffn)
            steps = [(si, ti) for si in range(len(SLABS)) for ti in range(5)]

            def gu(idx):
                si, ti = steps[idx]
                j0, S = SLABS[si]
                s = si % NSLOT
                t0, n = TT[ti]
                ab = idx % 2
                for j in range(S):
                    p = nxt("gu", 2)

                    def mm(h, j=j, p=p):
                        for kc in range(KC):
                            h.matmul(ps[:, 2 * p, :n], lhsT=slot_g[s][:, j, kc, :], rhs=XN[:, kc, t0:t0 + n], start=(kc == 0), stop=(kc == KC - 1))
                        for kc in range(KC):
                            ins = h.matmul(ps[:, 2 * p + 1, :n], lhsT=slot_u[s][:, j, kc, :], rhs=XN[:, kc, t0:t0 + n], start=(kc == 0), stop=(kc == KC - 1))
                        return ins
                    c.op("pe", mm, reads=[B["slot%d" % s]] + [BXN[kc][ti] for kc in range(KC)], writes=[BPS[2 * p], BPS[2 * p + 1]])
                    c.op("act", lambda h, p=p: h.activation(out=sgb[p][:, :n], in_=ps[:, 2 * p, :n], func=AF.Silu),
                         reads=[BPS[2 * p]], writes=[B["sg%d" % p]])
                    c.op("dve", lambda h, p=p, j=j: h.tensor_tensor(out=actb[ab][:, j, :n], in0=sgb[p][:, :n], in1=ps[:, 2 * p + 1, :n], op=ALU.mult),
                         reads=[B["sg%d" % p], BPS[2 * p + 1]], writes=[B["act%d_%d" % (ab, j)]])

            def down(idx):
                si, ti = steps[idx]
                j0, S = SLABS[si]
                s = si % NSLOT
                t0, n = TT[ti]
                ab = idx % 2
                for m in range(KC):
                    bk = 4 + nxt("dn", 3)

                    def mm(h, m=m, bk=bk):
                        for j in range(S):
                            ins = h.matmul(ps[:, bk, :n], lhsT=slot_d[s][:, j, m * 128:(m + 1) * 128], rhs=actb[ab][:, j, :n], start=(j == 0), stop=(j == S - 1))
                        return ins
                    c.op("pe", mm, reads=[B["slot%d" % s]] + [B["act%d_%d" % (ab, j)] for j in range(S)], writes=[BPS[bk]])
                    c.op("dve", lambda h, m=m, bk=bk: h.scalar_tensor_tensor(out=H[:, m, t0:t0 + n], in0=ps[:, bk, :n], scalar=0.5, in1=H[:, m, t0:t0 + n],
                                                                         op0=ALU.mult, op1=ALU.add),
                         reads=[BPS[bk], BH[m][ti]], writes=[BH[m][ti]])
                if ti == 4 and si + NSLOT < len(SLABS):
                    load_slab(si + NSLOT)
                if next_gi is not None and si == len(SLABS) - 1:
                    _norm_tile(next_gi, ti, t0, n)
                if tail is not None and si == len(SLABS) - 1:
                    tail(ti)

            for idx in range(len(steps)):
                gu(idx)
                if idx > 0:
                    down(idx - 1)
            down(len(steps) - 1)
            pending["gi"] = next_gi

        def attention(gi, next_gi=None, tail=None):
            A = OFF_A
            B = {}

            def rb(name, off, size, grp=None):
                B[name] = c.rbuf("att." + name, off, size, grp and "att." + grp)
            Vd = carve(A + 0, 17408, BF16).rearrange("p (b k d) -> p b k d", b=17, k=4)
            for i in range(17):
                rb("V%d" % i, 1024 * i, 1024)
            kTa = carve(A + 17408, 4128, BF16)
            rb("kTa", 17408, 4128)
            PT_off = [21536, 22560] + [45664 + 1024 * i for i in range(5)] + [100000]
            PT = [carve(A + o, 1024, BF16) for o in PT_off]
            for i, o in enumerate(PT_off):
                rb("PT%d" % i, o, 1024)
            qT = carve(A + 24576, 8256, BF16).rearrange("p (c t) -> p c t", c=2)
            rb("qT0", 24576, 4128)
            rb("qT1", 24576 + 4128, 4128)
            kTb = carve(A + 32832, 4128, BF16)
            rb("kTb", 32832, 4128)
            Bband = carve(A + 36960, 6144).rearrange("p (r x) -> p r x", r=3)
            Bmeta = carve(A + 43104, 2048)
            Bm0 = carve(A + 45152, 256)
            Bmm = carve(A + 45408, 256)
            rb("bias", 36960, 8704)
            tmp = [carve(A + 50784 + 2048 * i, 2048) for i in range(2)]
            rd = [carve(A + 54880 + 2048 * i, 2048) for i in range(2)] + [carve(A + 101024, 2048)]
            rb("rd2", 101024, 2048)
            wq_sb = [carve(A + 58976 + 4096 * i, 4096, BF16).rearrange("p (k c) -> p k c", k=KC) for i in range(2)]
            wk_sb = [carve(A + 67168 + 2048 * i, 2048, BF16).rearrange("p (k c) -> p k c", k=KC) for i in range(2)]
            for i in range(2):
                rb("tmp%d" % i, 50784 + 2048 * i, 2048)
                rb("rd%d" % i, 54880 + 2048 * i, 2048)
                rb("wq%d" % i, 58976 + 4096 * i, 4096)
                rb("wk%d" % i, 67168 + 2048 * i, 2048)
            oT = carve(A + 71264, 8256, BF16).rearrange("p (c t) -> p c t", c=2)
            for cc_ in range(2):
                for t_ in range(5):
                    rb("oT%d_%d" % (cc_, t_), 71264, 8256, grp="oT")
            wo_sb = carve(A + 79520, 16384, BF16).rearrange("p (k c) -> p k c", k=KC)
            rb("wo", 79520, 16384)
            wv_sb = carve(A + 95904, 4096, BF16).rearrange("p (k c) -> p k c", k=KC)
            rb("wv", 95904, 4096)
            NPT = len(PT)

            c.dma_group("pool", "mw", [(wv_sb, wv)], writes=[B["wv"]])
            c.op("dve", lambda h: h.memset(Vd[:, :, :, 64:128], 1.0), writes=[B["V%d" % i] for i in range(17)])
            c.op("dve", lambda h: h.memset(kTa[64:128, :], 0.0), writes=[B["kTa"]])
            c.op("dve", lambda h: h.memset(kTb[0:64, :], 0.0), writes=[B["kTb"]])

            def load_k(k):
                i = k % 2
                c.dma_group("pool", "mq%d" % i, [(wq_sb[i], wq[:, :, k * 256:(k + 1) * 256]), (wk_sb[i], wk2[:, :, k, :])],
                            writes=[B["wq%d" % i], B["wk%d" % i]])
            load_k(0)
            load_k(1)
            c.dma_group("pool", "mo", [(wo_sb, wo)], writes=[B["wo"]])
            rmsnorm(gi)
            def vproj(blk):
                ks, nk = (0, 16) if blk == 0 else (16 + 128 * (blk - 1), 128)
                ti = 0 if blk == 0 else 1 + (blk - 1) // 4
                bk = nxt("r3", 3)

                def mm(h):
                    for kc in range(KC):
                        ins = h.matmul(ps[:nk, bk, 0:256], lhsT=XN[:, kc, ks:ks + nk], rhs=wv_sb[:, kc, :], start=(kc == 0), stop=(kc == KC - 1))
                    return ins
                c.op("pe", mm, reads=[B["wv"]] + [BXN[kc][ti] for kc in range(KC)], writes=[BPS[bk]])
                src = ps[:nk, bk, 0:256].rearrange("p (k d) -> p k d", k=4)
                c.op("act", lambda h: h.activation(out=Vd[:nk, blk, :, 0:64], in_=src, func=AF.Copy),
                     reads=[BPS[bk]], writes=[B["V%d" % blk]])
            for blk in range(17):
                vproj(blk)
            if ATT_SUB <= 1:
                return

            def proj(k, wi, ti, t0, n, which):
                bk = nxt("r3", 3)
                if which < 2:
                    def mm(h):
                        for kc in range(KC):
                            ins = h.matmul(ps[:, bk, :n], lhsT=wq_sb[wi][:, kc, which * 128:(which + 1) * 128], rhs=XN[:, kc, t0:t0 + n],
                                           start=(kc == 0), stop=(kc == KC - 1))
                        return ins
                    wb = B["wq%d" % wi]
                else:
                    def mm(h):
                        for kc in range(KC):
                            ins = h.matmul(ps[:, bk, :n], lhsT=wk_sb[wi][:, kc, :], rhs=XN[:, kc, t0:t0 + n], start=(kc == 0), stop=(kc == KC - 1))
                        return ins
                    wb = B["wk%d" % wi]
                c.op("pe", mm, reads=[wb] + [BXN[kc][ti] for kc in range(KC)], writes=[BPS[bk]])
                sqi = nxt("sqp", 2)
                sbk = (7, 3)[sqi]
                c.op("act", lambda h: h.activation(out=sq[sqi][:, :n], in_=ps[:, bk, :n], func=AF.Square), reads=[BPS[bk]], writes=[Bsq[sqi]])
                c.op("pe", lambda h: h.matmul(ps[:, sbk, :n], lhsT=bd_ones, rhs=sq[sqi][:, :n], start=True, stop=True), reads=[Bsq[sqi], Bcst], writes=[BPS[sbk]])
                r_i = nxt("rs", 2)
                r = rs[r_i]
                if which < 2:
                    c.op("act", lambda h: h.activation(out=r[:, :n], in_=ps[:, sbk, :n], func=AF.Ln, bias=CST[:, 91:92]),
                         reads=[BPS[sbk], Bcst], writes=[Brs[r_i]])
                else:
                    c.op("act", lambda h: h.activation(out=r[:, :n], in_=ps[:, sbk, :n], func=AF.Ln, scale=1.0 / 64.0, bias=CST[:, 90:91]),
                         reads=[BPS[sbk], Bcst], writes=[Brs[r_i]])
                c.op("act", lambda h: h.activation(out=r[:, :n], in_=r[:, :n], func=AF.Exp, scale=-0.5), reads=[Brs[r_i]], writes=[Brs[r_i]])
                if which < 2:
                    c.op("dve", lambda h: h.scalar_tensor_tensor(out=qT[:, which, t0:t0 + n], in0=ps[:, bk, :n], scalar=qkg[:, 0:1],
                                                             in1=r[:, :n], op0=ALU.mult, op1=ALU.mult),
                         reads=[BPS[bk], Brs[r_i], Bcst], writes=[B["qT%d" % which]])
                else:
                    c.op("dve", lambda h: h.scalar_tensor_tensor(out=kTa[0:64, t0:t0 + n], in0=ps[0:64, bk, :n], scalar=qkg[0:64, 1:2],
                                                             in1=r[0:64, :n], op0=ALU.mult, op1=ALU.mult),
                         reads=[BPS[bk], Brs[r_i], Bcst], writes=[B["kTa"]])
                    c.op("dve", lambda h: h.scalar_tensor_tensor(out=kTb[64:128, t0:t0 + n], in0=ps[64:128, bk, :n], scalar=qkg[64:128, 1:2],
                                                             in1=r[64:128, :n], op0=ALU.mult, op1=ALU.mult),
                         reads=[BPS[bk], Brs[r_i], Bcst], writes=[B["kTb"]])

            def kg_scores(k, qs, nq, W, blk, ks, nk, bias_ap, cq):
                bk = nxt("r4", 4)

                def mm(h):
                    for g in range(4):
                        cc, hh = g // 2, g % 2
                        ins = h.matmul(ps[:nk, bk, g * nq:(g + 1) * nq], lhsT=(kTa if hh == 0 else kTb)[:, ks:ks + nk],
                                       rhs=qT[:, cc, qs:qs + nq], start=True, stop=True)
                    return ins
                c.op("pe", mm, reads=[B["kTa"], B["kTb"], B["qT0"], B["qT1"]], writes=[BPS[bk]])
                tb = nxt("tmp", 2)
                if cq is None:
                    c.op("dve", lambda h: h.tensor_tensor(out=tmp[tb][:nk, :W], in0=ps[:nk, bk, :W], in1=bias_ap, op=ALU.add),
                         reads=[BPS[bk], B["bias"]], writes=[B["tmp%d" % tb]])
                else:
                    for g in range(4):
                        cval = -slopes[4 * k + g] * 128.0 * cq
                        c.op("dve", lambda h, g=g, cval=cval: h.scalar_tensor_tensor(
                            out=tmp[tb][:nk, g * nq:(g + 1) * nq], in0=ps[:nk, bk, g * nq:(g + 1) * nq], scalar=cval,
                            in1=bias_ap[:, g * nq:(g + 1) * nq], op0=ALU.add, op1=ALU.add),
                            reads=[BPS[bk], B["bias"]], writes=[B["tmp%d" % tb]])
                pt = nxt("pt", NPT)
                c.op("act", lambda h: h.activation(out=PT[pt][:nk, :W], in_=tmp[tb][:nk, :W], func=AF.Exp),
                     reads=[B["tmp%d" % tb]], writes=[B["PT%d" % pt]])
                return (blk, nk, pt)

            def qgroup_scores(k, qb, qs, nq):
                W = 4 * nq
                if qb < 0:
                    kgs = [(0, 0, 16, Bmm[0:16, 0:W], None), (1, 16, 128, Bm0[:, 0:W], None)]
                else:
                    kgs = [(0, 0, 16, Bmeta[0:16, :], qb)]
                    for r_ in (-1, 0, 1):
                        if 0 <= qb + r_ <= 15:
                            kgs.append((qb + r_ + 1, 16 + 128 * (qb + r_), 128, Bband[:, r_ + 1, :], None))
                parts = [kg_scores(k, qs, nq, W, blk, ks, nk, bias_ap, cq) for (blk, ks, nk, bias_ap, cq) in kgs]
                return (k, qb, qs, nq, parts)

            def qgroup_pv(state):
                k, qb, qs, nq, parts = state
                W = 4 * nq
                bA = 4 + nxt("pv", 3)
                oti = 0 if qb < 0 else 1 + qb // 4

                def mm2(h):
                    for i, (blk, nk, pt) in enumerate(parts):
                        ins = h.matmul(ps[:, bA, :W], lhsT=Vd[:nk, blk, k, :], rhs=PT[pt][:nk, :W], start=(i == 0), stop=(i == len(parts) - 1))
                    return ins
                c.op("pe", mm2, reads=[B["V%d" % blk] for (blk, nk, pt) in parts] + [B["PT%d" % pt] for (blk, nk, pt) in parts],
                     writes=[BPS[bA]])
                ri = nxt("rd", 3)
                c.op("dve", lambda h: h.tensor_tensor(out=rd[ri][64:128, :W].rearrange("p (g n) -> p g n", g=4),
                                                      in0=ps[64:128, bA, :W].rearrange("p (g n) -> p g n", g=4),
                                                      in1=es3[64:128, 4 * k:4 * k + 4, :].to_broadcast([64, 4, nq]), op=ALU.add),
                     reads=[BPS[bA], Bcst], writes=[B["rd%d" % ri]])
                c.op("act", lambda h: h.activation(out=rd[ri][64:128, :W], in_=rd[ri][64:128, :W], func=AF.Ln), reads=[B["rd%d" % ri]], writes=[B["rd%d" % ri]])
                c.op("act", lambda h: h.activation(out=rd[ri][64:128, :W], in_=rd[ri][64:128, :W], func=AF.Exp, scale=-1.0), reads=[B["rd%d" % ri]], writes=[B["rd%d" % ri]])
                return (qs, nq, W, bA, ri, oti)

            def qgroup_fin(fin):
                qs, nq, W, bA, ri, oti = fin
                for hh in range(2):
                    lo = 64 * hh
                    c.op("dve", lambda h, hh=hh, lo=lo: h.tensor_tensor(
                        out=oT[lo:lo + 64, :, qs:qs + nq],
                        in0=ps[0:64, bA, :W].rearrange("p (c h n) -> p c h n", c=2, h=2)[:, :, hh, :],
                        in1=rd[ri][64:128, :W].rearrange("p (c h n) -> p c h n", c=2, h=2)[:, :, hh, :], op=ALU.mult),
                        reads=[BPS[bA], B["rd%d" % ri]], writes=[B["oT0_%d" % oti], B["oT1_%d" % oti]])

            def oproj(k, ti, t0, n):
                for m in range(KC):
                    bk = nxt("r3", 3)

                    def mm(h, m=m, bk=bk):
                        for cc in range(2):
                            ins = h.matmul(ps[:, bk, :n], lhsT=wo_sb[:, 2 * k + cc, m * 128:(m + 1) * 128], rhs=oT[:, cc, t0:t0 + n], start=(cc == 0), stop=(cc == 1))
                        return ins
                    c.op("pe", mm, reads=[B["wo"], B["oT0_%d" % ti], B["oT1_%d" % ti]], writes=[BPS[bk]])
                    c.op("dve", lambda h, m=m, bk=bk: h.tensor_tensor(out=H[:, m, t0:t0 + n], in0=ps[:, bk, :n], in1=H[:, m, t0:t0 + n], op=ALU.add),
                         reads=[BPS[bk], BH[m][ti]], writes=[BH[m][ti]])
                if k == 3 and next_gi is not None:
                    _norm_tile(next_gi, ti, t0, n)
                if k == 3 and tail is not None:
                    tail(ti)

            def att_k(k):
                wi = k % 2
                c.dma_group("sp", "tb", [(Bband.rearrange("p r x -> p (r x)"), bband[k].rearrange("p r g a -> p (r g a)")),
                                         (Bmeta[0:16, :], bmeta[k].rearrange("p g a -> p (g a)")),
                                         (Bm0, bm0[k].rearrange("p g a -> p (g a)")),
                                         (Bmm[0:16, :], bmm[k].rearrange("p g a -> p (g a)"))], writes=[B["bias"]])
                for ti, (t0, n) in enumerate(TT):
                    for which in range(3):
                        proj(k, wi, ti, t0, n, which)
                if k + 2 < 4:
                    load_k(k + 2)
                if ATT_SUB <= 2:
                    return
                prev = None
                fin = None
                for (qb, qs, nq) in [(-1, 0, 16)] + [(qb, 16 + 128 * qb, 128) for qb in range(16)]:
                    stt = qgroup_scores(k, qb, qs, nq)
                    nfin = qgroup_pv(prev) if prev is not None else None
                    if fin is not None:
                        qgroup_fin(fin)
                    fin = nfin
                    prev = stt
                nfin = qgroup_pv(prev)
                if fin is not None:
                    qgroup_fin(fin)
                qgroup_fin(nfin)
                if ATT_SUB <= 4:
                    return
                for ti, (t0, n) in enumerate(TT):
                    oproj(k, ti, t0, n)

            for k in range(4):
                att_k(k)
            if ATT_SUB >= 9:
                pending["gi"] = next_gi

        def poolmix(gi, next_gi=None, tail=None):
            A = OFF_A
            B = {}
            wpi_sb = carve(A, 16384, BF16).rearrange("p (k c) -> p k c", k=KC)
            B["wpi"] = c.rbuf("pool.wpi", 0, 16384)
            Us = [carve(A + 16384, 8320), carve(A + 94848, 8320)]
            B["U0"] = c.rbuf("pool.U0", 16384, 8320)
            B["U1"] = c.rbuf("pool.U1", 94848, 8320)
            T = [carve(A + 86528, 8320), carve(A + 24704, 8320)]
            B["T0"] = c.rbuf("pool.T0", 86528, 8320)
            B["T1"] = c.rbuf("pool.T1", 24704, 8320)
            wpg_sb = carve(A + 33024, 4096, BF16).rearrange("p (g k c) -> p g k c", g=4, k=2)
            B["wpg"] = c.rbuf("pool.wpg", 33024, 4096)
            wpo_sb = carve(A + 37120, 16384, BF16).rearrange("p (k c) -> p k c", k=KC)
            B["wpo"] = c.rbuf("pool.wpo", 37120, 16384)
            pooled = carve(A + 53504, 33024, BF16).rearrange("p (k t) -> p k t", k=KC)
            for i in range(KC):
                B["pl%d" % i] = c.rbuf("pool.pl%d" % i, 53504 + 4128 * i, 4128)
            mixed = carve(A + 86528, 8192, BF16).rearrange("p (k n) -> p k n", k=KC)
            BT = [B["T0"], B["T1"]]
            c.dma_group("pool", "mw", [(wpi_sb, wpi)], writes=[B["wpi"]])
            c.dma_group("pool", "mo", [(wpg_sb, wpg), (wpo_sb, wpo)], writes=[B["wpg"], B["wpo"]])
            rmsnorm(gi)
            for ui in range(2):
                c.op("dve", lambda h, ui=ui: h.memset(Us[ui][:, 0:8], 0.0), writes=[B["U%d" % ui]])
                c.op("dve", lambda h, ui=ui: h.memset(Us[ui][:, 2072:2080], 0.0), writes=[B["U%d" % ui]])
            def chunk(ch):
                g = ch // 2
                hw = 1 << g
                U = Us[ch % 2]
                BU = B["U%d" % (ch % 2)]
                for ti, (t0, n) in enumerate(TT):
                    bk = nxt("r3", 3)

                    def mm(h, bk=bk, t0=t0, n=n):
                        for kc in range(KC):
                            ins = h.matmul(ps[:, bk, :n], lhsT=wpi_sb[:, kc, ch * 128:(ch + 1) * 128], rhs=XN[:, kc, t0:t0 + n], start=(kc == 0), stop=(kc == KC - 1))
                        return ins
                    c.op("pe", mm, reads=[B["wpi"]] + [BXN[kc][ti] for kc in range(KC)], writes=[BPS[bk]])
                    c.op("act", lambda h, bk=bk, t0=t0, n=n: h.activation(out=U[:, 8 + t0:8 + t0 + n], in_=ps[:, bk, :n], func=AF.Copy),
                         reads=[BPS[bk]], writes=[BU])
                src, bsrc = U, BU
                s_ = 1
                lvl = 0
                while s_ <= hw:
                    ln = 2080 - 2 * s_ + 1
                    dst, bdst = T[lvl % 2], BT[lvl % 2]
                    c.op("dve", lambda h, src=src, dst=dst, s_=s_, ln=ln: h.tensor_tensor(out=dst[:, 0:ln], in0=src[:, 0:ln], in1=src[:, s_:s_ + ln], op=ALU.add),
                         reads=[bsrc], writes=[bdst])
                    src, bsrc = dst, bdst
                    s_ *= 2
                    lvl += 1
                w_ = 2 * hw
                c.op("dve", lambda h: h.scalar_tensor_tensor(out=pooled[:, ch, :], in0=src[:, 8 - hw:8 - hw + L], scalar=1.0 / w_, in1=U[:, 8:8 + L],
                                                         op0=ALU.mult, op1=ALU.subtract),
                     reads=[bsrc, BU], writes=[B["pl%d" % ch]])
                c.op("dve", lambda h: h.tensor_tensor(out=sm[:, 0:hw], in0=src[:, 8 - hw:8], in1=invc[:, g, 0:hw], op=ALU.mult),
                     reads=[bsrc, Bcst], writes=[Bsm])
                c.op("dve", lambda h: h.tensor_tensor(out=pooled[:, ch, 0:hw], in0=sm[:, 0:hw], in1=U[:, 8:8 + hw], op=ALU.subtract),
                     reads=[Bsm, BU], writes=[B["pl%d" % ch]])
                if hw > 1:
                    nr = hw - 1
                    tb_ = L - hw + 1
                    c.op("dve", lambda h: h.tensor_tensor(out=sm[:, 16:16 + nr], in0=src[:, 8 + tb_ - hw:8 + tb_ - hw + nr],
                                                          in1=invc[:, g, 8:8 + nr], op=ALU.mult),
                         reads=[bsrc, Bcst], writes=[Bsm])
                    c.op("dve", lambda h: h.tensor_tensor(out=pooled[:, ch, tb_:tb_ + nr], in0=sm[:, 16:16 + nr], in1=U[:, 8 + tb_:8 + tb_ + nr], op=ALU.subtract),
                         reads=[Bsm, BU], writes=[B["pl%d" % ch]])

            def mixtile(ti, t0, n):
                for g in range(4):
                    for mm_ in range(2):
                        bk = nxt("r3", 3)
                        oc = 2 * g + mm_

                        def mm(h, g=g, mm_=mm_, bk=bk):
                            for kk in range(2):
                                ins = h.matmul(ps[:, bk, :n], lhsT=wpg_sb[:, g, kk, mm_ * 128:(mm_ + 1) * 128], rhs=pooled[:, 2 * g + kk, t0:t0 + n], start=(kk == 0), stop=(kk == 1))
                            return ins
                        c.op("pe", mm, reads=[B["wpg"], B["pl%d" % (2 * g)], B["pl%d" % (2 * g + 1)]], writes=[BPS[bk]])
                        c.op("dve", lambda h, oc=oc, bk=bk: h.tensor_scalar(out=mixed[:, oc, :n], in0=ps[:, bk, :n], scalar1=gains[:, 6, oc:oc + 1], scalar2=None, op0=ALU.mult),
                             reads=[BPS[bk], Bcst], writes=[B["T0"]])
                for m in range(KC):
                    bk = nxt("r3", 3)

                    def mm(h, m=m, bk=bk):
                        for cc in range(KC):
                            ins = h.matmul(ps[:, bk, :n], lhsT=wpo_sb[:, cc, m * 128:(m + 1) * 128], rhs=mixed[:, cc, :n], start=(cc == 0), stop=(cc == KC - 1))
                        return ins
                    c.op("pe", mm, reads=[B["wpo"], B["T0"]], writes=[BPS[bk]])
                    c.op("dve", lambda h, m=m, bk=bk: h.tensor_tensor(out=H[:, m, t0:t0 + n], in0=ps[:, bk, :n], in1=H[:, m, t0:t0 + n], op=ALU.add),
                         reads=[BPS[bk], BH[m][ti]], writes=[BH[m][ti]])
                if next_gi is not None:
                    _norm_tile(next_gi, ti, t0, n)
                if tail is not None:
                    tail(ti)

            for ch in range(KC):
                chunk(ch)
            for ti, (t0, n) in enumerate(TT):
                mixtile(ti, t0, n)
            pending["gi"] = next_gi

        phases_sel = phases
        plist = [("ffn", 0, 0), ("att", 4, 4), ("ffn", 1, 1), ("ffn", 2, 2), ("pool", 5, 5), ("ffn", 3, 3)]
        sel = list(phases_sel if phases_sel is not None else range(min(stage, 6)))

        def load_tile(seq, ti):
            t0, n = TT[ti]
            if ti == 0:
                src = metaT[seq]
            else:
                src = xT[seq, :, :, t0 - NM:t0 - NM + n]
            c.dma_group("sp", "xin%d" % ti, [(H[:, :, t0:t0 + n], src)], writes=[BH[kc][ti] for kc in range(KC)])

        def store_tile(seq, ti):
            t0, n = TT[ti]
            if ti == 0:
                return
            c.dma_group("sp", "out%d" % ti, [(outT[seq, :, :, t0 - NM:t0 - NM + n], H[:, :, t0:t0 + n])], reads=[BH[kc][ti] for kc in range(KC)])

        for ti in range(5):
            load_tile(0, ti)
        for seq in range(2):
            if seq == 0:
                pending["gi"] = None

            def tail(ti, seq=seq):
                store_tile(seq, ti)
                if seq == 0:
                    load_tile(1, ti)
                    if len(sel) > 0:
                        _norm_tile(plist[sel[0]][2], ti, TT[ti][0], TT[ti][1])
            for ii, pi in enumerate(sel):
                kind, arg, g_i = plist[pi]
                last = (ii + 1 == len(sel))
                nx = plist[sel[ii + 1]][2] if not last else None
                tl = tail if last else None
                if kind == "ffn":
                    ffn(arg, nx, tl)
                elif kind == "att":
                    attention(arg, nx, tl)
                else:
                    poolmix(arg, nx, tl)
            if seq == 0 and len(sel) > 0:
                pending["gi"] = plist[sel[0]][2]
        c.wait_all("sp", allH)
        c.replay(block)
    return nc


def _const_tables():
    slopes = np.array([2.0 ** (-8.0 * (h + 1) / 16.0) for h in range(16)], dtype=np.float64).reshape(4, 4)
    a = np.arange(128)
    bband = np.zeros((4, 128, 3, 4, 128), np.float32)
    for r in range(3):
        rel = (r - 1) * 128 + a[:, None] - a[None, :]
        valid = np.abs(rel) <= 128
        for k in range(4):
            for g in range(4):
                bband[k, :, r, g, :] = np.where(valid, -slopes[k, g] * np.abs(rel), NEG)
    m = np.arange(16)
    bmeta = np.zeros((4, 16, 4, 128), np.float32)
    bmm = np.zeros((4, 16, 4, 16), np.float32)
    bm0 = np.zeros((4, 128, 4, 16), np.float32)
    for k in range(4):
        for g in range(4):
            bmeta[k, :, g, :] = -slopes[k, g] * (16 + a[None, :] - m[:, None])
            bmm[k, :, g, :] = -slopes[k, g] * np.abs(m[None, :] - m[:, None])
            dist = 16 + a[:, None] - m[None, :]
            bm0[k, :, g, :] = np.where(dist <= 128, -slopes[k, g] * dist, NEG)
    invc = np.ones((4, 16), np.float32)
    for g in range(4):
        hw = 1 << g
        for t in range(hw):
            invc[g, t] = 1.0 / (t + hw)
        for i in range(hw - 1):
            invc[g, 8 + i] = 1.0 / (2 * hw - 1 - i)
    cb = np.zeros((128, 16, 16), np.float32)
    cb[:] = (-slopes.reshape(16, 1) * 128.0 * np.arange(16)[None, :])[None]
    return bband, bmeta, bm0, bmm, invc, cb.reshape(128, 256)


def _fm(v):
    return np.ascontiguousarray(v.reshape(KC, 128).T)


def _wfm(w):
    return np.ascontiguousarray(w.reshape(KC, 128, -1).transpose(1, 0, 2))


_CACHE = {}


def kernel(x, meta_tokens, ffn_norm, w_gate_up, w_down, mixer_norm, w_qkv, q_norm, k_norm,
           sink_logit, w_o, w_pool_in, w_pool_group, pool_scale, w_pool_out):
    f = np.float32
    x = np.asarray(x, f)
    bband, bmeta, bm0, bmm, invc, cbias = _const_tables()
    wg = np.empty((4, 128, NJ, KC, 128), f)
    wu = np.empty((4, 128, NJ, KC, 128), f)
    wd = np.empty((4, 128, NJ, D), f)
    for i in range(2):
        for ff in range(2):
            n = 2 * i + ff
            wgu = np.asarray(w_gate_up[i, ff], f)
            wg[n] = wgu[:, :2816].reshape(KC, 128, NJ, 128).transpose(1, 2, 0, 3)
            wu[n] = wgu[:, 2816:].reshape(KC, 128, NJ, 128).transpose(1, 2, 0, 3)
            wd[n] = np.asarray(w_down[i, ff], f).reshape(NJ, 128, D).transpose(1, 0, 2)
    cst = np.zeros((128, 288), f)
    gl = [ffn_norm[0, 0], ffn_norm[0, 1], ffn_norm[1, 0], ffn_norm[1, 1], mixer_norm[0], mixer_norm[1], pool_scale[0]]
    for gi, v in enumerate(gl):
        cst[:, gi * 8:(gi + 1) * 8] = _fm(np.asarray(v, f))
    cst[:, 56] = np.tile(np.asarray(q_norm[0], f), 2)
    cst[:, 57] = np.tile(np.asarray(k_norm[0], f), 2)
    cst[:, 58:74] = np.asarray(sink_logit[0], f)[None, :]
    cst[:, 90] = EPS
    cst[:, 91] = 64.0 * EPS
    cst[:, 224:288] = invc.reshape(1, 64)
    wqkv = np.asarray(w_qkv[0], f)
    wq = _wfm(wqkv[:, :1024])
    wk = wqkv[:, 1024:1280].reshape(KC, 128, 4, 64).transpose(1, 0, 2, 3)
    wk2 = np.ascontiguousarray(np.concatenate([wk, wk], axis=3))
    wv = _wfm(wqkv[:, 1280:1536])
    wo = _wfm(np.asarray(w_o[0], f))
    wpi = _wfm(np.asarray(w_pool_in[0], f))
    wpg = np.ascontiguousarray(np.asarray(w_pool_group[0], f).reshape(4, 2, 128, 256).transpose(2, 0, 1, 3))
    wpo = _wfm(np.asarray(w_pool_out[0], f))
    metaT1 = np.ascontiguousarray(np.asarray(meta_tokens, f).T.reshape(KC, 128, NM).transpose(1, 0, 2))
    metaT = np.ascontiguousarray(np.stack([metaT1, metaT1]))
    shared = dict(metaT=metaT, wg=wg, wu=wu, wd=wd, cst=cst, wq=wq, wk2=wk2, wv=wv, wo=wo, wpi=wpi, wpg=wpg, wpo=wpo,
                  bband=bband, bmeta=bmeta, bm0=bm0, bmm=bmm, cbias=cbias)
    in_maps = []
    for core in range(8):
        xs = x[2 * core:2 * core + 2]
        xTc = np.ascontiguousarray(xs.reshape(2, SEQ, KC, 128).transpose(0, 3, 2, 1))
        d = dict(shared)
        d["xT"] = xTc
        in_maps.append(d)
    if STAGE not in _CACHE:
        _CACHE[STAGE] = build_program(STAGE)
    nc = _CACHE[STAGE]
    res = run_bass_kernel_spmd(nc, in_maps, core_ids=list(range(8)))
    out = np.empty((16, SEQ, D), f)
    for core in range(8):
        o = res.results[core]["outT"]
        out[2 * core:2 * core + 2] = o.transpose(0, 3, 2, 1).reshape(2, SEQ, D)
    return out
```

```python
import numpy as np
from contextlib import ExitStack
import concourse.bass as bass
import concourse.mybir as mybir
from concourse.bass_utils import run_bass_kernel_spmd

F32 = mybir.dt.float32
BF16 = mybir.dt.bfloat16
AF = mybir.ActivationFunctionType
ALU = mybir.AluOpType

D = 1024
SEQ = 2048
NM = 16
L = SEQ + NM
KC = 8
NJ = 22
EPS = 1e-6
NEG = -30000.0
TT = [(0, 16)] + [(16 + 512 * i, 512) for i in range(4)]
SLABS = [(0, 4), (4, 4), (8, 4), (12, 4), (16, 4), (20, 2)]
NSLOT = 3
ATT_SUB = 9
STAGE = 6


class Buf:
    __slots__ = ("name", "w", "r", "lo", "hi", "ov", "grp")

    def __init__(self, name, lo=None, hi=None, grp=None):
        self.name = name
        self.w = None
        self.r = {}
        self.lo = lo
        self.hi = hi
        self.ov = []
        self.grp = grp or name


class Eng:
    def __init__(self, name, sem):
        self.name = name
        self.sem = sem
        self.count = 0
        self.known = {}
        self.prog = []


class Ctx:
    def __init__(self, nc, sems):
        self.nc = nc
        self.E = {k: Eng(k, v) for k, v in sems.items()}
        self.semobj = dict(sems)
        self.dma_cnt = {}
        self.ninst = 0
        self.nwaits = 0
        self.ranged = {}

    def rbuf(self, name, lo, size, grp=None):
        key = (name, lo, size)
        if key in self.ranged:
            return self.ranged[key]
        b = Buf(name, lo, lo + size, grp)
        for o in self.ranged.values():
            if o.lo < b.hi and b.lo < o.hi and o.grp != b.grp:
                o.ov.append(b)
                b.ov.append(o)
        self.ranged[key] = b
        return b

    def add_sem(self, key, handle):
        self.semobj[key] = handle
        self.dma_cnt[key] = 0

    def _waits(self, e, reads, writes):
        need = {}
        for b in reads:
            if b.w is not None and need.get(b.w[0], 0) < b.w[1]:
                need[b.w[0]] = b.w[1]
        for b in writes:
            if b.w is not None and need.get(b.w[0], 0) < b.w[1]:
                need[b.w[0]] = b.w[1]
            for s, v in b.r.items():
                if need.get(s, 0) < v:
                    need[s] = v
            for o in b.ov:
                if o.w is not None and need.get(o.w[0], 0) < o.w[1]:
                    need[o.w[0]] = o.w[1]
                for s, v in o.r.items():
                    if need.get(s, 0) < v:
                        need[s] = v
        out = []
        for s, v in need.items():
            if e.name == "pe" and s == "pe":
                continue
            if e.known.get(s, 0) < v:
                e.known[s] = v
                out.append((s, v))
        return out

    def _mark(self, tok, reads, writes):
        for b in reads:
            if b.r.get(tok[0], 0) < tok[1]:
                b.r[tok[0]] = tok[1]
        for b in writes:
            b.w = tok
            b.r = {}

    def op(self, eng, fn, reads=(), writes=()):
        e = self.E[eng]
        waits = self._waits(e, reads, writes)
        e.count += 1
        tok = (eng, e.count)
        semobj = self.semobj
        mysem = e.sem

        def run(h):
            for s, v in waits:
                h.wait_ge(semobj[s], v)
            fn(h).then_inc(mysem, 1)

        e.prog.append(run)
        self.ninst += 1
        self.nwaits += len(waits)
        self._mark(tok, reads, writes)

    def dma_group(self, queue, semkey, items, reads=(), writes=()):
        e = self.E[queue]
        waits = self._waits(e, reads, writes)
        self.dma_cnt[semkey] += 16 * len(items)
        tok = (semkey, self.dma_cnt[semkey])
        semobj = self.semobj

        def run(h):
            for s, v in waits:
                h.wait_ge(semobj[s], v)
            for o, i in items:
                h.dma_start(out=o, in_=i).then_inc(semobj[semkey], 16)

        e.prog.append(run)
        self.ninst += len(items)
        self._mark(tok, reads, writes)

    def wait_all(self, eng, bufs):
        e = self.E[eng]
        waits = self._waits(e, bufs, bufs)
        semobj = self.semobj

        def run(h):
            for s, v in waits:
                h.wait_ge(semobj[s], v)

        e.prog.append(run)

    def replay(self, block):
        E = self.E

        @block.sync
        def _(h):
            for f in E["sp"].prog:
                f(h)

        @block.tensor
        def _(h):
            for f in E["pe"].prog:
                f(h)

        @block.scalar
        def _(h):
            for f in E["act"].prog:
                f(h)

        @block.vector
        def _(h):
            for f in E["dve"].prog:
                f(h)

        @block.gpsimd
        def _(h):
            for f in E["pool"].prog:
                f(h)


def build_program(stage=6, phases=None, nffn=4, nj=NJ):
    nc = bass.Bass("TRN2", target_bir_lowering=False)

    def din(name, shape):
        return nc.dram_tensor(name, list(shape), F32, kind="ExternalInput").ap()

    xT = din("xT", [2, 128, KC, SEQ])
    metaT = din("metaT", [2, 128, KC, NM])
    wg = din("wg", [nffn, 128, nj, KC, 128])
    wu = din("wu", [nffn, 128, nj, KC, 128])
    wd = din("wd", [nffn, 128, nj, D])
    cst = din("cst", [128, 288])
    cbias_d = din("cbias", [128, 256])
    wq = din("wq", [128, KC, D])
    wk2 = din("wk2", [128, KC, 4, 128])
    wv = din("wv", [128, KC, 256])
    wo = din("wo", [128, KC, D])
    wpi = din("wpi", [128, KC, D])
    wpg = din("wpg", [128, 4, 2, 256])
    wpo = din("wpo", [128, KC, D])
    bband = din("bband", [4, 128, 3, 4, 128])
    bmeta = din("bmeta", [4, 16, 4, 128])
    bm0 = din("bm0", [4, 128, 4, 16])
    bmm = din("bmm", [4, 16, 4, 16])
    outT = nc.dram_tensor("outT", [2, 128, KC, SEQ], F32, kind="ExternalOutput").ap()

    slopes = [2.0 ** (-8.0 * (h + 1) / 16.0) for h in range(16)]

    with ExitStack() as st:
        ent = st.enter_context
        TOTAL_F32 = 212480 // 4
        sb = ent(nc.sbuf_tensor("sb", [128, TOTAL_F32], F32))
        ps = ent(nc.psum_tensor("ps", [128, 8, 512], F32))
        sems = {k: ent(nc.semaphore("s_" + k)) for k in ["pe", "act", "dve", "pool", "sp"]}
        c = Ctx(nc, sems)
        for k in ["ws0", "ws1", "ws2", "mw", "mo", "mq0", "mq1", "tb", "cst"] + ["xin%d" % i for i in range(5)] + ["out%d" % i for i in range(5)]:
            c.add_sem(k, ent(nc.semaphore("d_" + k)))
        block = ent(nc.Block())

        def carve(off, nbytes, dt=F32):
            a = sb[:, off // 4:(off + nbytes) // 4]
            if dt == BF16:
                a = a.bitcast(BF16)
            return a

        OFF_H, OFF_XN, OFF_C, OFF_NS, OFF_A = 0, 66048, 99072, 101120, 109312
        H = carve(OFF_H, 66048).rearrange("p (k t) -> p k t", k=KC)
        XN = carve(OFF_XN, 33024, BF16).rearrange("p (k t) -> p k t", k=KC)
        CST = carve(OFF_C, 288 * 4)
        gains = CST[:, 0:56].rearrange("p (g k) -> p g k", k=KC)
        qkg = CST[:, 56:58]
        sink_sb = CST[:, 58:74]
        es3 = CST[:, 74:90].rearrange("p (h o) -> p h o", o=1)
        invc = CST[:, 224:288].rearrange("p (g e) -> p g e", e=16)
        ones_bf = carve(OFF_C + 1280, 256, BF16)
        bd_ones = carve(OFF_C + 1536, 256, BF16)
        sq = [carve(OFF_NS + 1024 * i, 1024, BF16) for i in range(2)]
        rs = [carve(OFF_NS + 2048 + 2048 * i, 2048) for i in range(2)]
        sm = carve(OFF_NS + 6144, 2048)
        cbias = sm[:, 256:512].rearrange("p (h q) -> p h q", q=16)

        BH = [[Buf("H%d_%d" % (k, t)) for t in range(5)] for k in range(KC)]
        BXN = [[Buf("XN%d_%d" % (k, t)) for t in range(5)] for k in range(KC)]
        Bsq = [Buf("sq0"), Buf("sq1")]
        Brs = [Buf("rs0"), Buf("rs1")]
        Bsm = Buf("sm")
        BPS = [Buf("ps%d" % i) for i in range(8)]
        Bcst = Buf("cst")
        allH = [b for row in BH for b in row]
        allXN = [b for row in BXN for b in row]

        pending = {"gi": None}

        rot = {"sqp": 0, "r4": 0, "r3": 0, "gu": 0, "dn": 0, "pv": 0, "pt": 0, "tmp": 0, "rd": 0, "rs": 0}

        def nxt(key, n):
            v = rot[key]
            rot[key] = (v + 1) % n
            return v

        c.dma_group("sp", "cst", [(CST, cst), (sm[:, 256:512], cbias_d)], writes=[Bcst])
        c.op("dve", lambda h: h.memset(ones_bf, 1.0), writes=[Bcst], reads=[Bcst])
        c.op("dve", lambda h: h.memset(bd_ones, 0.0), writes=[Bcst], reads=[Bcst])
        c.op("dve", lambda h: h.memset(bd_ones[0:64, 0:64], 1.0), writes=[Bcst], reads=[Bcst])
        c.op("dve", lambda h: h.memset(bd_ones[64:128, 64:128], 1.0), writes=[Bcst], reads=[Bcst])
        c.op("act", lambda h: h.activation(out=CST[:, 74:90], in_=sink_sb, func=AF.Exp), reads=[Bcst], writes=[Bcst])

        def rmsnorm(gi):
            if pending["gi"] == gi:
                pending["gi"] = None
                return
            for ti, (t0, n) in enumerate(TT):
                _norm_tile(gi, ti, t0, n)

        def _norm_tile(gi, ti, t0, n):
            for kc in range(KC):
                s_i = kc % 2
                c.op("act", lambda h, kc=kc, s_i=s_i: h.activation(out=sq[s_i][:, :n], in_=H[:, kc, t0:t0 + n], func=AF.Square),
                     reads=[BH[kc][ti]], writes=[Bsq[s_i]])
                c.op("pe", lambda h, kc=kc, s_i=s_i: h.matmul(ps[:, 7, :n], lhsT=ones_bf, rhs=sq[s_i][:, :n], start=(kc == 0), stop=(kc == KC - 1)),
                     reads=[Bsq[s_i], Bcst], writes=[BPS[7]])
            r_i = nxt("rs", 2)
            r = rs[r_i]
            c.op("act", lambda h: h.activation(out=r[:, :n], in_=ps[:, 7, :n], func=AF.Ln, scale=1.0 / D, bias=CST[:, 90:91]),
                 reads=[BPS[7], Bcst], writes=[Brs[r_i]])
            c.op("act", lambda h: h.activation(out=r[:, :n], in_=r[:, :n], func=AF.Exp, scale=-0.5), reads=[Brs[r_i]], writes=[Brs[r_i]])
            for kc in range(KC):
                c.op("dve", lambda h, kc=kc: h.scalar_tensor_tensor(out=XN[:, kc, t0:t0 + n], in0=H[:, kc, t0:t0 + n], scalar=gains[:, gi, kc:kc + 1],
                                                                in1=r[:, :n], op0=ALU.mult, op1=ALU.mult),
                     reads=[BH[kc][ti], Brs[r_i], Bcst], writes=[BXN[kc][ti]])

        def ffn(n_ffn, next_gi=None, tail=None):
            B = {}
            for s_ in range(NSLOT):
                B["slot%d" % s_] = c.rbuf("ffn.slot%d" % s_, s_ * 24576, 24576)
            for a_ in range(2):
                for j_ in range(4):
                    B["act%d_%d" % (a_, j_)] = c.rbuf("ffn.act%d_%d" % (a_, j_), 73728 + 4096 * a_ + 1024 * j_, 1024)
                B["sg%d" % a_] = c.rbuf("ffn.sg%d" % a_, 81920 + 2048 * a_, 2048)
            slot_g, slot_u, slot_d = [], [], []
            for s in range(NSLOT):
                base = OFF_A + s * 24576
                slot_g.append(carve(base, 8192, BF16).rearrange("p (j k c) -> p j k c", j=4, k=KC))
                slot_u.append(carve(base + 8192, 8192, BF16).rearrange("p (j k c) -> p j k c", j=4, k=KC))
                slot_d.append(carve(base + 16384, 8192, BF16).rearrange("p (j m) -> p j m", j=4))
            actb = [carve(OFF_A + 73728 + 4096 * a, 4096, BF16).rearrange("p (j n) -> p j n", j=4) for a in range(2)]
            sgb = [carve(OFF_A + 81920 + 2048 * a, 2048) for a in range(2)]

            def load_slab(si):
                j0, S = SLABS[si]
                s = si % NSLOT
                c.dma_group("pool", "ws%d" % s,
                            [(slot_g[s][:, 0:S], wg[n_ffn, :, j0:j0 + S]),
                             (slot_u[s][:, 0:S], wu[n_ffn, :, j0:j0 + S]),
                             (slot_d[s][:, 0:S], wd[n_ffn, :, j0:j0 + S])],
                            writes=[B["slot%d" % s]])

            for si in range(NSLOT):
                load_slab(si)
            rmsnorm(n_ffn)
            steps = [(si, ti) for si in range(len(SLABS)) for ti in range(5)]

            def gu(idx):
                si, ti = steps[idx]
                j0, S = SLABS[si]
                s = si % NSLOT
                t0, n = TT[ti]
                ab = idx % 2
                for j in range(S):
                    p = nxt("gu", 2)

                    def mm(h, j=j, p=p):
                        for kc in range(KC):
                            h.matmul(ps[:, 2 * p, :n], lhsT=slot_g[s][:, j, kc, :], rhs=XN[:, kc, t0:t0 + n], start=(kc == 0), stop=(kc == KC - 1))
                        for kc in range(KC):
                            ins = h.matmul(ps[:, 2 * p + 1, :n], lhsT=slot_u[s][:, j, kc, :], rhs=XN[:, kc, t0:t0 + n], start=(kc == 0), stop=(kc == KC - 1))
                        return ins
                    c.op("pe", mm, reads=[B["slot%d" % s]] + [BXN[kc][ti] for kc in range(KC)], writes=[BPS[2 * p], BPS[2 * p + 1]])
                    c.op("act", lambda h, p=p: h.activation(out=sgb[p][:, :n], in_=ps[:, 2 * p, :n], func=AF.Silu),
                         reads=[BPS[2 * p]], writes=[B["sg%d" % p]])
                    c.op("dve", lambda h, p=p, j=j: h.tensor_tensor(out=actb[ab][:, j, :n], in0=sgb[p][:, :n], in1=ps[:, 2 * p + 1, :n], op=ALU.mult),
                         reads=[B["sg%d" % p], BPS[2 * p + 1]], writes=[B["act%d_%d" % (ab, j)]])

            def down(idx):
                si, ti = steps[idx]
                j0, S = SLABS[si]
                s = si % NSLOT
                t0, n = TT[ti]
                ab = idx % 2
                for m in range(KC):
                    bk = 4 + nxt("dn", 3)

                    def mm(h, m=m, bk=bk):
                        for j in range(S):
                            ins = h.matmul(ps[:, bk, :n], lhsT=slot_d[s][:, j, m * 128:(m + 1) * 128], rhs=actb[ab][:, j, :n], start=(j == 0), stop=(j == S - 1))
                        return ins
                    c.op("pe", mm, reads=[B["slot%d" % s]] + [B["act%d_%d" % (ab, j)] for j in range(S)], writes=[BPS[bk]])
                    c.op("dve", lambda h, m=m, bk=bk: h.scalar_tensor_tensor(out=H[:, m, t0:t0 + n], in0=ps[:, bk, :n], scalar=0.5, in1=H[:, m, t0:t0 + n],
                                                                         op0=ALU.mult, op1=ALU.add),
                         reads=[BPS[bk], BH[m][ti]], writes=[BH[m][ti]])
                if ti == 4 and si + NSLOT < len(SLABS):
                    load_slab(si + NSLOT)
                if next_gi is not None and si == len(SLABS) - 1:
                    _norm_tile(next_gi, ti, t0, n)
                if tail is not None and si == len(SLABS) - 1:
                    tail(ti)

            for idx in range(len(steps)):
                gu(idx)
                if idx > 0:
                    down(idx - 1)
            down(len(steps) - 1)
            pending["gi"] = next_gi

        def attention(gi, next_gi=None, tail=None):
            A = OFF_A
            B = {}

            def rb(name, off, size, grp=None):
                B[name] = c.rbuf("att." + name, off, size, grp and "att." + grp)
            Vd = carve(A + 0, 17408, BF16).rearrange("p (b k d) -> p b k d", b=17, k=4)
            for i in range(17):
                rb("V%d" % i, 1024 * i, 1024)
            kTa = carve(A + 17408, 4128, BF16)
            rb("kTa", 17408, 4128)
            PT_off = [21536, 22560] + [45664 + 1024 * i for i in range(5)] + [100000]
            PT = [carve(A + o, 1024, BF16) for o in PT_off]
            for i, o in enumerate(PT_off):
                rb("PT%d" % i, o, 1024)
            qT = carve(A + 24576, 8256, BF16).rearrange("p (c t) -> p c t", c=2)
            rb("qT0", 24576, 4128)
            rb("qT1", 24576 + 4128, 4128)
            kTb = carve(A + 32832, 4128, BF16)
            rb("kTb", 32832, 4128)
            Bband = carve(A + 36960, 6144).rearrange("p (r x) -> p r x", r=3)
            Bmeta = carve(A + 43104, 2048)
            Bm0 = carve(A + 45152, 256)
            Bmm = carve(A + 45408, 256)
            rb("bias", 36960, 8704)
            tmp = [carve(A + 50784 + 2048 * i, 2048) for i in range(2)]
            rd = [carve(A + 54880 + 2048 * i, 2048) for i in range(2)] + [carve(A + 101024, 2048)]
            rb("rd2", 101024, 2048)
            wq_sb = [carve(A + 58976 + 4096 * i, 4096, BF16).rearrange("p (k c) -> p k c", k=KC) for i in range(2)]
            wk_sb = [carve(A + 67168 + 2048 * i, 2048, BF16).rearrange("p (k c) -> p k c", k=KC) for i in range(2)]
            for i in range(2):
                rb("tmp%d" % i, 50784 + 2048 * i, 2048)
                rb("rd%d" % i, 54880 + 2048 * i, 2048)
                rb("wq%d" % i, 58976 + 4096 * i, 4096)
                rb("wk%d" % i, 67168 + 2048 * i, 2048)
            oT = carve(A + 71264, 8256, BF16).rearrange("p (c t) -> p c t", c=2)
            for cc_ in range(2):
                for t_ in range(5):
                    rb("oT%d_%d" % (cc_, t_), 71264, 8256, grp="oT")
            wo_sb = carve(A + 79520, 16384, BF16).rearrange("p (k c) -> p k c", k=KC)
            rb("wo", 79520, 16384)
            wv_sb = carve(A + 95904, 4096, BF16).rearrange("p (k c) -> p k c", k=KC)
            rb("wv", 95904, 4096)
            NPT = len(PT)

            c.dma_group("pool", "mw", [(wv_sb, wv)], writes=[B["wv"]])
            c.op("dve", lambda h: h.memset(Vd[:, :, :, 64:128], 1.0), writes=[B["V%d" % i] for i in range(17)])
            c.op("dve", lambda h: h.memset(kTa[64:128, :], 0.0), writes=[B["kTa"]])
            c.op("dve", lambda h: h.memset(kTb[0:64, :], 0.0), writes=[B["kTb"]])

            def load_k(k):
                i = k % 2
                c.dma_group("pool", "mq%d" % i, [(wq_sb[i], wq[:, :, k * 256:(k + 1) * 256]), (wk_sb[i], wk2[:, :, k, :])],
                            writes=[B["wq%d" % i], B["wk%d" % i]])
            load_k(0)
            load_k(1)
            c.dma_group("pool", "mo", [(wo_sb, wo)], writes=[B["wo"]])
            rmsnorm(gi)
            def vproj(blk):
                ks, nk = (0, 16) if blk == 0 else (16 + 128 * (blk - 1), 128)
                ti = 0 if blk == 0 else 1 + (blk - 1) // 4
                bk = nxt("r3", 3)

                def mm(h):
                    for kc in range(KC):
                        ins = h.matmul(ps[:nk, bk, 0:256], lhsT=XN[:, kc, ks:ks + nk], rhs=wv_sb[:, kc, :], start=(kc == 0), stop=(kc == KC - 1))
                    return ins
                c.op("pe", mm, reads=[B["wv"]] + [BXN[kc][ti] for kc in range(KC)], writes=[BPS[bk]])
                src = ps[:nk, bk, 0:256].rearrange("p (k d) -> p k d", k=4)
                c.op("act", lambda h: h.activation(out=Vd[:nk, blk, :, 0:64], in_=src, func=AF.Copy),
                     reads=[BPS[bk]], writes=[B["V%d" % blk]])
            for blk in range(17):
                vproj(blk)
            if ATT_SUB <= 1:
                return

            def proj(k, wi, ti, t0, n, which):
                bk = nxt("r3", 3)
                if which < 2:
                    def mm(h):
                        for kc in range(KC):
                            ins = h.matmul(ps[:, bk, :n], lhsT=wq_sb[wi][:, kc, which * 128:(which + 1) * 128], rhs=XN[:, kc, t0:t0 + n],
                                           start=(kc == 0), stop=(kc == KC - 1))
                        return ins
                    wb = B["wq%d" % wi]
                else:
                    def mm(h):
                        for kc in range(KC):
                            ins = h.matmul(ps[:, bk, :n], lhsT=wk_sb[wi][:, kc, :], rhs=XN[:, kc, t0:t0 + n], start=(kc == 0), stop=(kc == KC - 1))
                        return ins
                    wb = B["wk%d" % wi]
                c.op("pe", mm, reads=[wb] + [BXN[kc][ti] for kc in range(KC)], writes=[BPS[bk]])
                sqi = nxt("sqp", 2)
                sbk = (7, 3)[sqi]
                c.op("act", lambda h: h.activation(out=sq[sqi][:, :n], in_=ps[:, bk, :n], func=AF.Square), reads=[BPS[bk]], writes=[Bsq[sqi]])
                c.op("pe", lambda h: h.matmul(ps[:, sbk, :n], lhsT=bd_ones, rhs=sq[sqi][:, :n], start=True, stop=True), reads=[Bsq[sqi], Bcst], writes=[BPS[sbk]])
                r_i = nxt("rs", 2)
                r = rs[r_i]
                if which < 2:
                    c.op("act", lambda h: h.activation(out=r[:, :n], in_=ps[:, sbk, :n], func=AF.Ln, bias=CST[:, 91:92]),
                         reads=[BPS[sbk], Bcst], writes=[Brs[r_i]])
                else:
                    c.op("act", lambda h: h.activation(out=r[:, :n], in_=ps[:, sbk, :n], func=AF.Ln, scale=1.0 / 64.0, bias=CST[:, 90:91]),
                         reads=[BPS[sbk], Bcst], writes=[Brs[r_i]])
                c.op("act", lambda h: h.activation(out=r[:, :n], in_=r[:, :n], func=AF.Exp, scale=-0.5), reads=[Brs[r_i]], writes=[Brs[r_i]])
                if which < 2:
                    c.op("dve", lambda h: h.scalar_tensor_tensor(out=qT[:, which, t0:t0 + n], in0=ps[:, bk, :n], scalar=qkg[:, 0:1],
                                                             in1=r[:, :n], op0=ALU.mult, op1=ALU.mult),
                         reads=[BPS[bk], Brs[r_i], Bcst], writes=[B["qT%d" % which]])
                else:
                    c.op("dve", lambda h: h.scalar_tensor_tensor(out=kTa[0:64, t0:t0 + n], in0=ps[0:64, bk, :n], scalar=qkg[0:64, 1:2],
                                                             in1=r[0:64, :n], op0=ALU.mult, op1=ALU.mult),
                         reads=[BPS[bk], Brs[r_i], Bcst], writes=[B["kTa"]])
                    c.op("dve", lambda h: h.scalar_tensor_tensor(out=kTb[64:128, t0:t0 + n], in0=ps[64:128, bk, :n], scalar=qkg[64:128, 1:2],
                                                             in1=r[64:128, :n], op0=ALU.mult, op1=ALU.mult),
                         reads=[BPS[bk], Brs[r_i], Bcst], writes=[B["kTb"]])

            def kg_scores(k, qs, nq, W, blk, ks, nk, bias_ap, cq):
                bk = nxt("r4", 4)

                def mm(h):
                    for hh in range(2):
                        ins = h.matmul(ps[:nk, bk, hh * 2 * nq:(hh + 1) * 2 * nq].rearrange("p (c n) -> p c n", c=2),
                                       lhsT=(kTa if hh == 0 else kTb)[:, ks:ks + nk],
                                       rhs=qT[:, :, qs:qs + nq], start=True, stop=True)
                    return ins
                c.op("pe", mm, reads=[B["kTa"], B["kTb"], B["qT0"], B["qT1"]], writes=[BPS[bk]])
                tb = nxt("tmp", 2)
                if cq is None:
                    c.op("dve", lambda h: h.tensor_tensor(out=tmp[tb][:nk, :W], in0=ps[:nk, bk, :W], in1=bias_ap, op=ALU.add),
                         reads=[BPS[bk], B["bias"]], writes=[B["tmp%d" % tb]])
                else:
                    c.op("dve", lambda h: h.tensor_tensor(out=tmp[tb][:nk, :W].rearrange("p (g n) -> p g n", g=4),
                                                          in0=ps[:nk, bk, :W].rearrange("p (g n) -> p g n", g=4),
                                                          in1=cbias[:nk, 4 * k:4 * k + 4, cq:cq + 1].to_broadcast([nk, 4, nq]), op=ALU.add),
                         reads=[BPS[bk], Bcst], writes=[B["tmp%d" % tb]])
                    c.op("dve", lambda h: h.tensor_tensor(out=tmp[tb][:nk, :W], in0=tmp[tb][:nk, :W], in1=bias_ap, op=ALU.add),
                         reads=[B["tmp%d" % tb], B["bias"]], writes=[B["tmp%d" % tb]])
                pt = nxt("pt", NPT)
                c.op("act", lambda h: h.activation(out=PT[pt][:nk, :W], in_=tmp[tb][:nk, :W], func=AF.Exp),
                     reads=[B["tmp%d" % tb]], writes=[B["PT%d" % pt]])
                return (blk, nk, pt)

            def qgroup_scores(k, qb, qs, nq):
                W = 4 * nq
                if qb < 0:
                    kgs = [(0, 0, 16, Bmm[0:16, 0:W], None), (1, 16, 128, Bm0[:, 0:W], None)]
                else:
                    kgs = [(0, 0, 16, Bmeta[0:16, :], qb)]
                    for r_ in (-1, 0, 1):
                        if 0 <= qb + r_ <= 15:
                            kgs.append((qb + r_ + 1, 16 + 128 * (qb + r_), 128, Bband[:, r_ + 1, :], None))
                parts = [kg_scores(k, qs, nq, W, blk, ks, nk, bias_ap, cq) for (blk, ks, nk, bias_ap, cq) in kgs]
                return (k, qb, qs, nq, parts)

            def qgroup_pv(state):
                k, qb, qs, nq, parts = state
                W = 4 * nq
                bA = 4 + nxt("pv", 3)
                oti = 0 if qb < 0 else 1 + qb // 4

                def mm2(h):
                    for i, (blk, nk, pt) in enumerate(parts):
                        ins = h.matmul(ps[:, bA, :W], lhsT=Vd[:nk, blk, k, :], rhs=PT[pt][:nk, :W], start=(i == 0), stop=(i == len(parts) - 1))
                    return ins
                c.op("pe", mm2, reads=[B["V%d" % blk] for (blk, nk, pt) in parts] + [B["PT%d" % pt] for (blk, nk, pt) in parts],
                     writes=[BPS[bA]])
                ri = nxt("rd", 3)
                c.op("dve", lambda h: h.tensor_tensor(out=rd[ri][64:128, :W].rearrange("p (g n) -> p g n", g=4),
                                                      in0=ps[64:128, bA, :W].rearrange("p (g n) -> p g n", g=4),
                                                      in1=es3[64:128, 4 * k:4 * k + 4, :].to_broadcast([64, 4, nq]), op=ALU.add),
                     reads=[BPS[bA], Bcst], writes=[B["rd%d" % ri]])
                c.op("act", lambda h: h.activation(out=rd[ri][64:128, :W], in_=rd[ri][64:128, :W], func=AF.Ln), reads=[B["rd%d" % ri]], writes=[B["rd%d" % ri]])
                c.op("act", lambda h: h.activation(out=rd[ri][64:128, :W], in_=rd[ri][64:128, :W], func=AF.Exp, scale=-1.0), reads=[B["rd%d" % ri]], writes=[B["rd%d" % ri]])
                return (qs, nq, W, bA, ri, oti)

            def qgroup_fin(fin):
                qs, nq, W, bA, ri, oti = fin
                for hh in range(2):
                    lo = 64 * hh
                    c.op("dve", lambda h, hh=hh, lo=lo: h.tensor_tensor(
                        out=oT[lo:lo + 64, :, qs:qs + nq],
                        in0=ps[0:64, bA, :W].rearrange("p (h c n) -> p h c n", c=2, h=2)[:, hh, :, :],
                        in1=rd[ri][64:128, :W].rearrange("p (h c n) -> p h c n", c=2, h=2)[:, hh, :, :], op=ALU.mult),
                        reads=[BPS[bA], B["rd%d" % ri]], writes=[B["oT0_%d" % oti], B["oT1_%d" % oti]])

            def oproj(k, ti, t0, n):
                for m in range(KC):
                    bk = nxt("r3", 3)

                    def mm(h, m=m, bk=bk):
                        for cc in range(2):
                            ins = h.matmul(ps[:, bk, :n], lhsT=wo_sb[:, 2 * k + cc, m * 128:(m + 1) * 128], rhs=oT[:, cc, t0:t0 + n], start=(cc == 0), stop=(cc == 1))
                        return ins
                    c.op("pe", mm, reads=[B["wo"], B["oT0_%d" % ti], B["oT1_%d" % ti]], writes=[BPS[bk]])
                    c.op("dve", lambda h, m=m, bk=bk: h.tensor_tensor(out=H[:, m, t0:t0 + n], in0=ps[:, bk, :n], in1=H[:, m, t0:t0 + n], op=ALU.add),
                         reads=[BPS[bk], BH[m][ti]], writes=[BH[m][ti]])
                if k == 3 and next_gi is not None:
                    _norm_tile(next_gi, ti, t0, n)
                if k == 3 and tail is not None:
                    tail(ti)

            def att_k(k):
                wi = k % 2
                c.dma_group("sp", "tb", [(Bband.rearrange("p r x -> p (r x)"), bband[k].rearrange("p r g a -> p (r g a)")),
                                         (Bmeta[0:16, :], bmeta[k].rearrange("p g a -> p (g a)")),
                                         (Bm0, bm0[k].rearrange("p g a -> p (g a)")),
                                         (Bmm[0:16, :], bmm[k].rearrange("p g a -> p (g a)"))], writes=[B["bias"]])
                for ti, (t0, n) in enumerate(TT):
                    for which in range(3):
                        proj(k, wi, ti, t0, n, which)
                if k + 2 < 4:
                    load_k(k + 2)
                if ATT_SUB <= 2:
                    return
                prev = None
                fin = None
                for (qb, qs, nq) in [(-1, 0, 16)] + [(qb, 16 + 128 * qb, 128) for qb in range(16)]:
                    stt = qgroup_scores(k, qb, qs, nq)
                    nfin = qgroup_pv(prev) if prev is not None else None
                    if fin is not None:
                        qgroup_fin(fin)
                    fin = nfin
                    prev = stt
                nfin = qgroup_pv(prev)
                if fin is not None:
                    qgroup_fin(fin)
                qgroup_fin(nfin)
                if ATT_SUB <= 4:
                    return
                for ti, (t0, n) in enumerate(TT):
                    oproj(k, ti, t0, n)

            for k in range(4):
                att_k(k)
            if ATT_SUB >= 9:
                pending["gi"] = next_gi

        def poolmix(gi, next_gi=None, tail=None):
            A = OFF_A
            B = {}
            wpi_sb = carve(A, 16384, BF16).rearrange("p (k c) -> p k c", k=KC)
            B["wpi"] = c.rbuf("pool.wpi", 0, 16384)
            Us = [carve(A + 16384, 8320), carve(A + 94848, 8320)]
            B["U0"] = c.rbuf("pool.U0", 16384, 8320)
            B["U1"] = c.rbuf("pool.U1", 94848, 8320)
            T = [carve(A + 86528, 8320), carve(A + 24704, 8320)]
            B["T0"] = c.rbuf("pool.T0", 86528, 8320)
            B["T1"] = c.rbuf("pool.T1", 24704, 8320)
            wpg_sb = carve(A + 33024, 4096, BF16).rearrange("p (g k c) -> p g k c", g=4, k=2)
            B["wpg"] = c.rbuf("pool.wpg", 33024, 4096)
            wpo_sb = carve(A + 37120, 16384, BF16).rearrange("p (k c) -> p k c", k=KC)
            B["wpo"] = c.rbuf("pool.wpo", 37120, 16384)
            pooled = carve(A + 53504, 33024, BF16).rearrange("p (k t) -> p k t", k=KC)
            for i in range(KC):
                B["pl%d" % i] = c.rbuf("pool.pl%d" % i, 53504 + 4128 * i, 4128)
            mixed = carve(A + 86528, 8192, BF16).rearrange("p (k n) -> p k n", k=KC)
            BT = [B["T0"], B["T1"]]
            c.dma_group("pool", "mw", [(wpi_sb, wpi)], writes=[B["wpi"]])
            c.dma_group("pool", "mo", [(wpg_sb, wpg), (wpo_sb, wpo)], writes=[B["wpg"], B["wpo"]])
            rmsnorm(gi)
            for ui in range(2):
                c.op("dve", lambda h, ui=ui: h.memset(Us[ui][:, 0:8], 0.0), writes=[B["U%d" % ui]])
                c.op("dve", lambda h, ui=ui: h.memset(Us[ui][:, 2072:2080], 0.0), writes=[B["U%d" % ui]])
            def chunk(ch):
                g = ch // 2
                hw = 1 << g
                U = Us[ch % 2]
                BU = B["U%d" % (ch % 2)]
                for ti, (t0, n) in enumerate(TT):
                    bk = nxt("r3", 3)

                    def mm(h, bk=bk, t0=t0, n=n):
                        for kc in range(KC):
                            ins = h.matmul(ps[:, bk, :n], lhsT=wpi_sb[:, kc, ch * 128:(ch + 1) * 128], rhs=XN[:, kc, t0:t0 + n], start=(kc == 0), stop=(kc == KC - 1))
                        return ins
                    c.op("pe", mm, reads=[B["wpi"]] + [BXN[kc][ti] for kc in range(KC)], writes=[BPS[bk]])
                    c.op("act", lambda h, bk=bk, t0=t0, n=n: h.activation(out=U[:, 8 + t0:8 + t0 + n], in_=ps[:, bk, :n], func=AF.Copy),
                         reads=[BPS[bk]], writes=[BU])
                src, bsrc = U, BU
                s_ = 1
                lvl = 0
                while s_ <= hw:
                    ln = 2080 - 2 * s_ + 1
                    dst, bdst = T[lvl % 2], BT[lvl % 2]
                    c.op("dve", lambda h, src=src, dst=dst, s_=s_, ln=ln: h.tensor_tensor(out=dst[:, 0:ln], in0=src[:, 0:ln], in1=src[:, s_:s_ + ln], op=ALU.add),
                         reads=[bsrc], writes=[bdst])
                    src, bsrc = dst, bdst
                    s_ *= 2
                    lvl += 1
                w_ = 2 * hw
                c.op("dve", lambda h: h.scalar_tensor_tensor(out=pooled[:, ch, :], in0=src[:, 8 - hw:8 - hw + L], scalar=1.0 / w_, in1=U[:, 8:8 + L],
                                                         op0=ALU.mult, op1=ALU.subtract),
                     reads=[bsrc, BU], writes=[B["pl%d" % ch]])
                c.op("dve", lambda h: h.tensor_tensor(out=sm[:, 0:hw], in0=src[:, 8 - hw:8], in1=invc[:, g, 0:hw], op=ALU.mult),
                     reads=[bsrc, Bcst], writes=[Bsm])
                c.op("dve", lambda h: h.tensor_tensor(out=pooled[:, ch, 0:hw], in0=sm[:, 0:hw], in1=U[:, 8:8 + hw], op=ALU.subtract),
                     reads=[Bsm, BU], writes=[B["pl%d" % ch]])
                if hw > 1:
                    nr = hw - 1
                    tb_ = L - hw + 1
                    c.op("dve", lambda h: h.tensor_tensor(out=sm[:, 16:16 + nr], in0=src[:, 8 + tb_ - hw:8 + tb_ - hw + nr],
                                                          in1=invc[:, g, 8:8 + nr], op=ALU.mult),
                         reads=[bsrc, Bcst], writes=[Bsm])
                    c.op("dve", lambda h: h.tensor_tensor(out=pooled[:, ch, tb_:tb_ + nr], in0=sm[:, 16:16 + nr], in1=U[:, 8 + tb_:8 + tb_ + nr], op=ALU.subtract),
                         reads=[Bsm, BU], writes=[B["pl%d" % ch]])

            def mixtile(ti, t0, n):
                for g in range(4):
                    for mm_ in range(2):
                        bk = nxt("r3", 3)
                        oc = 2 * g + mm_

                        def mm(h, g=g, mm_=mm_, bk=bk):
                            for kk in range(2):
                                ins = h.matmul(ps[:, bk, :n], lhsT=wpg_sb[:, g, kk, mm_ * 128:(mm_ + 1) * 128], rhs=pooled[:, 2 * g + kk, t0:t0 + n], start=(kk == 0), stop=(kk == 1))
                            return ins
                        c.op("pe", mm, reads=[B["wpg"], B["pl%d" % (2 * g)], B["pl%d" % (2 * g + 1)]], writes=[BPS[bk]])
                        c.op("dve", lambda h, oc=oc, bk=bk: h.tensor_scalar(out=mixed[:, oc, :n], in0=ps[:, bk, :n], scalar1=gains[:, 6, oc:oc + 1], scalar2=None, op0=ALU.mult),
                             reads=[BPS[bk], Bcst], writes=[B["T0"]])
                for m in range(KC):
                    bk = nxt("r3", 3)

                    def mm(h, m=m, bk=bk):
                        for cc in range(KC):
                            ins = h.matmul(ps[:, bk, :n], lhsT=wpo_sb[:, cc, m * 128:(m + 1) * 128], rhs=mixed[:, cc, :n], start=(cc == 0), stop=(cc == KC - 1))
                        return ins
                    c.op("pe", mm, reads=[B["wpo"], B["T0"]], writes=[BPS[bk]])
                    c.op("dve", lambda h, m=m, bk=bk: h.tensor_tensor(out=H[:, m, t0:t0 + n], in0=ps[:, bk, :n], in1=H[:, m, t0:t0 + n], op=ALU.add),
                         reads=[BPS[bk], BH[m][ti]], writes=[BH[m][ti]])
                if next_gi is not None:
                    _norm_tile(next_gi, ti, t0, n)
                if tail is not None:
                    tail(ti)

            for ch in range(KC):
                chunk(ch)
            for ti, (t0, n) in enumerate(TT):
                mixtile(ti, t0, n)
            pending["gi"] = next_gi

        phases_sel = phases
        plist = [("ffn", 0, 0), ("att", 4, 4), ("ffn", 1, 1), ("ffn", 2, 2), ("pool", 5, 5), ("ffn", 3, 3)]
        sel = list(phases_sel if phases_sel is not None else range(min(stage, 6)))

        def load_tile(seq, ti):
            t0, n = TT[ti]
            if ti == 0:
                src = metaT[seq]
            else:
                src = xT[seq, :, :, t0 - NM:t0 - NM + n]
            c.dma_group("sp", "xin%d" % ti, [(H[:, :, t0:t0 + n], src)], writes=[BH[kc][ti] for kc in range(KC)])

        def store_tile(seq, ti):
            t0, n = TT[ti]
            if ti == 0:
                return
            c.dma_group("sp", "out%d" % ti, [(outT[seq, :, :, t0 - NM:t0 - NM + n], H[:, :, t0:t0 + n])], reads=[BH[kc][ti] for kc in range(KC)])

        for ti in range(5):
            load_tile(0, ti)
        for seq in range(2):
            if seq == 0:
                pending["gi"] = None

            def tail(ti, seq=seq):
                store_tile(seq, ti)
                if seq == 0:
                    load_tile(1, ti)
                    if len(sel) > 0:
                        _norm_tile(plist[sel[0]][2], ti, TT[ti][0], TT[ti][1])
            for ii, pi in enumerate(sel):
                kind, arg, g_i = plist[pi]
                last = (ii + 1 == len(sel))
                nx = plist[sel[ii + 1]][2] if not last else None
                tl = tail if last else None
                if kind == "ffn":
                    ffn(arg, nx, tl)
                elif kind == "att":
                    attention(arg, nx, tl)
                else:
                    poolmix(arg, nx, tl)
            if seq == 0 and len(sel) > 0:
                pending["gi"] = plist[sel[0]][2]
        c.wait_all("sp", allH)
        c.replay(block)
    return nc


def _const_tables():
    slopes = np.array([2.0 ** (-8.0 * (h + 1) / 16.0) for h in range(16)], dtype=np.float64).reshape(4, 4)
    a = np.arange(128)
    bband = np.zeros((4, 128, 3, 4, 128), np.float32)
    for r in range(3):
        rel = (r - 1) * 128 + a[:, None] - a[None, :]
        valid = np.abs(rel) <= 128
        for k in range(4):
            for g in range(4):
                bband[k, :, r, g, :] = np.where(valid, -slopes[k, g] * np.abs(rel), NEG)
    m = np.arange(16)
    bmeta = np.zeros((4, 16, 4, 128), np.float32)
    bmm = np.zeros((4, 16, 4, 16), np.float32)
    bm0 = np.zeros((4, 128, 4, 16), np.float32)
    for k in range(4):
        for g in range(4):
            bmeta[k, :, g, :] = -slopes[k, g] * (16 + a[None, :] - m[:, None])
            bmm[k, :, g, :] = -slopes[k, g] * np.abs(m[None, :] - m[:, None])
            dist = 16 + a[:, None] - m[None, :]
            bm0[k, :, g, :] = np.where(dist <= 128, -slopes[k, g] * dist, NEG)
    invc = np.ones((4, 16), np.float32)
    for g in range(4):
        hw = 1 << g
        for t in range(hw):
            invc[g, t] = 1.0 / (t + hw)
        for i in range(hw - 1):
            invc[g, 8 + i] = 1.0 / (2 * hw - 1 - i)
    perm = [0, 2, 1, 3]
    bband = np.ascontiguousarray(bband[:, :, :, perm, :])
    bmeta = np.ascontiguousarray(bmeta[:, :, perm, :])
    bmm = np.ascontiguousarray(bmm[:, :, perm, :])
    bm0 = np.ascontiguousarray(bm0[:, :, perm, :])
    sl_p = slopes[:, perm].reshape(16, 1)
    cb = np.zeros((128, 16, 16), np.float32)
    cb[:] = (-sl_p * 128.0 * np.arange(16)[None, :])[None]
    return bband, bmeta, bm0, bmm, invc, cb.reshape(128, 256)


def _fm(v):
    return np.ascontiguousarray(v.reshape(KC, 128).T)


def _wfm(w):
    return np.ascontiguousarray(w.reshape(KC, 128, -1).transpose(1, 0, 2))


_CACHE = {}


def kernel(x, meta_tokens, ffn_norm, w_gate_up, w_down, mixer_norm, w_qkv, q_norm, k_norm,
           sink_logit, w_o, w_pool_in, w_pool_group, pool_scale, w_pool_out):
    f = np.float32
    x = np.asarray(x, f)
    bband, bmeta, bm0, bmm, invc, cbias = _const_tables()
    wg = np.empty((4, 128, NJ, KC, 128), f)
    wu = np.empty((4, 128, NJ, KC, 128), f)
    wd = np.empty((4, 128, NJ, D), f)
    for i in range(2):
        for ff in range(2):
            n = 2 * i + ff
            wgu = np.asarray(w_gate_up[i, ff], f)
            wg[n] = wgu[:, :2816].reshape(KC, 128, NJ, 128).transpose(1, 2, 0, 3)
            wu[n] = wgu[:, 2816:].reshape(KC, 128, NJ, 128).transpose(1, 2, 0, 3)
            wd[n] = np.asarray(w_down[i, ff], f).reshape(NJ, 128, D).transpose(1, 0, 2)
    cst = np.zeros((128, 288), f)
    gl = [ffn_norm[0, 0], ffn_norm[0, 1], ffn_norm[1, 0], ffn_norm[1, 1], mixer_norm[0], mixer_norm[1], pool_scale[0]]
    for gi, v in enumerate(gl):
        cst[:, gi * 8:(gi + 1) * 8] = _fm(np.asarray(v, f))
    cst[:, 56] = np.tile(np.asarray(q_norm[0], f), 2)
    cst[:, 57] = np.tile(np.asarray(k_norm[0], f), 2)
    cst[:, 58:74] = np.asarray(sink_logit[0], f).reshape(4, 4)[:, [0, 2, 1, 3]].reshape(1, 16)
    cst[:, 90] = EPS
    cst[:, 91] = 64.0 * EPS
    cst[:, 224:288] = invc.reshape(1, 64)
    wqkv = np.asarray(w_qkv[0], f)
    wq = _wfm(wqkv[:, :1024])
    wk = wqkv[:, 1024:1280].reshape(KC, 128, 4, 64).transpose(1, 0, 2, 3)
    wk2 = np.ascontiguousarray(np.concatenate([wk, wk], axis=3))
    wv = _wfm(wqkv[:, 1280:1536])
    wo = _wfm(np.asarray(w_o[0], f))
    wpi = _wfm(np.asarray(w_pool_in[0], f))
    wpg = np.ascontiguousarray(np.asarray(w_pool_group[0], f).reshape(4, 2, 128, 256).transpose(2, 0, 1, 3))
    wpo = _wfm(np.asarray(w_pool_out[0], f))
    metaT1 = np.ascontiguousarray(np.asarray(meta_tokens, f).T.reshape(KC, 128, NM).transpose(1, 0, 2))
    metaT = np.ascontiguousarray(np.stack([metaT1, metaT1]))
    shared = dict(metaT=metaT, wg=wg, wu=wu, wd=wd, cst=cst, wq=wq, wk2=wk2, wv=wv, wo=wo, wpi=wpi, wpg=wpg, wpo=wpo,
                  bband=bband, bmeta=bmeta, bm0=bm0, bmm=bmm, cbias=cbias)
    in_maps = []
    for core in range(8):
        xs = x[2 * core:2 * core + 2]
        xTc = np.ascontiguousarray(xs.reshape(2, SEQ, KC, 128).transpose(0, 3, 2, 1))
        d = dict(shared)
        d["xT"] = xTc
        in_maps.append(d)
    if STAGE not in _CACHE:
        _CACHE[STAGE] = build_program(STAGE)
    nc = _CACHE[STAGE]
    res = run_bass_kernel_spmd(nc, in_maps, core_ids=list(range(8)))
    out = np.empty((16, SEQ, D), f)
    for core in range(8):
        o = res.results[core]["outT"]
        out[2 * core:2 * core + 2] = o.transpose(0, 3, 2, 1).reshape(2, SEQ, D)
    return out
```

```python
import numpy as np
from contextlib import ExitStack
import concourse.bass as bass
import concourse.mybir as mybir
from concourse.bass_utils import run_bass_kernel_spmd

F32 = mybir.dt.float32
BF16 = mybir.dt.bfloat16
AF = mybir.ActivationFunctionType
ALU = mybir.AluOpType

D = 1024
SEQ = 2048
NM = 16
L = SEQ + NM
KC = 8
NJ = 22
EPS = 1e-6
NEG = -30000.0
TT = [(0, 400), (400, 384), (784, 384), (1168, 384), (1552, 512)]


def tile_of(tok):
    for i, (t0, n) in enumerate(TT):
        if t0 <= tok < t0 + n:
            return i
    raise ValueError(tok)
SLABS = [(0, 4), (4, 4), (8, 4), (12, 4), (16, 4), (20, 2)]
NSLOT = 3
ATT_SUB = 9
STAGE = 6


class Buf:
    __slots__ = ("name", "w", "r", "lo", "hi", "ov", "grp")

    def __init__(self, name, lo=None, hi=None, grp=None):
        self.name = name
        self.w = None
        self.r = {}
        self.lo = lo
        self.hi = hi
        self.ov = []
        self.grp = grp or name


class Eng:
    def __init__(self, name, sem):
        self.name = name
        self.sem = sem
        self.count = 0
        self.known = {}
        self.prog = []


class Ctx:
    def __init__(self, nc, sems):
        self.nc = nc
        self.E = {k: Eng(k, v) for k, v in sems.items()}
        self.semobj = dict(sems)
        self.dma_cnt = {}
        self.ninst = 0
        self.nwaits = 0
        self.ranged = {}

    def rbuf(self, name, lo, size, grp=None):
        key = (name, lo, size)
        if key in self.ranged:
            return self.ranged[key]
        b = Buf(name, lo, lo + size, grp)
        for o in self.ranged.values():
            if o.lo < b.hi and b.lo < o.hi and o.grp != b.grp:
                o.ov.append(b)
                b.ov.append(o)
        self.ranged[key] = b
        return b

    def add_sem(self, key, handle):
        self.semobj[key] = handle
        self.dma_cnt[key] = 0

    def _waits(self, e, reads, writes):
        need = {}
        for b in reads:
            if b.w is not None and need.get(b.w[0], 0) < b.w[1]:
                need[b.w[0]] = b.w[1]
        for b in writes:
            if b.w is not None and need.get(b.w[0], 0) < b.w[1]:
                need[b.w[0]] = b.w[1]
            for s, v in b.r.items():
                if need.get(s, 0) < v:
                    need[s] = v
            for o in b.ov:
                if o.w is not None and need.get(o.w[0], 0) < o.w[1]:
                    need[o.w[0]] = o.w[1]
                for s, v in o.r.items():
                    if need.get(s, 0) < v:
                        need[s] = v
        out = []
        for s, v in need.items():
            if e.name == "pe" and s == "pe":
                continue
            if e.known.get(s, 0) < v:
                e.known[s] = v
                out.append((s, v))
        return out

    def _mark(self, tok, reads, writes):
        for b in reads:
            if b.r.get(tok[0], 0) < tok[1]:
                b.r[tok[0]] = tok[1]
        for b in writes:
            b.w = tok
            b.r = {}

    def op(self, eng, fn, reads=(), writes=()):
        e = self.E[eng]
        waits = self._waits(e, reads, writes)
        e.count += 1
        tok = (eng, e.count)
        semobj = self.semobj
        mysem = e.sem

        def run(h):
            for s, v in waits:
                h.wait_ge(semobj[s], v)
            fn(h).then_inc(mysem, 1)

        e.prog.append(run)
        self.ninst += 1
        self.nwaits += len(waits)
        self._mark(tok, reads, writes)

    def dma_group(self, queue, semkey, items, reads=(), writes=()):
        e = self.E[queue]
        waits = self._waits(e, reads, writes)
        self.dma_cnt[semkey] += 16 * len(items)
        tok = (semkey, self.dma_cnt[semkey])
        semobj = self.semobj

        def run(h):
            for s, v in waits:
                h.wait_ge(semobj[s], v)
            for o, i in items:
                h.dma_start(out=o, in_=i).then_inc(semobj[semkey], 16)

        e.prog.append(run)
        self.ninst += len(items)
        self._mark(tok, reads, writes)

    def wait_all(self, eng, bufs):
        e = self.E[eng]
        waits = self._waits(e, bufs, bufs)
        semobj = self.semobj

        def run(h):
            for s, v in waits:
                h.wait_ge(semobj[s], v)

        e.prog.append(run)

    def replay(self, block):
        E = self.E

        @block.sync
        def _(h):
            for f in E["sp"].prog:
                f(h)

        @block.tensor
        def _(h):
            for f in E["pe"].prog:
                f(h)

        @block.scalar
        def _(h):
            for f in E["act"].prog:
                f(h)

        @block.vector
        def _(h):
            for f in E["dve"].prog:
                f(h)

        @block.gpsimd
        def _(h):
            for f in E["pool"].prog:
                f(h)


def build_program(stage=6, phases=None, nffn=4, nj=NJ):
    nc = bass.Bass("TRN2", target_bir_lowering=False)

    def din(name, shape):
        return nc.dram_tensor(name, list(shape), F32, kind="ExternalInput").ap()

    xT = din("xT", [2, 128, KC, SEQ])
    metaT = din("metaT", [2, 128, KC, NM])
    wg = din("wg", [nffn, 128, nj, KC, 128])
    wu = din("wu", [nffn, 128, nj, KC, 128])
    wd = din("wd", [nffn, 128, nj, D])
    cst = din("cst", [128, 288])
    cbias_d = din("cbias", [128, 256])
    wq = din("wq", [128, KC, D])
    wk2 = din("wk2", [128, KC, 4, 128])
    wv = din("wv", [128, KC, 256])
    wo = din("wo", [128, KC, D])
    wpi = din("wpi", [128, KC, D])
    wpg = din("wpg", [128, 4, 2, 256])
    wpo = din("wpo", [128, KC, D])
    bband = din("bband", [4, 128, 3, 4, 128])
    bmeta = din("bmeta", [4, 16, 4, 128])
    bm0 = din("bm0", [4, 128, 4, 16])
    bmm = din("bmm", [4, 16, 4, 16])
    outT = nc.dram_tensor("outT", [2, 128, KC, SEQ], F32, kind="ExternalOutput").ap()

    slopes = [2.0 ** (-8.0 * (h + 1) / 16.0) for h in range(16)]

    with ExitStack() as st:
        ent = st.enter_context
        TOTAL_F32 = 212480 // 4
        sb = ent(nc.sbuf_tensor("sb", [128, TOTAL_F32], F32))
        ps = ent(nc.psum_tensor("ps", [128, 8, 512], F32))
        sems = {k: ent(nc.semaphore("s_" + k)) for k in ["pe", "act", "dve", "pool", "sp"]}
        c = Ctx(nc, sems)
        for k in ["ws0", "ws1", "ws2", "mw", "mo", "mq0", "mq1", "tb", "cst"] + ["xin%d" % i for i in range(5)] + ["out%d" % i for i in range(5)]:
            c.add_sem(k, ent(nc.semaphore("d_" + k)))
        block = ent(nc.Block())

        def carve(off, nbytes, dt=F32):
            a = sb[:, off // 4:(off + nbytes) // 4]
            if dt == BF16:
                a = a.bitcast(BF16)
            return a

        OFF_H, OFF_XN, OFF_C, OFF_NS, OFF_A = 0, 66048, 99072, 101120, 109312
        H = carve(OFF_H, 66048).rearrange("p (k t) -> p k t", k=KC)
        XN = carve(OFF_XN, 33024, BF16).rearrange("p (k t) -> p k t", k=KC)
        CST = carve(OFF_C, 288 * 4)
        gains = CST[:, 0:56].rearrange("p (g k) -> p g k", k=KC)
        qkg = CST[:, 56:58]
        sink_sb = CST[:, 58:74]
        es3 = CST[:, 74:90].rearrange("p (h o) -> p h o", o=1)
        invc = CST[:, 224:288].rearrange("p (g e) -> p g e", e=16)
        ones_bf = carve(OFF_C + 1280, 256, BF16)
        bd_ones = carve(OFF_C + 1536, 256, BF16)
        sq = [carve(OFF_NS + 1024 * i, 1024, BF16) for i in range(2)]
        rs = [carve(OFF_NS + 2048 + 2048 * i, 2048) for i in range(2)]
        sm = carve(OFF_NS + 6144, 2048)
        cbias = sm[:, 256:512].rearrange("p (h q) -> p h q", q=16)

        BH = [[Buf("H%d_%d" % (k, t)) for t in range(5)] for k in range(KC)]
        BXN = [[Buf("XN%d_%d" % (k, t)) for t in range(5)] for k in range(KC)]
        Bsq = [Buf("sq0"), Buf("sq1")]
        Brs = [Buf("rs0"), Buf("rs1")]
        Bsm = Buf("sm")
        BPS = [Buf("ps%d" % i) for i in range(8)]
        Bcst = Buf("cst")
        allH = [b for row in BH for b in row]
        allXN = [b for row in BXN for b in row]

        pending = {"gi": None}

        rot = {"sqp": 0, "r4": 0, "r3": 0, "gu": 0, "dn": 0, "pv": 0, "pt": 0, "tmp": 0, "rd": 0, "rs": 0}

        def nxt(key, n):
            v = rot[key]
            rot[key] = (v + 1) % n
            return v

        c.dma_group("sp", "cst", [(CST, cst), (sm[:, 256:512], cbias_d)], writes=[Bcst])
        c.op("dve", lambda h: h.memset(ones_bf, 1.0), writes=[Bcst], reads=[Bcst])
        c.op("dve", lambda h: h.memset(bd_ones, 0.0), writes=[Bcst], reads=[Bcst])
        c.op("dve", lambda h: h.memset(bd_ones[0:64, 0:64], 1.0), writes=[Bcst], reads=[Bcst])
        c.op("dve", lambda h: h.memset(bd_ones[64:128, 64:128], 1.0), writes=[Bcst], reads=[Bcst])
        c.op("act", lambda h: h.activation(out=CST[:, 74:90], in_=sink_sb, func=AF.Exp), reads=[Bcst], writes=[Bcst])

        def rmsnorm(gi):
            if pending["gi"] == gi:
                pending["gi"] = None
                return
            for ti, (t0, n) in enumerate(TT):
                _norm_tile(gi, ti, t0, n)

        def _norm_tile(gi, ti, t0, n):
            for kc in range(KC):
                s_i = kc % 2
                c.op("act", lambda h, kc=kc, s_i=s_i: h.activation(out=sq[s_i][:, :n], in_=H[:, kc, t0:t0 + n], func=AF.Square),
                     reads=[BH[kc][ti]], writes=[Bsq[s_i]])
                c.op("pe", lambda h, kc=kc, s_i=s_i: h.matmul(ps[:, 7, :n], lhsT=ones_bf, rhs=sq[s_i][:, :n], start=(kc == 0), stop=(kc == KC - 1)),
                     reads=[Bsq[s_i], Bcst], writes=[BPS[7]])
            r_i = nxt("rs", 2)
            r = rs[r_i]
            c.op("act", lambda h: h.activation(out=r[:, :n], in_=ps[:, 7, :n], func=AF.Ln, scale=1.0 / D, bias=CST[:, 90:91]),
                 reads=[BPS[7], Bcst], writes=[Brs[r_i]])
            c.op("act", lambda h: h.activation(out=r[:, :n], in_=r[:, :n], func=AF.Exp, scale=-0.5), reads=[Brs[r_i]], writes=[Brs[r_i]])
            for kc in range(KC):
                c.op("dve", lambda h, kc=kc: h.scalar_tensor_tensor(out=XN[:, kc, t0:t0 + n], in0=H[:, kc, t0:t0 + n], scalar=gains[:, gi, kc:kc + 1],
                                                                in1=r[:, :n], op0=ALU.mult, op1=ALU.mult),
                     reads=[BH[kc][ti], Brs[r_i], Bcst], writes=[BXN[kc][ti]])

        def ffn(n_ffn, next_gi=None, tail=None):
            B = {}
            for s_ in range(NSLOT):
                B["slot%d" % s_] = c.rbuf("ffn.slot%d" % s_, s_ * 24576, 24576)
            for a_ in range(2):
                for j_ in range(4):
                    B["act%d_%d" % (a_, j_)] = c.rbuf("ffn.act%d_%d" % (a_, j_), 73728 + 4096 * a_ + 1024 * j_, 1024)
                B["sg%d" % a_] = c.rbuf("ffn.sg%d" % a_, 81920 + 2048 * a_, 2048)
            slot_g, slot_u, slot_d = [], [], []
            for s in range(NSLOT):
                base = OFF_A + s * 24576
                slot_g.append(carve(base, 8192, BF16).rearrange("p (j k c) -> p j k c", j=4, k=KC))
                slot_u.append(carve(base + 8192, 8192, BF16).rearrange("p (j k c) -> p j k c", j=4, k=KC))
                slot_d.append(carve(base + 16384, 8192, BF16).rearrange("p (j m) -> p j m", j=4))
            actb = [carve(OFF_A + 73728 + 4096 * a, 4096, BF16).rearrange("p (j n) -> p j n", j=4) for a in range(2)]
            sgb = [carve(OFF_A + 81920 + 2048 * a, 2048) for a in range(2)]

            def load_slab(si):
                j0, S = SLABS[si]
                s = si % NSLOT
                c.dma_group("pool", "ws%d" % s,
                            [(slot_g[s][:, 0:S], wg[n_ffn, :, j0:j0 + S]),
                             (slot_u[s][:, 0:S], wu[n_ffn, :, j0:j0 + S]),
                             (slot_d[s][:, 0:S], wd[n_ffn, :, j0:j0 + S])],
                            writes=[B["slot%d" % s]])

            for si in range(NSLOT):
                load_slab(si)
            rmsnorm(n_ffn)
            steps = [(si, ti) for si in range(len(SLABS)) for ti in range(5)]

            def gu(idx):
                si, ti = steps[idx]
                j0, S = SLABS[si]
                s = si % NSLOT
                t0, n = TT[ti]
                ab = idx % 2
                for j in range(S):
                    p = nxt("gu", 2)

                    def mm(h, j=j, p=p):
                        for kc in range(KC):
                            h.matmul(ps[:, 2 * p, :n], lhsT=slot_g[s][:, j, kc, :], rhs=XN[:, kc, t0:t0 + n], start=(kc == 0), stop=(kc == KC - 1))
                        for kc in range(KC):
                            ins = h.matmul(ps[:, 2 * p + 1, :n], lhsT=slot_u[s][:, j, kc, :], rhs=XN[:, kc, t0:t0 + n], start=(kc == 0), stop=(kc == KC - 1))
                        return ins
                    c.op("pe", mm, reads=[B["slot%d" % s]] + [BXN[kc][ti] for kc in range(KC)], writes=[BPS[2 * p], BPS[2 * p + 1]])
                    c.op("act", lambda h, p=p: h.activation(out=sgb[p][:, :n], in_=ps[:, 2 * p, :n], func=AF.Silu),
                         reads=[BPS[2 * p]], writes=[B["sg%d" % p]])
                    c.op("dve", lambda h, p=p, j=j: h.tensor_tensor(out=actb[ab][:, j, :n], in0=sgb[p][:, :n], in1=ps[:, 2 * p + 1, :n], op=ALU.mult),
                         reads=[B["sg%d" % p], BPS[2 * p + 1]], writes=[B["act%d_%d" % (ab, j)]])

            def down(idx):
                si, ti = steps[idx]
                j0, S = SLABS[si]
                s = si % NSLOT
                t0, n = TT[ti]
                ab = idx % 2
                for m in range(KC):
                    bk = 4 + nxt("dn", 3)

                    def mm(h, m=m, bk=bk):
                        for j in range(S):
                            ins = h.matmul(ps[:, bk, :n], lhsT=slot_d[s][:, j, m * 128:(m + 1) * 128], rhs=actb[ab][:, j, :n], start=(j == 0), stop=(j == S - 1))
                        return ins
                    c.op("pe", mm, reads=[B["slot%d" % s]] + [B["act%d_%d" % (ab, j)] for j in range(S)], writes=[BPS[bk]])
                    c.op("dve", lambda h, m=m, bk=bk: h.scalar_tensor_tensor(out=H[:, m, t0:t0 + n], in0=ps[:, bk, :n], scalar=0.5, in1=H[:, m, t0:t0 + n],
                                                                         op0=ALU.mult, op1=ALU.add),
                         reads=[BPS[bk], BH[m][ti]], writes=[BH[m][ti]])
                if ti == 4 and si + NSLOT < len(SLABS):
                    load_slab(si + NSLOT)
                if next_gi is not None and si == len(SLABS) - 1:
                    _norm_tile(next_gi, ti, t0, n)
                if tail is not None and si == len(SLABS) - 1:
                    tail(ti)

            for idx in range(len(steps)):
                gu(idx)
                if idx > 0:
                    down(idx - 1)
            down(len(steps) - 1)
            pending["gi"] = next_gi

        def attention(gi, next_gi=None, tail=None):
            A = OFF_A
            B = {}

            def rb(name, off, size, grp=None):
                B[name] = c.rbuf("att." + name, off, size, grp and "att." + grp)
            Vd = carve(A + 0, 17408, BF16).rearrange("p (b k d) -> p b k d", b=17, k=4)
            for i in range(17):
                rb("V%d" % i, 1024 * i, 1024)
            kTa = carve(A + 17408, 4128, BF16)
            rb("kTa", 17408, 4128)
            PT_off = [21536, 22560] + [45664 + 1024 * i for i in range(5)] + [100000]
            PT = [carve(A + o, 1024, BF16) for o in PT_off]
            for i, o in enumerate(PT_off):
                rb("PT%d" % i, o, 1024)
            qT = carve(A + 24576, 8256, BF16).rearrange("p (c t) -> p c t", c=2)
            rb("qT0", 24576, 4128)
            rb("qT1", 24576 + 4128, 4128)
            kTb = carve(A + 32832, 4128, BF16)
            rb("kTb", 32832, 4128)
            Bband = carve(A + 36960, 6144).rearrange("p (r x) -> p r x", r=3)
            Bmeta = carve(A + 43104, 2048)
            Bm0 = carve(A + 45152, 256)
            Bmm = carve(A + 45408, 256)
            rb("bias", 36960, 8704)
            tmp = [carve(A + 50784 + 2048 * i, 2048) for i in range(2)]
            rd = [carve(A + 54880 + 2048 * i, 2048) for i in range(2)] + [carve(A + 101024, 2048)]
            rb("rd2", 101024, 2048)
            wq_sb = [carve(A + 58976 + 4096 * i, 4096, BF16).rearrange("p (k c) -> p k c", k=KC) for i in range(2)]
            wk_sb = [carve(A + 67168 + 2048 * i, 2048, BF16).rearrange("p (k c) -> p k c", k=KC) for i in range(2)]
            for i in range(2):
                rb("tmp%d" % i, 50784 + 2048 * i, 2048)
                rb("rd%d" % i, 54880 + 2048 * i, 2048)
                rb("wq%d" % i, 58976 + 4096 * i, 4096)
                rb("wk%d" % i, 67168 + 2048 * i, 2048)
            oT = carve(A + 71264, 8256, BF16).rearrange("p (c t) -> p c t", c=2)
            for cc_ in range(2):
                for t_ in range(5):
                    rb("oT%d_%d" % (cc_, t_), 71264, 8256, grp="oT")
            wo_sb = carve(A + 79520, 16384, BF16).rearrange("p (k c) -> p k c", k=KC)
            rb("wo", 79520, 16384)
            wv_sb = carve(A + 95904, 4096, BF16).rearrange("p (k c) -> p k c", k=KC)
            rb("wv", 95904, 4096)
            NPT = len(PT)

            c.dma_group("pool", "mw", [(wv_sb, wv)], writes=[B["wv"]])
            c.op("dve", lambda h: h.memset(Vd[:, :, :, 64:128], 1.0), writes=[B["V%d" % i] for i in range(17)])
            c.op("dve", lambda h: h.memset(kTa[64:128, :], 0.0), writes=[B["kTa"]])
            c.op("dve", lambda h: h.memset(kTb[0:64, :], 0.0), writes=[B["kTb"]])

            def load_k(k):
                i = k % 2
                c.dma_group("pool", "mq%d" % i, [(wq_sb[i], wq[:, :, k * 256:(k + 1) * 256]), (wk_sb[i], wk2[:, :, k, :])],
                            writes=[B["wq%d" % i], B["wk%d" % i]])
            load_k(0)
            load_k(1)
            c.dma_group("pool", "mo", [(wo_sb, wo)], writes=[B["wo"]])
            rmsnorm(gi)
            def vproj(blk):
                ks, nk = (0, 16) if blk == 0 else (16 + 128 * (blk - 1), 128)
                ti = tile_of(ks)
                bk = nxt("r3", 3)

                def mm(h):
                    for kc in range(KC):
                        ins = h.matmul(ps[:nk, bk, 0:256], lhsT=XN[:, kc, ks:ks + nk], rhs=wv_sb[:, kc, :], start=(kc == 0), stop=(kc == KC - 1))
                    return ins
                c.op("pe", mm, reads=[B["wv"]] + [BXN[kc][ti] for kc in range(KC)], writes=[BPS[bk]])
                src = ps[:nk, bk, 0:256].rearrange("p (k d) -> p k d", k=4)
                c.op("act", lambda h: h.activation(out=Vd[:nk, blk, :, 0:64], in_=src, func=AF.Copy),
                     reads=[BPS[bk]], writes=[B["V%d" % blk]])
            for blk in range(17):
                vproj(blk)
            if ATT_SUB <= 1:
                return

            def proj(k, wi, ti, t0, n, which):
                bk = nxt("r3", 3)
                if which < 2:
                    def mm(h):
                        for kc in range(KC):
                            ins = h.matmul(ps[:, bk, :n], lhsT=wq_sb[wi][:, kc, which * 128:(which + 1) * 128], rhs=XN[:, kc, t0:t0 + n],
                                           start=(kc == 0), stop=(kc == KC - 1))
                        return ins
                    wb = B["wq%d" % wi]
                else:
                    def mm(h):
                        for kc in range(KC):
                            ins = h.matmul(ps[:, bk, :n], lhsT=wk_sb[wi][:, kc, :], rhs=XN[:, kc, t0:t0 + n], start=(kc == 0), stop=(kc == KC - 1))
                        return ins
                    wb = B["wk%d" % wi]
                c.op("pe", mm, reads=[wb] + [BXN[kc][ti] for kc in range(KC)], writes=[BPS[bk]])
                sqi = nxt("sqp", 2)
                sbk = (7, 3)[sqi]
                c.op("act", lambda h: h.activation(out=sq[sqi][:, :n], in_=ps[:, bk, :n], func=AF.Square), reads=[BPS[bk]], writes=[Bsq[sqi]])
                c.op("pe", lambda h: h.matmul(ps[:, sbk, :n], lhsT=bd_ones, rhs=sq[sqi][:, :n], start=True, stop=True), reads=[Bsq[sqi], Bcst], writes=[BPS[sbk]])
                r_i = nxt("rs", 2)
                r = rs[r_i]
                if which < 2:
                    c.op("act", lambda h: h.activation(out=r[:, :n], in_=ps[:, sbk, :n], func=AF.Ln, bias=CST[:, 91:92]),
                         reads=[BPS[sbk], Bcst], writes=[Brs[r_i]])
                else:
                    c.op("act", lambda h: h.activation(out=r[:, :n], in_=ps[:, sbk, :n], func=AF.Ln, scale=1.0 / 64.0, bias=CST[:, 90:91]),
                         reads=[BPS[sbk], Bcst], writes=[Brs[r_i]])
                c.op("act", lambda h: h.activation(out=r[:, :n], in_=r[:, :n], func=AF.Exp, scale=-0.5), reads=[Brs[r_i]], writes=[Brs[r_i]])
                if which < 2:
                    c.op("dve", lambda h: h.scalar_tensor_tensor(out=qT[:, which, t0:t0 + n], in0=ps[:, bk, :n], scalar=qkg[:, 0:1],
                                                             in1=r[:, :n], op0=ALU.mult, op1=ALU.mult),
                         reads=[BPS[bk], Brs[r_i], Bcst], writes=[B["qT%d" % which]])
                else:
                    c.op("dve", lambda h: h.scalar_tensor_tensor(out=kTa[0:64, t0:t0 + n], in0=ps[0:64, bk, :n], scalar=qkg[0:64, 1:2],
                                                             in1=r[0:64, :n], op0=ALU.mult, op1=ALU.mult),
                         reads=[BPS[bk], Brs[r_i], Bcst], writes=[B["kTa"]])
                    c.op("dve", lambda h: h.scalar_tensor_tensor(out=kTb[64:128, t0:t0 + n], in0=ps[64:128, bk, :n], scalar=qkg[64:128, 1:2],
                                                             in1=r[64:128, :n], op0=ALU.mult, op1=ALU.mult),
                         reads=[BPS[bk], Brs[r_i], Bcst], writes=[B["kTb"]])

            def kg_scores(k, qs, nq, W, blk, ks, nk, bias_ap, cq):
                bk = nxt("r4", 4)

                def mm(h):
                    for hh in range(2):
                        ins = h.matmul(ps[:nk, bk, hh * 2 * nq:(hh + 1) * 2 * nq].rearrange("p (c n) -> p c n", c=2),
                                       lhsT=(kTa if hh == 0 else kTb)[:, ks:ks + nk],
                                       rhs=qT[:, :, qs:qs + nq], start=True, stop=True)
                    return ins
                c.op("pe", mm, reads=[B["kTa"], B["kTb"], B["qT0"], B["qT1"]], writes=[BPS[bk]])
                tb = nxt("tmp", 2)
                if cq is None:
                    c.op("dve", lambda h: h.tensor_tensor(out=tmp[tb][:nk, :W], in0=ps[:nk, bk, :W], in1=bias_ap, op=ALU.add),
                         reads=[BPS[bk], B["bias"]], writes=[B["tmp%d" % tb]])
                else:
                    c.op("dve", lambda h: h.tensor_tensor(out=tmp[tb][:nk, :W].rearrange("p (g n) -> p g n", g=4),
                                                          in0=ps[:nk, bk, :W].rearrange("p (g n) -> p g n", g=4),
                                                          in1=cbias[:nk, 4 * k:4 * k + 4, cq:cq + 1].to_broadcast([nk, 4, nq]), op=ALU.add),
                         reads=[BPS[bk], Bcst], writes=[B["tmp%d" % tb]])
                    c.op("dve", lambda h: h.tensor_tensor(out=tmp[tb][:nk, :W], in0=tmp[tb][:nk, :W], in1=bias_ap, op=ALU.add),
                         reads=[B["tmp%d" % tb], B["bias"]], writes=[B["tmp%d" % tb]])
                pt = nxt("pt", NPT)
                c.op("act", lambda h: h.activation(out=PT[pt][:nk, :W], in_=tmp[tb][:nk, :W], func=AF.Exp),
                     reads=[B["tmp%d" % tb]], writes=[B["PT%d" % pt]])
                return (blk, nk, pt)

            def qgroup_scores(k, qb, qs, nq):
                W = 4 * nq
                if qb < 0:
                    kgs = [(0, 0, 16, Bmm[0:16, 0:W], None), (1, 16, 128, Bm0[:, 0:W], None)]
                else:
                    kgs = [(0, 0, 16, Bmeta[0:16, :], qb)]
                    for r_ in (-1, 0, 1):
                        if 0 <= qb + r_ <= 15:
                            kgs.append((qb + r_ + 1, 16 + 128 * (qb + r_), 128, Bband[:, r_ + 1, :], None))
                parts = [kg_scores(k, qs, nq, W, blk, ks, nk, bias_ap, cq) for (blk, ks, nk, bias_ap, cq) in kgs]
                return (k, qb, qs, nq, parts)

            def qgroup_pv(state):
                k, qb, qs, nq, parts = state
                W = 4 * nq
                bA = 4 + nxt("pv", 3)
                oti = tile_of(qs)

                def mm2(h):
                    for i, (blk, nk, pt) in enumerate(parts):
                        ins = h.matmul(ps[:, bA, :W], lhsT=Vd[:nk, blk, k, :], rhs=PT[pt][:nk, :W], start=(i == 0), stop=(i == len(parts) - 1))
                    return ins
                c.op("pe", mm2, reads=[B["V%d" % blk] for (blk, nk, pt) in parts] + [B["PT%d" % pt] for (blk, nk, pt) in parts],
                     writes=[BPS[bA]])
                ri = nxt("rd", 3)
                c.op("dve", lambda h: h.tensor_tensor(out=rd[ri][64:128, :W].rearrange("p (g n) -> p g n", g=4),
                                                      in0=ps[64:128, bA, :W].rearrange("p (g n) -> p g n", g=4),
                                                      in1=es3[64:128, 4 * k:4 * k + 4, :].to_broadcast([64, 4, nq]), op=ALU.add),
                     reads=[BPS[bA], Bcst], writes=[B["rd%d" % ri]])
                c.op("act", lambda h: h.activation(out=rd[ri][64:128, :W], in_=rd[ri][64:128, :W], func=AF.Ln), reads=[B["rd%d" % ri]], writes=[B["rd%d" % ri]])
                c.op("act", lambda h: h.activation(out=rd[ri][64:128, :W], in_=rd[ri][64:128, :W], func=AF.Exp, scale=-1.0), reads=[B["rd%d" % ri]], writes=[B["rd%d" % ri]])
                return (qs, nq, W, bA, ri, oti)

            def qgroup_fin(fin):
                qs, nq, W, bA, ri, oti = fin
                for hh in range(2):
                    lo = 64 * hh
                    c.op("dve", lambda h, hh=hh, lo=lo: h.tensor_tensor(
                        out=oT[lo:lo + 64, :, qs:qs + nq],
                        in0=ps[0:64, bA, :W].rearrange("p (h c n) -> p h c n", c=2, h=2)[:, hh, :, :],
                        in1=rd[ri][64:128, :W].rearrange("p (h c n) -> p h c n", c=2, h=2)[:, hh, :, :], op=ALU.mult),
                        reads=[BPS[bA], B["rd%d" % ri]], writes=[B["oT0_%d" % oti], B["oT1_%d" % oti]])

            def oproj(k, ti, t0, n):
                for m in range(KC):
                    bk = nxt("r3", 3)

                    def mm(h, m=m, bk=bk):
                        for cc in range(2):
                            ins = h.matmul(ps[:, bk, :n], lhsT=wo_sb[:, 2 * k + cc, m * 128:(m + 1) * 128], rhs=oT[:, cc, t0:t0 + n], start=(cc == 0), stop=(cc == 1))
                        return ins
                    c.op("pe", mm, reads=[B["wo"], B["oT0_%d" % ti], B["oT1_%d" % ti]], writes=[BPS[bk]])
                    c.op("dve", lambda h, m=m, bk=bk: h.tensor_tensor(out=H[:, m, t0:t0 + n], in0=ps[:, bk, :n], in1=H[:, m, t0:t0 + n], op=ALU.add),
                         reads=[BPS[bk], BH[m][ti]], writes=[BH[m][ti]])
                if k == 3 and next_gi is not None:
                    _norm_tile(next_gi, ti, t0, n)
                if k == 3 and tail is not None:
                    tail(ti)

            def att_k(k):
                wi = k % 2
                c.dma_group("sp", "tb", [(Bband.rearrange("p r x -> p (r x)"), bband[k].rearrange("p r g a -> p (r g a)")),
                                         (Bmeta[0:16, :], bmeta[k].rearrange("p g a -> p (g a)")),
                                         (Bm0, bm0[k].rearrange("p g a -> p (g a)")),
                                         (Bmm[0:16, :], bmm[k].rearrange("p g a -> p (g a)"))], writes=[B["bias"]])
                for ti, (t0, n) in enumerate(TT):
                    for which in range(3):
                        proj(k, wi, ti, t0, n, which)
                if k + 2 < 4:
                    load_k(k + 2)
                if ATT_SUB <= 2:
                    return
                prev = None
                fin = None
                for (qb, qs, nq) in [(-1, 0, 16)] + [(qb, 16 + 128 * qb, 128) for qb in range(16)]:
                    stt = qgroup_scores(k, qb, qs, nq)
                    nfin = qgroup_pv(prev) if prev is not None else None
                    if fin is not None:
                        qgroup_fin(fin)
                    fin = nfin
                    prev = stt
                nfin = qgroup_pv(prev)
                if fin is not None:
                    qgroup_fin(fin)
                qgroup_fin(nfin)
                if ATT_SUB <= 4:
                    return
                for ti, (t0, n) in enumerate(TT):
                    oproj(k, ti, t0, n)

            for k in range(4):
                att_k(k)
            if ATT_SUB >= 9:
                pending["gi"] = next_gi

        def poolmix(gi, next_gi=None, tail=None):
            A = OFF_A
            B = {}
            wpi_sb = carve(A, 16384, BF16).rearrange("p (k c) -> p k c", k=KC)
            B["wpi"] = c.rbuf("pool.wpi", 0, 16384)
            Us = [carve(A + 16384, 8320), carve(A + 94848, 8320)]
            B["U0"] = c.rbuf("pool.U0", 16384, 8320)
            B["U1"] = c.rbuf("pool.U1", 94848, 8320)
            T = [carve(A + 86528, 8320), carve(A + 24704, 8320)]
            B["T0"] = c.rbuf("pool.T0", 86528, 8320)
            B["T1"] = c.rbuf("pool.T1", 24704, 8320)
            wpg_sb = carve(A + 33024, 4096, BF16).rearrange("p (g k c) -> p g k c", g=4, k=2)
            B["wpg"] = c.rbuf("pool.wpg", 33024, 4096)
            wpo_sb = carve(A + 37120, 16384, BF16).rearrange("p (k c) -> p k c", k=KC)
            B["wpo"] = c.rbuf("pool.wpo", 37120, 16384)
            pooled = carve(A + 53504, 33024, BF16).rearrange("p (k t) -> p k t", k=KC)
            for i in range(KC):
                B["pl%d" % i] = c.rbuf("pool.pl%d" % i, 53504 + 4128 * i, 4128)
            mixed = carve(A + 86528, 8192, BF16).rearrange("p (k n) -> p k n", k=KC)
            BT = [B["T0"], B["T1"]]
            c.dma_group("pool", "mw", [(wpi_sb, wpi)], writes=[B["wpi"]])
            c.dma_group("pool", "mo", [(wpg_sb, wpg), (wpo_sb, wpo)], writes=[B["wpg"], B["wpo"]])
            rmsnorm(gi)
            for ui in range(2):
                c.op("dve", lambda h, ui=ui: h.memset(Us[ui][:, 0:8], 0.0), writes=[B["U%d" % ui]])
                c.op("dve", lambda h, ui=ui: h.memset(Us[ui][:, 2072:2080], 0.0), writes=[B["U%d" % ui]])
            def chunk(ch):
                g = ch // 2
                hw = 1 << g
                U = Us[ch % 2]
                BU = B["U%d" % (ch % 2)]
                for ti, (t0, n) in enumerate(TT):
                    bk = nxt("r3", 3)

                    def mm(h, bk=bk, t0=t0, n=n):
                        for kc in range(KC):
                            ins = h.matmul(ps[:, bk, :n], lhsT=wpi_sb[:, kc, ch * 128:(ch + 1) * 128], rhs=XN[:, kc, t0:t0 + n], start=(kc == 0), stop=(kc == KC - 1))
                        return ins
                    c.op("pe", mm, reads=[B["wpi"]] + [BXN[kc][ti] for kc in range(KC)], writes=[BPS[bk]])
                    c.op("act", lambda h, bk=bk, t0=t0, n=n: h.activation(out=U[:, 8 + t0:8 + t0 + n], in_=ps[:, bk, :n], func=AF.Copy),
                         reads=[BPS[bk]], writes=[BU])
                src, bsrc = U, BU
                s_ = 1
                lvl = 0
                while s_ <= hw:
                    ln = 2080 - 2 * s_ + 1
                    dst, bdst = T[lvl % 2], BT[lvl % 2]
                    c.op("dve", lambda h, src=src, dst=dst, s_=s_, ln=ln: h.tensor_tensor(out=dst[:, 0:ln], in0=src[:, 0:ln], in1=src[:, s_:s_ + ln], op=ALU.add),
                         reads=[bsrc], writes=[bdst])
                    src, bsrc = dst, bdst
                    s_ *= 2
                    lvl += 1
                w_ = 2 * hw
                c.op("dve", lambda h: h.scalar_tensor_tensor(out=pooled[:, ch, :], in0=src[:, 8 - hw:8 - hw + L], scalar=1.0 / w_, in1=U[:, 8:8 + L],
                                                         op0=ALU.mult, op1=ALU.subtract),
                     reads=[bsrc, BU], writes=[B["pl%d" % ch]])
                c.op("dve", lambda h: h.tensor_tensor(out=sm[:, 0:hw], in0=src[:, 8 - hw:8], in1=invc[:, g, 0:hw], op=ALU.mult),
                     reads=[bsrc, Bcst], writes=[Bsm])
                c.op("dve", lambda h: h.tensor_tensor(out=pooled[:, ch, 0:hw], in0=sm[:, 0:hw], in1=U[:, 8:8 + hw], op=ALU.subtract),
                     reads=[Bsm, BU], writes=[B["pl%d" % ch]])
                if hw > 1:
                    nr = hw - 1
                    tb_ = L - hw + 1
                    c.op("dve", lambda h: h.tensor_tensor(out=sm[:, 16:16 + nr], in0=src[:, 8 + tb_ - hw:8 + tb_ - hw + nr],
                                                          in1=invc[:, g, 8:8 + nr], op=ALU.mult),
                         reads=[bsrc, Bcst], writes=[Bsm])
                    c.op("dve", lambda h: h.tensor_tensor(out=pooled[:, ch, tb_:tb_ + nr], in0=sm[:, 16:16 + nr], in1=U[:, 8 + tb_:8 + tb_ + nr], op=ALU.subtract),
                         reads=[Bsm, BU], writes=[B["pl%d" % ch]])

            def mixtile(ti, t0, n):
                for g in range(4):
                    for mm_ in range(2):
                        bk = nxt("r3", 3)
                        oc = 2 * g + mm_

                        def mm(h, g=g, mm_=mm_, bk=bk):
                            for kk in range(2):
                                ins = h.matmul(ps[:, bk, :n], lhsT=wpg_sb[:, g, kk, mm_ * 128:(mm_ + 1) * 128], rhs=pooled[:, 2 * g + kk, t0:t0 + n], start=(kk == 0), stop=(kk == 1))
                            return ins
                        c.op("pe", mm, reads=[B["wpg"], B["pl%d" % (2 * g)], B["pl%d" % (2 * g + 1)]], writes=[BPS[bk]])
                        c.op("dve", lambda h, oc=oc, bk=bk: h.tensor_scalar(out=mixed[:, oc, :n], in0=ps[:, bk, :n], scalar1=gains[:, 6, oc:oc + 1], scalar2=None, op0=ALU.mult),
                             reads=[BPS[bk], Bcst], writes=[B["T0"]])
                for m in range(KC):
                    bk = nxt("r3", 3)

                    def mm(h, m=m, bk=bk):
                        for cc in range(KC):
                            ins = h.matmul(ps[:, bk, :n], lhsT=wpo_sb[:, cc, m * 128:(m + 1) * 128], rhs=mixed[:, cc, :n], start=(cc == 0), stop=(cc == KC - 1))
                        return ins
                    c.op("pe", mm, reads=[B["wpo"], B["T0"]], writes=[BPS[bk]])
                    c.op("dve", lambda h, m=m, bk=bk: h.tensor_tensor(out=H[:, m, t0:t0 + n], in0=ps[:, bk, :n], in1=H[:, m, t0:t0 + n], op=ALU.add),
                         reads=[BPS[bk], BH[m][ti]], writes=[BH[m][ti]])
                if next_gi is not None:
                    _norm_tile(next_gi, ti, t0, n)
                if tail is not None:
                    tail(ti)

            for ch in range(KC):
                chunk(ch)
            for ti, (t0, n) in enumerate(TT):
                mixtile(ti, t0, n)
            pending["gi"] = next_gi

        phases_sel = phases
        plist = [("ffn", 0, 0), ("att", 4, 4), ("ffn", 1, 1), ("ffn", 2, 2), ("pool", 5, 5), ("ffn", 3, 3)]
        sel = list(phases_sel if phases_sel is not None else range(min(stage, 6)))

        def load_tile(seq, ti):
            t0, n = TT[ti]
            items = []
            if t0 < NM:
                items.append((H[:, :, 0:NM], metaT[seq]))
            r0, r1 = max(t0, NM), t0 + n
            items.append((H[:, :, r0:r1], xT[seq, :, :, r0 - NM:r1 - NM]))
            c.dma_group("sp", "xin%d" % ti, items, writes=[BH[kc][ti] for kc in range(KC)])

        def store_tile(seq, ti):
            t0, n = TT[ti]
            r0, r1 = max(t0, NM), t0 + n
            c.dma_group("sp", "out%d" % ti, [(outT[seq, :, :, r0 - NM:r1 - NM], H[:, :, r0:r1])], reads=[BH[kc][ti] for kc in range(KC)])

        for ti in range(5):
            load_tile(0, ti)
        for seq in range(2):
            if seq == 0:
                pending["gi"] = None

            def tail(ti, seq=seq):
                store_tile(seq, ti)
                if seq == 0:
                    load_tile(1, ti)
                    if len(sel) > 0:
                        _norm_tile(plist[sel[0]][2], ti, TT[ti][0], TT[ti][1])
            for ii, pi in enumerate(sel):
                kind, arg, g_i = plist[pi]
                last = (ii + 1 == len(sel))
                nx = plist[sel[ii + 1]][2] if not last else None
                tl = tail if last else None
                if kind == "ffn":
                    ffn(arg, nx, tl)
                elif kind == "att":
                    attention(arg, nx, tl)
                else:
                    poolmix(arg, nx, tl)
            if seq == 0 and len(sel) > 0:
                pending["gi"] = plist[sel[0]][2]
        c.wait_all("sp", allH)
        c.replay(block)
    return nc


def _const_tables():
    slopes = np.array([2.0 ** (-8.0 * (h + 1) / 16.0) for h in range(16)], dtype=np.float64).reshape(4, 4)
    a = np.arange(128)
    bband = np.zeros((4, 128, 3, 4, 128), np.float32)
    for r in range(3):
        rel = (r - 1) * 128 + a[:, None] - a[None, :]
        valid = np.abs(rel) <= 128
        for k in range(4):
            for g in range(4):
                bband[k, :, r, g, :] = np.where(valid, -slopes[k, g] * np.abs(rel), NEG)
    m = np.arange(16)
    bmeta = np.zeros((4, 16, 4, 128), np.float32)
    bmm = np.zeros((4, 16, 4, 16), np.float32)
    bm0 = np.zeros((4, 128, 4, 16), np.float32)
    for k in range(4):
        for g in range(4):
            bmeta[k, :, g, :] = -slopes[k, g] * (16 + a[None, :] - m[:, None])
            bmm[k, :, g, :] = -slopes[k, g] * np.abs(m[None, :] - m[:, None])
            dist = 16 + a[:, None] - m[None, :]
            bm0[k, :, g, :] = np.where(dist <= 128, -slopes[k, g] * dist, NEG)
    invc = np.ones((4, 16), np.float32)
    for g in range(4):
        hw = 1 << g
        for t in range(hw):
            invc[g, t] = 1.0 / (t + hw)
        for i in range(hw - 1):
            invc[g, 8 + i] = 1.0 / (2 * hw - 1 - i)
    perm = [0, 2, 1, 3]
    bband = np.ascontiguousarray(bband[:, :, :, perm, :])
    bmeta = np.ascontiguousarray(bmeta[:, :, perm, :])
    bmm = np.ascontiguousarray(bmm[:, :, perm, :])
    bm0 = np.ascontiguousarray(bm0[:, :, perm, :])
    sl_p = slopes[:, perm].reshape(16, 1)
    cb = np.zeros((128, 16, 16), np.float32)
    cb[:] = (-sl_p * 128.0 * np.arange(16)[None, :])[None]
    return bband, bmeta, bm0, bmm, invc, cb.reshape(128, 256)


def _fm(v):
    return np.ascontiguousarray(v.reshape(KC, 128).T)


def _wfm(w):
    return np.ascontiguousarray(w.reshape(KC, 128, -1).transpose(1, 0, 2))


_CACHE = {}


def kernel(x, meta_tokens, ffn_norm, w_gate_up, w_down, mixer_norm, w_qkv, q_norm, k_norm,
           sink_logit, w_o, w_pool_in, w_pool_group, pool_scale, w_pool_out):
    f = np.float32
    x = np.asarray(x, f)
    bband, bmeta, bm0, bmm, invc, cbias = _const_tables()
    wg = np.empty((4, 128, NJ, KC, 128), f)
    wu = np.empty((4, 128, NJ, KC, 128), f)
    wd = np.empty((4, 128, NJ, D), f)
    for i in range(2):
        for ff in range(2):
            n = 2 * i + ff
            wgu = np.asarray(w_gate_up[i, ff], f)
            wg[n] = wgu[:, :2816].reshape(KC, 128, NJ, 128).transpose(1, 2, 0, 3)
            wu[n] = wgu[:, 2816:].reshape(KC, 128, NJ, 128).transpose(1, 2, 0, 3)
            wd[n] = np.asarray(w_down[i, ff], f).reshape(NJ, 128, D).transpose(1, 0, 2)
    cst = np.zeros((128, 288), f)
    gl = [ffn_norm[0, 0], ffn_norm[0, 1], ffn_norm[1, 0], ffn_norm[1, 1], mixer_norm[0], mixer_norm[1], pool_scale[0]]
    for gi, v in enumerate(gl):
        cst[:, gi * 8:(gi + 1) * 8] = _fm(np.asarray(v, f))
    cst[:, 56] = np.tile(np.asarray(q_norm[0], f), 2)
    cst[:, 57] = np.tile(np.asarray(k_norm[0], f), 2)
    cst[:, 58:74] = np.asarray(sink_logit[0], f).reshape(4, 4)[:, [0, 2, 1, 3]].reshape(1, 16)
    cst[:, 90] = EPS
    cst[:, 91] = 64.0 * EPS
    cst[:, 224:288] = invc.reshape(1, 64)
    wqkv = np.asarray(w_qkv[0], f)
    wq = _wfm(wqkv[:, :1024])
    wk = wqkv[:, 1024:1280].reshape(KC, 128, 4, 64).transpose(1, 0, 2, 3)
    wk2 = np.ascontiguousarray(np.concatenate([wk, wk], axis=3))
    wv = _wfm(wqkv[:, 1280:1536])
    wo = _wfm(np.asarray(w_o[0], f))
    wpi = _wfm(np.asarray(w_pool_in[0], f))
    wpg = np.ascontiguousarray(np.asarray(w_pool_group[0], f).reshape(4, 2, 128, 256).transpose(2, 0, 1, 3))
    wpo = _wfm(np.asarray(w_pool_out[0], f))
    metaT1 = np.ascontiguousarray(np.asarray(meta_tokens, f).T.reshape(KC, 128, NM).transpose(1, 0, 2))
    metaT = np.ascontiguousarray(np.stack([metaT1, metaT1]))
    shared = dict(metaT=metaT, wg=wg, wu=wu, wd=wd, cst=cst, wq=wq, wk2=wk2, wv=wv, wo=wo, wpi=wpi, wpg=wpg, wpo=wpo,
                  bband=bband, bmeta=bmeta, bm0=bm0, bmm=bmm, cbias=cbias)
    in_maps = []
    for core in range(8):
        xs = x[2 * core:2 * core + 2]
        xTc = np.ascontiguousarray(xs.reshape(2, SEQ, KC, 128).transpose(0, 3, 2, 1))
        d = dict(shared)
        d["xT"] = xTc
        in_maps.append(d)
    if STAGE not in _CACHE:
        _CACHE[STAGE] = build_program(STAGE)
    nc = _CACHE[STAGE]
    res = run_bass_kernel_spmd(nc, in_maps, core_ids=list(range(8)))
    out = np.empty((16, SEQ, D), f)
    for core in range(8):
        o = res.results[core]["outT"]
        out[2 * core:2 * core + 2] = o.transpose(0, 3, 2, 1).reshape(2, SEQ, D)
    return out
```

```python
import numpy as np
from contextlib import ExitStack
import concourse.bass as bass
import concourse.mybir as mybir
from concourse.bass_utils import run_bass_kernel_spmd

F32 = mybir.dt.float32
BF16 = mybir.dt.bfloat16
AF = mybir.ActivationFunctionType
ALU = mybir.AluOpType

D = 1024
SEQ = 2048
NM = 16
L = SEQ + NM
KC = 8
NJ = 22
EPS = 1e-6
NEG = -30000.0
TT = [(0, 400), (400, 384), (784, 384), (1168, 384), (1552, 512)]


def tile_of(tok):
    for i, (t0, n) in enumerate(TT):
        if t0 <= tok < t0 + n:
            return i
    raise ValueError(tok)
SLABS = [(0, 4), (4, 4), (8, 4), (12, 4), (16, 4), (20, 2)]
NSLOT = 3
ATT_SUB = 9
STAGE = 6


class Buf:
    __slots__ = ("name", "w", "r", "lo", "hi", "ov", "grp")

    def __init__(self, name, lo=None, hi=None, grp=None):
        self.name = name
        self.w = None
        self.r = {}
        self.lo = lo
        self.hi = hi
        self.ov = []
        self.grp = grp or name


class Eng:
    def __init__(self, name, sem):
        self.name = name
        self.sem = sem
        self.count = 0
        self.known = {}
        self.prog = []


class Ctx:
    def __init__(self, nc, sems):
        self.nc = nc
        self.E = {k: Eng(k, v) for k, v in sems.items()}
        self.semobj = dict(sems)
        self.dma_cnt = {}
        self.ninst = 0
        self.nwaits = 0
        self.ranged = {}

    def rbuf(self, name, lo, size, grp=None):
        key = (name, lo, size)
        if key in self.ranged:
            return self.ranged[key]
        b = Buf(name, lo, lo + size, grp)
        for o in self.ranged.values():
            if o.lo < b.hi and b.lo < o.hi and o.grp != b.grp:
                o.ov.append(b)
                b.ov.append(o)
        self.ranged[key] = b
        return b

    def add_sem(self, key, handle):
        self.semobj[key] = handle
        self.dma_cnt[key] = 0

    def _waits(self, e, reads, writes):
        need = {}
        for b in reads:
            if b.w is not None and need.get(b.w[0], 0) < b.w[1]:
                need[b.w[0]] = b.w[1]
        for b in writes:
            if b.w is not None and need.get(b.w[0], 0) < b.w[1]:
                need[b.w[0]] = b.w[1]
            for s, v in b.r.items():
                if need.get(s, 0) < v:
                    need[s] = v
            for o in b.ov:
                if o.w is not None and need.get(o.w[0], 0) < o.w[1]:
                    need[o.w[0]] = o.w[1]
                for s, v in o.r.items():
                    if need.get(s, 0) < v:
                        need[s] = v
        out = []
        for s, v in need.items():
            if e.name == "pe" and s == "pe":
                continue
            if e.known.get(s, 0) < v:
                e.known[s] = v
                out.append((s, v))
        return out

    def _mark(self, tok, reads, writes):
        for b in reads:
            if b.r.get(tok[0], 0) < tok[1]:
                b.r[tok[0]] = tok[1]
        for b in writes:
            b.w = tok
            b.r = {}

    def op(self, eng, fn, reads=(), writes=()):
        e = self.E[eng]
        waits = self._waits(e, reads, writes)
        e.count += 1
        tok = (eng, e.count)
        semobj = self.semobj
        mysem = e.sem

        def run(h):
            for s, v in waits:
                h.wait_ge(semobj[s], v)
            fn(h).then_inc(mysem, 1)

        e.prog.append(run)
        self.ninst += 1
        self.nwaits += len(waits)
        self._mark(tok, reads, writes)

    def dma_group(self, queue, semkey, items, reads=(), writes=()):
        e = self.E[queue]
        waits = self._waits(e, reads, writes)
        self.dma_cnt[semkey] += 16 * len(items)
        tok = (semkey, self.dma_cnt[semkey])
        semobj = self.semobj

        def run(h):
            for s, v in waits:
                h.wait_ge(semobj[s], v)
            for o, i in items:
                h.dma_start(out=o, in_=i).then_inc(semobj[semkey], 16)

        e.prog.append(run)
        self.ninst += len(items)
        self._mark(tok, reads, writes)

    def wait_all(self, eng, bufs):
        e = self.E[eng]
        waits = self._waits(e, bufs, bufs)
        semobj = self.semobj

        def run(h):
            for s, v in waits:
                h.wait_ge(semobj[s], v)

        e.prog.append(run)

    def replay(self, block):
        E = self.E

        @block.sync
        def _(h):
            for f in E["sp"].prog:
                f(h)

        @block.tensor
        def _(h):
            for f in E["pe"].prog:
                f(h)

        @block.scalar
        def _(h):
            for f in E["act"].prog:
                f(h)

        @block.vector
        def _(h):
            for f in E["dve"].prog:
                f(h)

        @block.gpsimd
        def _(h):
            for f in E["pool"].prog:
                f(h)


def build_program(stage=6, phases=None, nffn=4, nj=NJ):
    nc = bass.Bass("TRN2", target_bir_lowering=False)

    def din(name, shape):
        return nc.dram_tensor(name, list(shape), F32, kind="ExternalInput").ap()

    xT = din("xT", [2, 128, KC * SEQ])
    metaT = din("metaT", [2, 128, KC, NM])
    wg = din("wg", [nffn, 128, nj, KC, 128])
    wu = din("wu", [nffn, 128, nj, KC, 128])
    wd = din("wd", [nffn, 128, nj, D])
    cst = din("cst", [128, 288])
    cbias_d = din("cbias", [128, 256])
    wq = din("wq", [128, KC, D])
    wk2 = din("wk2", [128, KC, 4, 128])
    wv = din("wv", [128, KC, 256])
    wo = din("wo", [128, KC, D])
    wpi = din("wpi", [128, KC, D])
    wpg = din("wpg", [128, 4, 2, 256])
    wpo = din("wpo", [128, KC, D])
    bband = din("bband", [4, 128, 3, 4, 128])
    bmeta = din("bmeta", [4, 16, 4, 128])
    bm0 = din("bm0", [4, 128, 4, 16])
    bmm = din("bmm", [4, 16, 4, 16])
    outT = nc.dram_tensor("outT", [2, 128, KC * SEQ], F32, kind="ExternalOutput").ap()

    slopes = [2.0 ** (-8.0 * (h + 1) / 16.0) for h in range(16)]

    with ExitStack() as st:
        ent = st.enter_context
        TOTAL_F32 = 212480 // 4
        sb = ent(nc.sbuf_tensor("sb", [128, TOTAL_F32], F32))
        ps = ent(nc.psum_tensor("ps", [128, 8, 512], F32))
        sems = {k: ent(nc.semaphore("s_" + k)) for k in ["pe", "act", "dve", "pool", "sp"]}
        c = Ctx(nc, sems)
        for k in ["ws0", "ws1", "ws2", "mw", "mo", "mq0", "mq1", "tb", "cst"] + ["xin%d" % i for i in range(5)] + ["out%d" % i for i in range(5)]:
            c.add_sem(k, ent(nc.semaphore("d_" + k)))
        block = ent(nc.Block())

        def carve(off, nbytes, dt=F32):
            a = sb[:, off // 4:(off + nbytes) // 4]
            if dt == BF16:
                a = a.bitcast(BF16)
            return a

        OFF_H, OFF_XN, OFF_C, OFF_NS, OFF_A = 0, 66048, 99072, 101120, 109312
        H = carve(OFF_H, 66048).rearrange("p (k t) -> p k t", k=KC)
        XN = carve(OFF_XN, 33024, BF16).rearrange("p (k t) -> p k t", k=KC)
        CST = carve(OFF_C, 288 * 4)
        gains = CST[:, 0:56].rearrange("p (g k) -> p g k", k=KC)
        qkg = CST[:, 56:58]
        sink_sb = CST[:, 58:74]
        es3 = CST[:, 74:90].rearrange("p (h o) -> p h o", o=1)
        invc = CST[:, 224:288].rearrange("p (g e) -> p g e", e=16)
        ones_bf = carve(OFF_C + 1280, 256, BF16)
        bd_ones = carve(OFF_C + 1536, 256, BF16)
        sq = [carve(OFF_NS + 1024 * i, 1024, BF16) for i in range(2)]
        rs = [carve(OFF_NS + 2048 + 2048 * i, 2048) for i in range(2)]
        sm = carve(OFF_NS + 6144, 2048)
        cbias = sm[:, 256:512].rearrange("p (h q) -> p h q", q=16)

        BH = [[Buf("H%d_%d" % (k, t)) for t in range(5)] for k in range(KC)]
        BXN = [[Buf("XN%d_%d" % (k, t)) for t in range(5)] for k in range(KC)]
        Bsq = [Buf("sq0"), Buf("sq1")]
        Brs = [Buf("rs0"), Buf("rs1")]
        Bsm = Buf("sm")
        BPS = [Buf("ps%d" % i) for i in range(8)]
        Bcst = Buf("cst")
        allH = [b for row in BH for b in row]
        allXN = [b for row in BXN for b in row]

        pending = {"gi": None}

        rot = {"sqp": 0, "r4": 0, "r3": 0, "gu": 0, "dn": 0, "pv": 0, "pt": 0, "tmp": 0, "rd": 0, "rs": 0}

        def nxt(key, n):
            v = rot[key]
            rot[key] = (v + 1) % n
            return v

        c.dma_group("sp", "cst", [(CST, cst), (sm[:, 256:512], cbias_d)], writes=[Bcst])
        c.op("dve", lambda h: h.memset(ones_bf, 1.0), writes=[Bcst], reads=[Bcst])
        c.op("dve", lambda h: h.memset(bd_ones, 0.0), writes=[Bcst], reads=[Bcst])
        c.op("dve", lambda h: h.memset(bd_ones[0:64, 0:64], 1.0), writes=[Bcst], reads=[Bcst])
        c.op("dve", lambda h: h.memset(bd_ones[64:128, 64:128], 1.0), writes=[Bcst], reads=[Bcst])
        c.op("act", lambda h: h.activation(out=CST[:, 74:90], in_=sink_sb, func=AF.Exp), reads=[Bcst], writes=[Bcst])

        def rmsnorm(gi):
            if pending["gi"] == gi:
                pending["gi"] = None
                return
            for ti, (t0, n) in enumerate(TT):
                _norm_tile(gi, ti, t0, n)

        def _norm_tile(gi, ti, t0, n):
            for kc in range(KC):
                s_i = kc % 2
                c.op("act", lambda h, kc=kc, s_i=s_i: h.activation(out=sq[s_i][:, :n], in_=H[:, kc, t0:t0 + n], func=AF.Square),
                     reads=[BH[kc][ti]], writes=[Bsq[s_i]])
                c.op("pe", lambda h, kc=kc, s_i=s_i: h.matmul(ps[:, 7, :n], lhsT=ones_bf, rhs=sq[s_i][:, :n], start=(kc == 0), stop=(kc == KC - 1)),
                     reads=[Bsq[s_i], Bcst], writes=[BPS[7]])
            r_i = nxt("rs", 2)
            r = rs[r_i]
            c.op("act", lambda h: h.activation(out=r[:, :n], in_=ps[:, 7, :n], func=AF.Ln, scale=1.0 / D, bias=CST[:, 90:91]),
                 reads=[BPS[7], Bcst], writes=[Brs[r_i]])
            c.op("act", lambda h: h.activation(out=r[:, :n], in_=r[:, :n], func=AF.Exp, scale=-0.5), reads=[Brs[r_i]], writes=[Brs[r_i]])
            for kc in range(KC):
                c.op("dve", lambda h, kc=kc: h.scalar_tensor_tensor(out=XN[:, kc, t0:t0 + n], in0=H[:, kc, t0:t0 + n], scalar=gains[:, gi, kc:kc + 1],
                                                                in1=r[:, :n], op0=ALU.mult, op1=ALU.mult),
                     reads=[BH[kc][ti], Brs[r_i], Bcst], writes=[BXN[kc][ti]])

        def ffn(n_ffn, next_gi=None, tail=None):
            B = {}
            for s_ in range(NSLOT):
                B["slot%d" % s_] = c.rbuf("ffn.slot%d" % s_, s_ * 24576, 24576)
            for a_ in range(2):
                for j_ in range(4):
                    B["act%d_%d" % (a_, j_)] = c.rbuf("ffn.act%d_%d" % (a_, j_), 73728 + 4096 * a_ + 1024 * j_, 1024)
                B["sg%d" % a_] = c.rbuf("ffn.sg%d" % a_, 81920 + 2048 * a_, 2048)
            slot_g, slot_u, slot_d = [], [], []
            for s in range(NSLOT):
                base = OFF_A + s * 24576
                slot_g.append(carve(base, 8192, BF16).rearrange("p (j k c) -> p j k c", j=4, k=KC))
                slot_u.append(carve(base + 8192, 8192, BF16).rearrange("p (j k c) -> p j k c", j=4, k=KC))
                slot_d.append(carve(base + 16384, 8192, BF16).rearrange("p (j m) -> p j m", j=4))
            actb = [carve(OFF_A + 73728 + 4096 * a, 4096, BF16).rearrange("p (j n) -> p j n", j=4) for a in range(2)]
            sgb = [carve(OFF_A + 81920 + 2048 * a, 2048) for a in range(2)]

            def load_slab(si):
                j0, S = SLABS[si]
                s = si % NSLOT
                c.dma_group("pool", "ws%d" % s,
                            [(slot_g[s][:, 0:S], wg[n_ffn, :, j0:j0 + S]),
                             (slot_u[s][:, 0:S], wu[n_ffn, :, j0:j0 + S]),
                             (slot_d[s][:, 0:S], wd[n_ffn, :, j0:j0 + S])],
                            writes=[B["slot%d" % s]])

            for si in range(NSLOT):
                load_slab(si)
            rmsnorm(n_ffn)
            steps = [(si, ti) for si in range(len(SLABS)) for ti in range(5)]

            def gu(idx):
                si, ti = steps[idx]
                j0, S = SLABS[si]
                s = si % NSLOT
                t0, n = TT[ti]
                ab = idx % 2
                for j in range(S):
                    p = nxt("gu", 2)

                    def mm(h, j=j, p=p):
                        for kc in range(KC):
                            h.matmul(ps[:, 2 * p, :n], lhsT=slot_g[s][:, j, kc, :], rhs=XN[:, kc, t0:t0 + n], start=(kc == 0), stop=(kc == KC - 1))
                        for kc in range(KC):
                            ins = h.matmul(ps[:, 2 * p + 1, :n], lhsT=slot_u[s][:, j, kc, :], rhs=XN[:, kc, t0:t0 + n], start=(kc == 0), stop=(kc == KC - 1))
                        return ins
                    c.op("pe", mm, reads=[B["slot%d" % s]] + [BXN[kc][ti] for kc in range(KC)], writes=[BPS[2 * p], BPS[2 * p + 1]])
                    c.op("act", lambda h, p=p: h.activation(out=sgb[p][:, :n], in_=ps[:, 2 * p, :n], func=AF.Silu),
                         reads=[BPS[2 * p]], writes=[B["sg%d" % p]])
                    c.op("dve", lambda h, p=p, j=j: h.tensor_tensor(out=actb[ab][:, j, :n], in0=sgb[p][:, :n], in1=ps[:, 2 * p + 1, :n], op=ALU.mult),
                         reads=[B["sg%d" % p], BPS[2 * p + 1]], writes=[B["act%d_%d" % (ab, j)]])

            def down(idx):
                si, ti = steps[idx]
                j0, S = SLABS[si]
                s = si % NSLOT
                t0, n = TT[ti]
                ab = idx % 2
                for m in range(KC):
                    bk = 4 + nxt("dn", 3)

                    def mm(h, m=m, bk=bk):
                        for j in range(S):
                            ins = h.matmul(ps[:, bk, :n], lhsT=slot_d[s][:, j, m * 128:(m + 1) * 128], rhs=actb[ab][:, j, :n], start=(j == 0), stop=(j == S - 1))
                        return ins
                    c.op("pe", mm, reads=[B["slot%d" % s]] + [B["act%d_%d" % (ab, j)] for j in range(S)], writes=[BPS[bk]])
                    c.op("dve", lambda h, m=m, bk=bk: h.scalar_tensor_tensor(out=H[:, m, t0:t0 + n], in0=ps[:, bk, :n], scalar=0.5, in1=H[:, m, t0:t0 + n],
                                                                         op0=ALU.mult, op1=ALU.add),
                         reads=[BPS[bk], BH[m][ti]], writes=[BH[m][ti]])
                if ti == 4 and si + NSLOT < len(SLABS):
                    load_slab(si + NSLOT)
                if next_gi is not None and si == len(SLABS) - 1:
                    _norm_tile(next_gi, ti, t0, n)
                if tail is not None and si == len(SLABS) - 1:
                    tail(ti)

            for idx in range(len(steps)):
                gu(idx)
                if idx > 0:
                    down(idx - 1)
            down(len(steps) - 1)
            pending["gi"] = next_gi

        def attention(gi, next_gi=None, tail=None):
            A = OFF_A
            B = {}

            def rb(name, off, size, grp=None):
                B[name] = c.rbuf("att." + name, off, size, grp and "att." + grp)
            Vd = carve(A + 0, 17408, BF16).rearrange("p (b k d) -> p b k d", b=17, k=4)
            for i in range(17):
                rb("V%d" % i, 1024 * i, 1024)
            kTa = carve(A + 17408, 4128, BF16)
            rb("kTa", 17408, 4128)
            PT_off = [21536, 22560] + [45664 + 1024 * i for i in range(5)] + [100000]
            PT = [carve(A + o, 1024, BF16) for o in PT_off]
            for i, o in enumerate(PT_off):
                rb("PT%d" % i, o, 1024)
            qT = carve(A + 24576, 8256, BF16).rearrange("p (c t) -> p c t", c=2)
            rb("qT0", 24576, 4128)
            rb("qT1", 24576 + 4128, 4128)
            kTb = carve(A + 32832, 4128, BF16)
            rb("kTb", 32832, 4128)
            Bband = carve(A + 36960, 6144).rearrange("p (r x) -> p r x", r=3)
            Bmeta = carve(A + 43104, 2048)
            Bm0 = carve(A + 45152, 256)
            Bmm = carve(A + 45408, 256)
            rb("bias", 36960, 8704)
            tmp = [carve(A + 50784 + 2048 * i, 2048) for i in range(2)]
            rd = [carve(A + 54880 + 2048 * i, 2048) for i in range(2)] + [carve(A + 101024, 2048)]
            rb("rd2", 101024, 2048)
            wq_sb = [carve(A + 58976 + 4096 * i, 4096, BF16).rearrange("p (k c) -> p k c", k=KC) for i in range(2)]
            wk_sb = [carve(A + 67168 + 2048 * i, 2048, BF16).rearrange("p (k c) -> p k c", k=KC) for i in range(2)]
            for i in range(2):
                rb("tmp%d" % i, 50784 + 2048 * i, 2048)
                rb("rd%d" % i, 54880 + 2048 * i, 2048)
                rb("wq%d" % i, 58976 + 4096 * i, 4096)
                rb("wk%d" % i, 67168 + 2048 * i, 2048)
            oT = carve(A + 71264, 8256, BF16).rearrange("p (c t) -> p c t", c=2)
            for cc_ in range(2):
                for t_ in range(5):
                    rb("oT%d_%d" % (cc_, t_), 71264, 8256, grp="oT")
            wo_sb = carve(A + 79520, 16384, BF16).rearrange("p (k c) -> p k c", k=KC)
            rb("wo", 79520, 16384)
            wv_sb = carve(A + 95904, 4096, BF16).rearrange("p (k c) -> p k c", k=KC)
            rb("wv", 95904, 4096)
            NPT = len(PT)

            c.dma_group("pool", "mw", [(wv_sb, wv)], writes=[B["wv"]])
            c.op("dve", lambda h: h.memset(Vd[:, :, :, 64:128], 1.0), writes=[B["V%d" % i] for i in range(17)])
            c.op("dve", lambda h: h.memset(kTa[64:128, :], 0.0), writes=[B["kTa"]])
            c.op("dve", lambda h: h.memset(kTb[0:64, :], 0.0), writes=[B["kTb"]])

            def load_k(k):
                i = k % 2
                c.dma_group("pool", "mq%d" % i, [(wq_sb[i], wq[:, :, k * 256:(k + 1) * 256]), (wk_sb[i], wk2[:, :, k, :])],
                            writes=[B["wq%d" % i], B["wk%d" % i]])
            load_k(0)
            load_k(1)
            c.dma_group("pool", "mo", [(wo_sb, wo)], writes=[B["wo"]])
            rmsnorm(gi)
            def vproj(blk):
                ks, nk = (0, 16) if blk == 0 else (16 + 128 * (blk - 1), 128)
                ti = tile_of(ks)
                bk = nxt("r3", 3)

                def mm(h):
                    for kc in range(KC):
                        ins = h.matmul(ps[:nk, bk, 0:256], lhsT=XN[:, kc, ks:ks + nk], rhs=wv_sb[:, kc, :], start=(kc == 0), stop=(kc == KC - 1))
                    return ins
                c.op("pe", mm, reads=[B["wv"]] + [BXN[kc][ti] for kc in range(KC)], writes=[BPS[bk]])
                src = ps[:nk, bk, 0:256].rearrange("p (k d) -> p k d", k=4)
                c.op("act", lambda h: h.activation(out=Vd[:nk, blk, :, 0:64], in_=src, func=AF.Copy),
                     reads=[BPS[bk]], writes=[B["V%d" % blk]])
            for blk in range(17):
                vproj(blk)
            if ATT_SUB <= 1:
                return

            def proj(k, wi, ti, t0, n, which):
                bk = nxt("r3", 3)
                if which < 2:
                    def mm(h):
                        for kc in range(KC):
                            ins = h.matmul(ps[:, bk, :n], lhsT=wq_sb[wi][:, kc, which * 128:(which + 1) * 128], rhs=XN[:, kc, t0:t0 + n],
                                           start=(kc == 0), stop=(kc == KC - 1))
                        return ins
                    wb = B["wq%d" % wi]
                else:
                    def mm(h):
                        for kc in range(KC):
                            ins = h.matmul(ps[:, bk, :n], lhsT=wk_sb[wi][:, kc, :], rhs=XN[:, kc, t0:t0 + n], start=(kc == 0), stop=(kc == KC - 1))
                        return ins
                    wb = B["wk%d" % wi]
                c.op("pe", mm, reads=[wb] + [BXN[kc][ti] for kc in range(KC)], writes=[BPS[bk]])
                sqi = nxt("sqp", 2)
                sbk = (7, 3)[sqi]
                c.op("act", lambda h: h.activation(out=sq[sqi][:, :n], in_=ps[:, bk, :n], func=AF.Square), reads=[BPS[bk]], writes=[Bsq[sqi]])
                c.op("pe", lambda h: h.matmul(ps[:, sbk, :n], lhsT=bd_ones, rhs=sq[sqi][:, :n], start=True, stop=True), reads=[Bsq[sqi], Bcst], writes=[BPS[sbk]])
                r_i = nxt("rs", 2)
                r = rs[r_i]
                if which < 2:
                    c.op("act", lambda h: h.activation(out=r[:, :n], in_=ps[:, sbk, :n], func=AF.Ln, bias=CST[:, 91:92]),
                         reads=[BPS[sbk], Bcst], writes=[Brs[r_i]])
                else:
                    c.op("act", lambda h: h.activation(out=r[:, :n], in_=ps[:, sbk, :n], func=AF.Ln, scale=1.0 / 64.0, bias=CST[:, 90:91]),
                         reads=[BPS[sbk], Bcst], writes=[Brs[r_i]])
                c.op("act", lambda h: h.activation(out=r[:, :n], in_=r[:, :n], func=AF.Exp, scale=-0.5), reads=[Brs[r_i]], writes=[Brs[r_i]])
                if which < 2:
                    c.op("dve", lambda h: h.scalar_tensor_tensor(out=qT[:, which, t0:t0 + n], in0=ps[:, bk, :n], scalar=qkg[:, 0:1],
                                                             in1=r[:, :n], op0=ALU.mult, op1=ALU.mult),
                         reads=[BPS[bk], Brs[r_i], Bcst], writes=[B["qT%d" % which]])
                else:
                    c.op("dve", lambda h: h.scalar_tensor_tensor(out=kTa[0:64, t0:t0 + n], in0=ps[0:64, bk, :n], scalar=qkg[0:64, 1:2],
                                                             in1=r[0:64, :n], op0=ALU.mult, op1=ALU.mult),
                         reads=[BPS[bk], Brs[r_i], Bcst], writes=[B["kTa"]])
                    c.op("dve", lambda h: h.scalar_tensor_tensor(out=kTb[64:128, t0:t0 + n], in0=ps[64:128, bk, :n], scalar=qkg[64:128, 1:2],
                                                             in1=r[64:128, :n], op0=ALU.mult, op1=ALU.mult),
                         reads=[BPS[bk], Brs[r_i], Bcst], writes=[B["kTb"]])

            def kg_scores(k, qs, nq, W, blk, ks, nk, bias_ap, cq):
                bk = nxt("r4", 4)

                def mm(h):
                    for hh in range(2):
                        ins = h.matmul(ps[:nk, bk, hh * 2 * nq:(hh + 1) * 2 * nq].rearrange("p (c n) -> p c n", c=2),
                                       lhsT=(kTa if hh == 0 else kTb)[:, ks:ks + nk],
                                       rhs=qT[:, :, qs:qs + nq], start=True, stop=True)
                    return ins
                c.op("pe", mm, reads=[B["kTa"], B["kTb"], B["qT0"], B["qT1"]], writes=[BPS[bk]])
                tb = nxt("tmp", 2)
                if cq is None:
                    c.op("dve", lambda h: h.tensor_tensor(out=tmp[tb][:nk, :W], in0=ps[:nk, bk, :W], in1=bias_ap, op=ALU.add),
                         reads=[BPS[bk], B["bias"]], writes=[B["tmp%d" % tb]])
                else:
                    c.op("dve", lambda h: h.tensor_tensor(out=tmp[tb][:nk, :W].rearrange("p (g n) -> p g n", g=4),
                                                          in0=ps[:nk, bk, :W].rearrange("p (g n) -> p g n", g=4),
                                                          in1=cbias[:nk, 4 * k:4 * k + 4, cq:cq + 1].to_broadcast([nk, 4, nq]), op=ALU.add),
                         reads=[BPS[bk], Bcst], writes=[B["tmp%d" % tb]])
                    c.op("dve", lambda h: h.tensor_tensor(out=tmp[tb][:nk, :W], in0=tmp[tb][:nk, :W], in1=bias_ap, op=ALU.add),
                         reads=[B["tmp%d" % tb], B["bias"]], writes=[B["tmp%d" % tb]])
                pt = nxt("pt", NPT)
                c.op("act", lambda h: h.activation(out=PT[pt][:nk, :W], in_=tmp[tb][:nk, :W], func=AF.Exp),
                     reads=[B["tmp%d" % tb]], writes=[B["PT%d" % pt]])
                return (blk, nk, pt)

            def qgroup_scores(k, qb, qs, nq):
                W = 4 * nq
                if qb < 0:
                    kgs = [(0, 0, 16, Bmm[0:16, 0:W], None), (1, 16, 128, Bm0[:, 0:W], None)]
                else:
                    kgs = [(0, 0, 16, Bmeta[0:16, :], qb)]
                    for r_ in (-1, 0, 1):
                        if 0 <= qb + r_ <= 15:
                            kgs.append((qb + r_ + 1, 16 + 128 * (qb + r_), 128, Bband[:, r_ + 1, :], None))
                parts = [kg_scores(k, qs, nq, W, blk, ks, nk, bias_ap, cq) for (blk, ks, nk, bias_ap, cq) in kgs]
                return (k, qb, qs, nq, parts)

            def qgroup_pv(state):
                k, qb, qs, nq, parts = state
                W = 4 * nq
                bA = 4 + nxt("pv", 3)
                oti = tile_of(qs)

                def mm2(h):
                    for i, (blk, nk, pt) in enumerate(parts):
                        ins = h.matmul(ps[:, bA, :W], lhsT=Vd[:nk, blk, k, :], rhs=PT[pt][:nk, :W], start=(i == 0), stop=(i == len(parts) - 1))
                    return ins
                c.op("pe", mm2, reads=[B["V%d" % blk] for (blk, nk, pt) in parts] + [B["PT%d" % pt] for (blk, nk, pt) in parts],
                     writes=[BPS[bA]])
                ri = nxt("rd", 3)
                c.op("dve", lambda h: h.tensor_tensor(out=rd[ri][64:128, :W].rearrange("p (g n) -> p g n", g=4),
                                                      in0=ps[64:128, bA, :W].rearrange("p (g n) -> p g n", g=4),
                                                      in1=es3[64:128, 4 * k:4 * k + 4, :].to_broadcast([64, 4, nq]), op=ALU.add),
                     reads=[BPS[bA], Bcst], writes=[B["rd%d" % ri]])
                c.op("act", lambda h: h.activation(out=rd[ri][64:128, :W], in_=rd[ri][64:128, :W], func=AF.Ln), reads=[B["rd%d" % ri]], writes=[B["rd%d" % ri]])
                c.op("act", lambda h: h.activation(out=rd[ri][64:128, :W], in_=rd[ri][64:128, :W], func=AF.Exp, scale=-1.0), reads=[B["rd%d" % ri]], writes=[B["rd%d" % ri]])
                return (qs, nq, W, bA, ri, oti)

            def qgroup_fin(fin):
                qs, nq, W, bA, ri, oti = fin
                for hh in range(2):
                    lo = 64 * hh
                    c.op("dve", lambda h, hh=hh, lo=lo: h.tensor_tensor(
                        out=oT[lo:lo + 64, :, qs:qs + nq],
                        in0=ps[0:64, bA, :W].rearrange("p (h c n) -> p h c n", c=2, h=2)[:, hh, :, :],
                        in1=rd[ri][64:128, :W].rearrange("p (h c n) -> p h c n", c=2, h=2)[:, hh, :, :], op=ALU.mult),
                        reads=[BPS[bA], B["rd%d" % ri]], writes=[B["oT0_%d" % oti], B["oT1_%d" % oti]])

            def oproj(k, ti, t0, n):
                for m in range(KC):
                    bk = nxt("r3", 3)

                    def mm(h, m=m, bk=bk):
                        for cc in range(2):
                            ins = h.matmul(ps[:, bk, :n], lhsT=wo_sb[:, 2 * k + cc, m * 128:(m + 1) * 128], rhs=oT[:, cc, t0:t0 + n], start=(cc == 0), stop=(cc == 1))
                        return ins
                    c.op("pe", mm, reads=[B["wo"], B["oT0_%d" % ti], B["oT1_%d" % ti]], writes=[BPS[bk]])
                    c.op("dve", lambda h, m=m, bk=bk: h.tensor_tensor(out=H[:, m, t0:t0 + n], in0=ps[:, bk, :n], in1=H[:, m, t0:t0 + n], op=ALU.add),
                         reads=[BPS[bk], BH[m][ti]], writes=[BH[m][ti]])
                if k == 3 and next_gi is not None:
                    _norm_tile(next_gi, ti, t0, n)
                if k == 3 and tail is not None:
                    tail(ti)

            def att_k(k):
                wi = k % 2
                c.dma_group("sp", "tb", [(Bband.rearrange("p r x -> p (r x)"), bband[k].rearrange("p r g a -> p (r g a)")),
                                         (Bmeta[0:16, :], bmeta[k].rearrange("p g a -> p (g a)")),
                                         (Bm0, bm0[k].rearrange("p g a -> p (g a)")),
                                         (Bmm[0:16, :], bmm[k].rearrange("p g a -> p (g a)"))], writes=[B["bias"]])
                for ti, (t0, n) in enumerate(TT):
                    for which in range(3):
                        proj(k, wi, ti, t0, n, which)
                if k + 2 < 4:
                    load_k(k + 2)
                if ATT_SUB <= 2:
                    return
                prev = None
                fin = None
                for (qb, qs, nq) in [(-1, 0, 16)] + [(qb, 16 + 128 * qb, 128) for qb in range(16)]:
                    stt = qgroup_scores(k, qb, qs, nq)
                    nfin = qgroup_pv(prev) if prev is not None else None
                    if fin is not None:
                        qgroup_fin(fin)
                    fin = nfin
                    prev = stt
                nfin = qgroup_pv(prev)
                if fin is not None:
                    qgroup_fin(fin)
                qgroup_fin(nfin)
                if ATT_SUB <= 4:
                    return
                for ti, (t0, n) in enumerate(TT):
                    oproj(k, ti, t0, n)

            for k in range(4):
                att_k(k)
            if ATT_SUB >= 9:
                pending["gi"] = next_gi

        def poolmix(gi, next_gi=None, tail=None):
            A = OFF_A
            B = {}
            wpi_sb = carve(A, 16384, BF16).rearrange("p (k c) -> p k c", k=KC)
            B["wpi"] = c.rbuf("pool.wpi", 0, 16384)
            Us = [carve(A + 16384, 8320), carve(A + 94848, 8320)]
            B["U0"] = c.rbuf("pool.U0", 16384, 8320)
            B["U1"] = c.rbuf("pool.U1", 94848, 8320)
            T = [carve(A + 86528, 8320), carve(A + 24704, 8320)]
            B["T0"] = c.rbuf("pool.T0", 86528, 8320)
            B["T1"] = c.rbuf("pool.T1", 24704, 8320)
            wpg_sb = carve(A + 33024, 4096, BF16).rearrange("p (g k c) -> p g k c", g=4, k=2)
            B["wpg"] = c.rbuf("pool.wpg", 33024, 4096)
            wpo_sb = carve(A + 37120, 16384, BF16).rearrange("p (k c) -> p k c", k=KC)
            B["wpo"] = c.rbuf("pool.wpo", 37120, 16384)
            pooled = carve(A + 53504, 33024, BF16).rearrange("p (k t) -> p k t", k=KC)
            for i in range(KC):
                B["pl%d" % i] = c.rbuf("pool.pl%d" % i, 53504 + 4128 * i, 4128)
            mixed = carve(A + 86528, 8192, BF16).rearrange("p (k n) -> p k n", k=KC)
            BT = [B["T0"], B["T1"]]
            c.dma_group("pool", "mw", [(wpi_sb, wpi)], writes=[B["wpi"]])
            c.dma_group("pool", "mo", [(wpg_sb, wpg), (wpo_sb, wpo)], writes=[B["wpg"], B["wpo"]])
            rmsnorm(gi)
            for ui in range(2):
                c.op("dve", lambda h, ui=ui: h.memset(Us[ui][:, 0:8], 0.0), writes=[B["U%d" % ui]])
                c.op("dve", lambda h, ui=ui: h.memset(Us[ui][:, 2072:2080], 0.0), writes=[B["U%d" % ui]])
            def chunk(ch):
                g = ch // 2
                hw = 1 << g
                U = Us[ch % 2]
                BU = B["U%d" % (ch % 2)]
                for ti, (t0, n) in enumerate(TT):
                    bk = nxt("r3", 3)

                    def mm(h, bk=bk, t0=t0, n=n):
                        for kc in range(KC):
                            ins = h.matmul(ps[:, bk, :n], lhsT=wpi_sb[:, kc, ch * 128:(ch + 1) * 128], rhs=XN[:, kc, t0:t0 + n], start=(kc == 0), stop=(kc == KC - 1))
                        return ins
                    c.op("pe", mm, reads=[B["wpi"]] + [BXN[kc][ti] for kc in range(KC)], writes=[BPS[bk]])
                    c.op("act", lambda h, bk=bk, t0=t0, n=n: h.activation(out=U[:, 8 + t0:8 + t0 + n], in_=ps[:, bk, :n], func=AF.Copy),
                         reads=[BPS[bk]], writes=[BU])
                src, bsrc = U, BU
                s_ = 1
                lvl = 0
                while s_ <= hw:
                    ln = 2080 - 2 * s_ + 1
                    dst, bdst = T[lvl % 2], BT[lvl % 2]
                    c.op("dve", lambda h, src=src, dst=dst, s_=s_, ln=ln: h.tensor_tensor(out=dst[:, 0:ln], in0=src[:, 0:ln], in1=src[:, s_:s_ + ln], op=ALU.add),
                         reads=[bsrc], writes=[bdst])
                    src, bsrc = dst, bdst
                    s_ *= 2
                    lvl += 1
                w_ = 2 * hw
                c.op("dve", lambda h: h.scalar_tensor_tensor(out=pooled[:, ch, :], in0=src[:, 8 - hw:8 - hw + L], scalar=1.0 / w_, in1=U[:, 8:8 + L],
                                                         op0=ALU.mult, op1=ALU.subtract),
                     reads=[bsrc, BU], writes=[B["pl%d" % ch]])
                c.op("dve", lambda h: h.tensor_tensor(out=sm[:, 0:hw], in0=src[:, 8 - hw:8], in1=invc[:, g, 0:hw], op=ALU.mult),
                     reads=[bsrc, Bcst], writes=[Bsm])
                c.op("dve", lambda h: h.tensor_tensor(out=pooled[:, ch, 0:hw], in0=sm[:, 0:hw], in1=U[:, 8:8 + hw], op=ALU.subtract),
                     reads=[Bsm, BU], writes=[B["pl%d" % ch]])
                if hw > 1:
                    nr = hw - 1
                    tb_ = L - hw + 1
                    c.op("dve", lambda h: h.tensor_tensor(out=sm[:, 16:16 + nr], in0=src[:, 8 + tb_ - hw:8 + tb_ - hw + nr],
                                                          in1=invc[:, g, 8:8 + nr], op=ALU.mult),
                         reads=[bsrc, Bcst], writes=[Bsm])
                    c.op("dve", lambda h: h.tensor_tensor(out=pooled[:, ch, tb_:tb_ + nr], in0=sm[:, 16:16 + nr], in1=U[:, 8 + tb_:8 + tb_ + nr], op=ALU.subtract),
                         reads=[Bsm, BU], writes=[B["pl%d" % ch]])

            def mixtile(ti, t0, n):
                for g in range(4):
                    for mm_ in range(2):
                        bk = nxt("r3", 3)
                        oc = 2 * g + mm_

                        def mm(h, g=g, mm_=mm_, bk=bk):
                            for kk in range(2):
                                ins = h.matmul(ps[:, bk, :n], lhsT=wpg_sb[:, g, kk, mm_ * 128:(mm_ + 1) * 128], rhs=pooled[:, 2 * g + kk, t0:t0 + n], start=(kk == 0), stop=(kk == 1))
                            return ins
                        c.op("pe", mm, reads=[B["wpg"], B["pl%d" % (2 * g)], B["pl%d" % (2 * g + 1)]], writes=[BPS[bk]])
                        c.op("dve", lambda h, oc=oc, bk=bk: h.tensor_scalar(out=mixed[:, oc, :n], in0=ps[:, bk, :n], scalar1=gains[:, 6, oc:oc + 1], scalar2=None, op0=ALU.mult),
                             reads=[BPS[bk], Bcst], writes=[B["T0"]])
                for m in range(KC):
                    bk = nxt("r3", 3)

                    def mm(h, m=m, bk=bk):
                        for cc in range(KC):
                            ins = h.matmul(ps[:, bk, :n], lhsT=wpo_sb[:, cc, m * 128:(m + 1) * 128], rhs=mixed[:, cc, :n], start=(cc == 0), stop=(cc == KC - 1))
                        return ins
                    c.op("pe", mm, reads=[B["wpo"], B["T0"]], writes=[BPS[bk]])
                    c.op("dve", lambda h, m=m, bk=bk: h.tensor_tensor(out=H[:, m, t0:t0 + n], in0=ps[:, bk, :n], in1=H[:, m, t0:t0 + n], op=ALU.add),
                         reads=[BPS[bk], BH[m][ti]], writes=[BH[m][ti]])
                if next_gi is not None:
                    _norm_tile(next_gi, ti, t0, n)
                if tail is not None:
                    tail(ti)

            for ch in range(KC):
                chunk(ch)
            for ti, (t0, n) in enumerate(TT):
                mixtile(ti, t0, n)
            pending["gi"] = next_gi

        phases_sel = phases
        plist = [("ffn", 0, 0), ("att", 4, 4), ("ffn", 1, 1), ("ffn", 2, 2), ("pool", 5, 5), ("ffn", 3, 3)]
        sel = list(phases_sel if phases_sel is not None else range(min(stage, 6)))

        def load_tile(seq, ti, queue="sp"):
            t0, n = TT[ti]
            items = []
            if t0 < NM:
                items.append((H[:, :, 0:NM], metaT[seq]))
            r0, r1 = max(t0, NM), t0 + n
            items.append((H[:, :, r0:r1], xT[seq, :, KC * (r0 - NM):KC * (r1 - NM)].rearrange("p (k n) -> p k n", k=KC)))
            c.dma_group(queue, "xin%d" % ti, items, writes=[BH[kc][ti] for kc in range(KC)])

        def store_tile(seq, ti):
            t0, n = TT[ti]
            r0, r1 = max(t0, NM), t0 + n
            c.dma_group("sp", "out%d" % ti, [(outT[seq, :, KC * (r0 - NM):KC * (r1 - NM)].rearrange("p (k n) -> p k n", k=KC), H[:, :, r0:r1])],
                        reads=[BH[kc][ti] for kc in range(KC)])

        for ti in range(5):
            load_tile(0, ti)
        for seq in range(2):
            if seq == 0:
                pending["gi"] = None

            def tail(ti, seq=seq):
                store_tile(seq, ti)
                if seq == 0:
                    load_tile(1, ti, "pool")
                    if len(sel) > 0:
                        _norm_tile(plist[sel[0]][2], ti, TT[ti][0], TT[ti][1])
            for ii, pi in enumerate(sel):
                kind, arg, g_i = plist[pi]
                last = (ii + 1 == len(sel))
                nx = plist[sel[ii + 1]][2] if not last else None
                tl = tail if last else None
                if kind == "ffn":
                    ffn(arg, nx, tl)
                elif kind == "att":
                    attention(arg, nx, tl)
                else:
                    poolmix(arg, nx, tl)
            if seq == 0 and len(sel) > 0:
                pending["gi"] = plist[sel[0]][2]
        c.wait_all("sp", allH)
        c.replay(block)
    return nc


def _const_tables():
    slopes = np.array([2.0 ** (-8.0 * (h + 1) / 16.0) for h in range(16)], dtype=np.float64).reshape(4, 4)
    a = np.arange(128)
    bband = np.zeros((4, 128, 3, 4, 128), np.float32)
    for r in range(3):
        rel = (r - 1) * 128 + a[:, None] - a[None, :]
        valid = np.abs(rel) <= 128
        for k in range(4):
            for g in range(4):
                bband[k, :, r, g, :] = np.where(valid, -slopes[k, g] * np.abs(rel), NEG)
    m = np.arange(16)
    bmeta = np.zeros((4, 16, 4, 128), np.float32)
    bmm = np.zeros((4, 16, 4, 16), np.float32)
    bm0 = np.zeros((4, 128, 4, 16), np.float32)
    for k in range(4):
        for g in range(4):
            bmeta[k, :, g, :] = -slopes[k, g] * (16 + a[None, :] - m[:, None])
            bmm[k, :, g, :] = -slopes[k, g] * np.abs(m[None, :] - m[:, None])
            dist = 16 + a[:, None] - m[None, :]
            bm0[k, :, g, :] = np.where(dist <= 128, -slopes[k, g] * dist, NEG)
    invc = np.ones((4, 16), np.float32)
    for g in range(4):
        hw = 1 << g
        for t in range(hw):
            invc[g, t] = 1.0 / (t + hw)
        for i in range(hw - 1):
            invc[g, 8 + i] = 1.0 / (2 * hw - 1 - i)
    perm = [0, 2, 1, 3]
    bband = np.ascontiguousarray(bband[:, :, :, perm, :])
    bmeta = np.ascontiguousarray(bmeta[:, :, perm, :])
    bmm = np.ascontiguousarray(bmm[:, :, perm, :])
    bm0 = np.ascontiguousarray(bm0[:, :, perm, :])
    sl_p = slopes[:, perm].reshape(16, 1)
    cb = np.zeros((128, 16, 16), np.float32)
    cb[:] = (-sl_p * 128.0 * np.arange(16)[None, :])[None]
    return bband, bmeta, bm0, bmm, invc, cb.reshape(128, 256)


def _fm(v):
    return np.ascontiguousarray(v.reshape(KC, 128).T)


def _wfm(w):
    return np.ascontiguousarray(w.reshape(KC, 128, -1).transpose(1, 0, 2))


_CACHE = {}


def _real_ranges():
    return [(max(t0, NM) - NM, t0 + n - NM) for (t0, n) in TT]


def _pack_x(xs):
    out = np.empty((2, 128, KC * SEQ), np.float32)
    for (r0, r1) in _real_ranges():
        ln = r1 - r0
        blk = xs[:, r0:r1, :].reshape(2, ln, KC, 128).transpose(0, 3, 2, 1)
        out[:, :, KC * r0:KC * r1] = blk.reshape(2, 128, KC * ln)
    return out


def _unpack_out(o):
    out = np.empty((2, SEQ, D), np.float32)
    for (r0, r1) in _real_ranges():
        ln = r1 - r0
        blk = o[:, :, KC * r0:KC * r1].reshape(2, 128, KC, ln)
        out[:, r0:r1, :] = blk.transpose(0, 3, 2, 1).reshape(2, ln, D)
    return out


def kernel(x, meta_tokens, ffn_norm, w_gate_up, w_down, mixer_norm, w_qkv, q_norm, k_norm,
           sink_logit, w_o, w_pool_in, w_pool_group, pool_scale, w_pool_out):
    f = np.float32
    x = np.asarray(x, f)
    bband, bmeta, bm0, bmm, invc, cbias = _const_tables()
    wg = np.empty((4, 128, NJ, KC, 128), f)
    wu = np.empty((4, 128, NJ, KC, 128), f)
    wd = np.empty((4, 128, NJ, D), f)
    for i in range(2):
        for ff in range(2):
            n = 2 * i + ff
            wgu = np.asarray(w_gate_up[i, ff], f)
            wg[n] = wgu[:, :2816].reshape(KC, 128, NJ, 128).transpose(1, 2, 0, 3)
            wu[n] = wgu[:, 2816:].reshape(KC, 128, NJ, 128).transpose(1, 2, 0, 3)
            wd[n] = np.asarray(w_down[i, ff], f).reshape(NJ, 128, D).transpose(1, 0, 2)
    cst = np.zeros((128, 288), f)
    gl = [ffn_norm[0, 0], ffn_norm[0, 1], ffn_norm[1, 0], ffn_norm[1, 1], mixer_norm[0], mixer_norm[1], pool_scale[0]]
    for gi, v in enumerate(gl):
        cst[:, gi * 8:(gi + 1) * 8] = _fm(np.asarray(v, f))
    cst[:, 56] = np.tile(np.asarray(q_norm[0], f), 2)
    cst[:, 57] = np.tile(np.asarray(k_norm[0], f), 2)
    cst[:, 58:74] = np.asarray(sink_logit[0], f).reshape(4, 4)[:, [0, 2, 1, 3]].reshape(1, 16)
    cst[:, 90] = EPS
    cst[:, 91] = 64.0 * EPS
    cst[:, 224:288] = invc.reshape(1, 64)
    wqkv = np.asarray(w_qkv[0], f)
    wq = _wfm(wqkv[:, :1024])
    wk = wqkv[:, 1024:1280].reshape(KC, 128, 4, 64).transpose(1, 0, 2, 3)
    wk2 = np.ascontiguousarray(np.concatenate([wk, wk], axis=3))
    wv = _wfm(wqkv[:, 1280:1536])
    wo = _wfm(np.asarray(w_o[0], f))
    wpi = _wfm(np.asarray(w_pool_in[0], f))
    wpg = np.ascontiguousarray(np.asarray(w_pool_group[0], f).reshape(4, 2, 128, 256).transpose(2, 0, 1, 3))
    wpo = _wfm(np.asarray(w_pool_out[0], f))
    metaT1 = np.ascontiguousarray(np.asarray(meta_tokens, f).T.reshape(KC, 128, NM).transpose(1, 0, 2))
    metaT = np.ascontiguousarray(np.stack([metaT1, metaT1]))
    shared = dict(metaT=metaT, wg=wg, wu=wu, wd=wd, cst=cst, wq=wq, wk2=wk2, wv=wv, wo=wo, wpi=wpi, wpg=wpg, wpo=wpo,
                  bband=bband, bmeta=bmeta, bm0=bm0, bmm=bmm, cbias=cbias)
    in_maps = []
    for core in range(8):
        xs = x[2 * core:2 * core + 2]
        d = dict(shared)
        d["xT"] = _pack_x(xs)
        in_maps.append(d)
    if STAGE not in _CACHE:
        _CACHE[STAGE] = build_program(STAGE)
    nc = _CACHE[STAGE]
    res = run_bass_kernel_spmd(nc, in_maps, core_ids=list(range(8)))
    out = np.empty((16, SEQ, D), f)
    for core in range(8):
        out[2 * core:2 * core + 2] = _unpack_out(np.asarray(res.results[core]["outT"]))
    return out
```

```python
import numpy as np
from contextlib import ExitStack
import concourse.bass as bass
import concourse.mybir as mybir
from concourse.bass_utils import run_bass_kernel_spmd

F32 = mybir.dt.float32
BF16 = mybir.dt.bfloat16
AF = mybir.ActivationFunctionType
ALU = mybir.AluOpType

D = 1024
SEQ = 2048
NM = 16
L = SEQ + NM
KC = 8
NJ = 22
EPS = 1e-6
NEG = -30000.0
TT = [(0, 400), (400, 384), (784, 384), (1168, 384), (1552, 512)]


def tile_of(tok):
    for i, (t0, n) in enumerate(TT):
        if t0 <= tok < t0 + n:
            return i
    raise ValueError(tok)
SLABS = [(0, 4), (4, 4), (8, 4), (12, 4), (16, 4), (20, 2)]
NSLOT = 3
ATT_SUB = 9
STAGE = 6


class Buf:
    __slots__ = ("name", "w", "r", "lo", "hi", "ov", "grp")

    def __init__(self, name, lo=None, hi=None, grp=None):
        self.name = name
        self.w = None
        self.r = {}
        self.lo = lo
        self.hi = hi
        self.ov = []
        self.grp = grp or name


class Eng:
    def __init__(self, name, sem):
        self.name = name
        self.sem = sem
        self.count = 0
        self.known = {}
        self.prog = []


class Ctx:
    def __init__(self, nc, sems):
        self.nc = nc
        self.E = {k: Eng(k, v) for k, v in sems.items()}
        self.semobj = dict(sems)
        self.dma_cnt = {}
        self.ninst = 0
        self.nwaits = 0
        self.ranged = {}

    def rbuf(self, name, lo, size, grp=None):
        key = (name, lo, size)
        if key in self.ranged:
            return self.ranged[key]
        b = Buf(name, lo, lo + size, grp)
        for o in self.ranged.values():
            if o.lo < b.hi and b.lo < o.hi and o.grp != b.grp:
                o.ov.append(b)
                b.ov.append(o)
        self.ranged[key] = b
        return b

    def add_sem(self, key, handle):
        self.semobj[key] = handle
        self.dma_cnt[key] = 0

    def _waits(self, e, reads, writes):
        need = {}
        for b in reads:
            if b.w is not None and need.get(b.w[0], 0) < b.w[1]:
                need[b.w[0]] = b.w[1]
        for b in writes:
            if b.w is not None and need.get(b.w[0], 0) < b.w[1]:
                need[b.w[0]] = b.w[1]
            for s, v in b.r.items():
                if need.get(s, 0) < v:
                    need[s] = v
            for o in b.ov:
                if o.w is not None and need.get(o.w[0], 0) < o.w[1]:
                    need[o.w[0]] = o.w[1]
                for s, v in o.r.items():
                    if need.get(s, 0) < v:
                        need[s] = v
        out = []
        for s, v in need.items():
            if e.name == "pe" and s == "pe":
                continue
            if e.known.get(s, 0) < v:
                e.known[s] = v
                out.append((s, v))
        return out

    def _mark(self, tok, reads, writes):
        for b in reads:
            if b.r.get(tok[0], 0) < tok[1]:
                b.r[tok[0]] = tok[1]
        for b in writes:
            b.w = tok
            b.r = {}

    def op(self, eng, fn, reads=(), writes=()):
        e = self.E[eng]
        waits = self._waits(e, reads, writes)
        e.count += 1
        tok = (eng, e.count)
        semobj = self.semobj
        mysem = e.sem

        def run(h):
            for s, v in waits:
                h.wait_ge(semobj[s], v)
            fn(h).then_inc(mysem, 1)

        e.prog.append(run)
        self.ninst += 1
        self.nwaits += len(waits)
        self._mark(tok, reads, writes)

    def dma_group(self, queue, semkey, items, reads=(), writes=()):
        e = self.E[queue]
        waits = self._waits(e, reads, writes)
        self.dma_cnt[semkey] += 16 * len(items)
        tok = (semkey, self.dma_cnt[semkey])
        semobj = self.semobj

        def run(h):
            for s, v in waits:
                h.wait_ge(semobj[s], v)
            for o, i in items:
                h.dma_start(out=o, in_=i).then_inc(semobj[semkey], 16)

        e.prog.append(run)
        self.ninst += len(items)
        self._mark(tok, reads, writes)

    def wait_all(self, eng, bufs):
        e = self.E[eng]
        waits = self._waits(e, bufs, bufs)
        semobj = self.semobj

        def run(h):
            for s, v in waits:
                h.wait_ge(semobj[s], v)

        e.prog.append(run)

    def replay(self, block):
        E = self.E

        @block.sync
        def _(h):
            for f in E["sp"].prog:
                f(h)

        @block.tensor
        def _(h):
            for f in E["pe"].prog:
                f(h)

        @block.scalar
        def _(h):
            for f in E["act"].prog:
                f(h)

        @block.vector
        def _(h):
            for f in E["dve"].prog:
                f(h)

        @block.gpsimd
        def _(h):
            for f in E["pool"].prog:
                f(h)


def build_program(stage=6, phases=None, nffn=4, nj=NJ):
    nc = bass.Bass("TRN2", target_bir_lowering=False)

    def din(name, shape):
        return nc.dram_tensor(name, list(shape), F32, kind="ExternalInput").ap()

    xT = din("xT", [2, 128, KC * SEQ])
    metaT = din("metaT", [2, 128, KC, NM])
    wg = din("wg", [nffn, 128, nj, KC, 128])
    wu = din("wu", [nffn, 128, nj, KC, 128])
    wd = din("wd", [nffn, 128, nj, D])
    cst = din("cst", [128, 288])
    cbias_d = din("cbias", [128, 256])
    wq = din("wq", [128, KC, D])
    wk2 = din("wk2", [128, KC, 4, 128])
    wv = din("wv", [128, KC, 256])
    wo = din("wo", [128, KC, D])
    wpi = din("wpi", [128, KC, D])
    wpg = din("wpg", [128, 4, 2, 256])
    wpo = din("wpo", [128, KC, D])
    bband = din("bband", [4, 128, 3, 4, 128])
    bmeta = din("bmeta", [4, 16, 4, 128])
    bm0 = din("bm0", [4, 128, 4, 16])
    bmm = din("bmm", [4, 16, 4, 16])
    outT = nc.dram_tensor("outT", [2, 128, KC * SEQ], F32, kind="ExternalOutput").ap()

    slopes = [2.0 ** (-8.0 * (h + 1) / 16.0) for h in range(16)]

    with ExitStack() as st:
        ent = st.enter_context
        TOTAL_F32 = 212480 // 4
        sb = ent(nc.sbuf_tensor("sb", [128, TOTAL_F32], F32))
        ps = ent(nc.psum_tensor("ps", [128, 8, 512], F32))
        sems = {k: ent(nc.semaphore("s_" + k)) for k in ["pe", "act", "dve", "pool", "sp"]}
        c = Ctx(nc, sems)
        for k in ["ws0", "ws1", "ws2", "mw", "mo", "mq0", "mq1", "tb", "cst"] + ["xin%d" % i for i in range(5)] + ["xinp%d" % i for i in range(5)] + ["out%d" % i for i in range(5)]:
            c.add_sem(k, ent(nc.semaphore("d_" + k)))
        block = ent(nc.Block())

        def carve(off, nbytes, dt=F32):
            a = sb[:, off // 4:(off + nbytes) // 4]
            if dt == BF16:
                a = a.bitcast(BF16)
            return a

        OFF_H, OFF_XN, OFF_C, OFF_NS, OFF_A = 0, 66048, 99072, 101120, 109312
        H = carve(OFF_H, 66048).rearrange("p (k t) -> p k t", k=KC)
        XN = carve(OFF_XN, 33024, BF16).rearrange("p (k t) -> p k t", k=KC)
        CST = carve(OFF_C, 288 * 4)
        gains = CST[:, 0:56].rearrange("p (g k) -> p g k", k=KC)
        qkg = CST[:, 56:58]
        sink_sb = CST[:, 58:74]
        es3 = CST[:, 74:90].rearrange("p (h o) -> p h o", o=1)
        invc = CST[:, 224:288].rearrange("p (g e) -> p g e", e=16)
        ones_bf = carve(OFF_C + 1280, 256, BF16)
        bd_ones = carve(OFF_C + 1536, 256, BF16)
        sq = [carve(OFF_NS + 1024 * i, 1024, BF16) for i in range(2)]
        rs = [carve(OFF_NS + 2048 + 2048 * i, 2048) for i in range(2)]
        sm = carve(OFF_NS + 6144, 2048)
        cbias = sm[:, 256:512].rearrange("p (h q) -> p h q", q=16)

        BH = [[Buf("H%d_%d" % (k, t)) for t in range(5)] for k in range(KC)]
        BXN = [[Buf("XN%d_%d" % (k, t)) for t in range(5)] for k in range(KC)]
        Bsq = [Buf("sq0"), Buf("sq1")]
        Brs = [Buf("rs0"), Buf("rs1")]
        Bsm = Buf("sm")
        BPS = [Buf("ps%d" % i) for i in range(8)]
        Bcst = Buf("cst")
        allH = [b for row in BH for b in row]
        allXN = [b for row in BXN for b in row]

        pending = {"gi": None}

        rot = {"sqp": 0, "r4": 0, "r3": 0, "gu": 0, "dn": 0, "pv": 0, "pt": 0, "tmp": 0, "rd": 0, "rs": 0}

        def nxt(key, n):
            v = rot[key]
            rot[key] = (v + 1) % n
            return v

        c.dma_group("sp", "cst", [(CST, cst), (sm[:, 256:512], cbias_d)], writes=[Bcst])
        c.op("dve", lambda h: h.memset(ones_bf, 1.0), writes=[Bcst], reads=[Bcst])
        c.op("dve", lambda h: h.memset(bd_ones, 0.0), writes=[Bcst], reads=[Bcst])
        c.op("dve", lambda h: h.memset(bd_ones[0:64, 0:64], 1.0), writes=[Bcst], reads=[Bcst])
        c.op("dve", lambda h: h.memset(bd_ones[64:128, 64:128], 1.0), writes=[Bcst], reads=[Bcst])
        c.op("act", lambda h: h.activation(out=CST[:, 74:90], in_=sink_sb, func=AF.Exp), reads=[Bcst], writes=[Bcst])

        def rmsnorm(gi):
            if pending["gi"] == gi:
                pending["gi"] = None
                return
            for ti, (t0, n) in enumerate(TT):
                _norm_tile(gi, ti, t0, n)

        def _norm_tile(gi, ti, t0, n):
            for kc in range(KC):
                s_i = kc % 2
                c.op("act", lambda h, kc=kc, s_i=s_i: h.activation(out=sq[s_i][:, :n], in_=H[:, kc, t0:t0 + n], func=AF.Square),
                     reads=[BH[kc][ti]], writes=[Bsq[s_i]])
                c.op("pe", lambda h, kc=kc, s_i=s_i: h.matmul(ps[:, 7, :n], lhsT=ones_bf, rhs=sq[s_i][:, :n], start=(kc == 0), stop=(kc == KC - 1)),
                     reads=[Bsq[s_i], Bcst], writes=[BPS[7]])
            r_i = nxt("rs", 2)
            r = rs[r_i]
            c.op("act", lambda h: h.activation(out=r[:, :n], in_=ps[:, 7, :n], func=AF.Ln, scale=1.0 / D, bias=CST[:, 90:91]),
                 reads=[BPS[7], Bcst], writes=[Brs[r_i]])
            c.op("act", lambda h: h.activation(out=r[:, :n], in_=r[:, :n], func=AF.Exp, scale=-0.5), reads=[Brs[r_i]], writes=[Brs[r_i]])
            for kc in range(KC):
                c.op("dve", lambda h, kc=kc: h.scalar_tensor_tensor(out=XN[:, kc, t0:t0 + n], in0=H[:, kc, t0:t0 + n], scalar=gains[:, gi, kc:kc + 1],
                                                                in1=r[:, :n], op0=ALU.mult, op1=ALU.mult),
                     reads=[BH[kc][ti], Brs[r_i], Bcst], writes=[BXN[kc][ti]])

        def ffn(n_ffn, next_gi=None, tail=None):
            B = {}
            for s_ in range(NSLOT):
                B["slot%d" % s_] = c.rbuf("ffn.slot%d" % s_, s_ * 24576, 24576)
            for a_ in range(2):
                for j_ in range(4):
                    B["act%d_%d" % (a_, j_)] = c.rbuf("ffn.act%d_%d" % (a_, j_), 73728 + 4096 * a_ + 1024 * j_, 1024)
                B["sg%d" % a_] = c.rbuf("ffn.sg%d" % a_, 81920 + 2048 * a_, 2048)
            slot_g, slot_u, slot_d = [], [], []
            for s in range(NSLOT):
                base = OFF_A + s * 24576
                slot_g.append(carve(base, 8192, BF16).rearrange("p (j k c) -> p j k c", j=4, k=KC))
                slot_u.append(carve(base + 8192, 8192, BF16).rearrange("p (j k c) -> p j k c", j=4, k=KC))
                slot_d.append(carve(base + 16384, 8192, BF16).rearrange("p (j m) -> p j m", j=4))
            actb = [carve(OFF_A + 73728 + 4096 * a, 4096, BF16).rearrange("p (j n) -> p j n", j=4) for a in range(2)]
            sgb = [carve(OFF_A + 81920 + 2048 * a, 2048) for a in range(2)]

            def load_slab(si):
                j0, S = SLABS[si]
                s = si % NSLOT
                c.dma_group("pool", "ws%d" % s,
                            [(slot_g[s][:, 0:S], wg[n_ffn, :, j0:j0 + S]),
                             (slot_u[s][:, 0:S], wu[n_ffn, :, j0:j0 + S]),
                             (slot_d[s][:, 0:S], wd[n_ffn, :, j0:j0 + S])],
                            writes=[B["slot%d" % s]])

            for si in range(NSLOT):
                load_slab(si)
            rmsnorm(n_ffn)
            steps = [(si, ti) for si in range(len(SLABS)) for ti in range(5)]

            def gu(idx):
                si, ti = steps[idx]
                j0, S = SLABS[si]
                s = si % NSLOT
                t0, n = TT[ti]
                ab = idx % 2
                for j in range(S):
                    p = nxt("gu", 2)

                    def mm(h, j=j, p=p):
                        for kc in range(KC):
                            h.matmul(ps[:, 2 * p, :n], lhsT=slot_g[s][:, j, kc, :], rhs=XN[:, kc, t0:t0 + n], start=(kc == 0), stop=(kc == KC - 1))
                        for kc in range(KC):
                            ins = h.matmul(ps[:, 2 * p + 1, :n], lhsT=slot_u[s][:, j, kc, :], rhs=XN[:, kc, t0:t0 + n], start=(kc == 0), stop=(kc == KC - 1))
                        return ins
                    c.op("pe", mm, reads=[B["slot%d" % s]] + [BXN[kc][ti] for kc in range(KC)], writes=[BPS[2 * p], BPS[2 * p + 1]])
                    c.op("act", lambda h, p=p: h.activation(out=sgb[p][:, :n], in_=ps[:, 2 * p, :n], func=AF.Silu),
                         reads=[BPS[2 * p]], writes=[B["sg%d" % p]])
                    c.op("dve", lambda h, p=p, j=j: h.tensor_tensor(out=actb[ab][:, j, :n], in0=sgb[p][:, :n], in1=ps[:, 2 * p + 1, :n], op=ALU.mult),
                         reads=[B["sg%d" % p], BPS[2 * p + 1]], writes=[B["act%d_%d" % (ab, j)]])

            def down(idx):
                si, ti = steps[idx]
                j0, S = SLABS[si]
                s = si % NSLOT
                t0, n = TT[ti]
                ab = idx % 2
                for m in range(KC):
                    bk = 4 + nxt("dn", 3)

                    def mm(h, m=m, bk=bk):
                        for j in range(S):
                            ins = h.matmul(ps[:, bk, :n], lhsT=slot_d[s][:, j, m * 128:(m + 1) * 128], rhs=actb[ab][:, j, :n], start=(j == 0), stop=(j == S - 1))
                        return ins
                    c.op("pe", mm, reads=[B["slot%d" % s]] + [B["act%d_%d" % (ab, j)] for j in range(S)], writes=[BPS[bk]])
                    c.op("dve", lambda h, m=m, bk=bk: h.scalar_tensor_tensor(out=H[:, m, t0:t0 + n], in0=ps[:, bk, :n], scalar=0.5, in1=H[:, m, t0:t0 + n],
                                                                         op0=ALU.mult, op1=ALU.add),
                         reads=[BPS[bk], BH[m][ti]], writes=[BH[m][ti]])
                if ti == 4 and si + NSLOT < len(SLABS):
                    load_slab(si + NSLOT)
                if next_gi is not None and si == len(SLABS) - 1:
                    _norm_tile(next_gi, ti, t0, n)
                if tail is not None and si == len(SLABS) - 1:
                    tail(ti)

            for idx in range(len(steps)):
                gu(idx)
                if idx > 0:
                    down(idx - 1)
            down(len(steps) - 1)
            pending["gi"] = next_gi

        def attention(gi, next_gi=None, tail=None):
            A = OFF_A
            B = {}

            def rb(name, off, size, grp=None):
                B[name] = c.rbuf("att." + name, off, size, grp and "att." + grp)
            Vd = carve(A + 0, 17408, BF16).rearrange("p (b k d) -> p b k d", b=17, k=4)
            for i in range(17):
                rb("V%d" % i, 1024 * i, 1024)
            kTa = carve(A + 17408, 4128, BF16)
            rb("kTa", 17408, 4128)
            PT_off = [21536, 22560] + [45664 + 1024 * i for i in range(5)] + [100000]
            PT = [carve(A + o, 1024, BF16) for o in PT_off]
            for i, o in enumerate(PT_off):
                rb("PT%d" % i, o, 1024)
            qT = carve(A + 24576, 8256, BF16).rearrange("p (c t) -> p c t", c=2)
            rb("qT0", 24576, 4128)
            rb("qT1", 24576 + 4128, 4128)
            kTb = carve(A + 32832, 4128, BF16)
            rb("kTb", 32832, 4128)
            Bband = carve(A + 36960, 6144).rearrange("p (r x) -> p r x", r=3)
            Bmeta = carve(A + 43104, 2048)
            Bm0 = carve(A + 45152, 256)
            Bmm = carve(A + 45408, 256)
            rb("bias", 36960, 8704)
            tmp = [carve(A + 50784 + 2048 * i, 2048) for i in range(2)]
            rd = [carve(A + 54880 + 2048 * i, 2048) for i in range(2)] + [carve(A + 101024, 2048)]
            rb("rd2", 101024, 2048)
            wq_sb = [carve(A + 58976 + 4096 * i, 4096, BF16).rearrange("p (k c) -> p k c", k=KC) for i in range(2)]
            wk_sb = [carve(A + 67168 + 2048 * i, 2048, BF16).rearrange("p (k c) -> p k c", k=KC) for i in range(2)]
            for i in range(2):
                rb("tmp%d" % i, 50784 + 2048 * i, 2048)
                rb("rd%d" % i, 54880 + 2048 * i, 2048)
                rb("wq%d" % i, 58976 + 4096 * i, 4096)
                rb("wk%d" % i, 67168 + 2048 * i, 2048)
            oT = carve(A + 71264, 8256, BF16).rearrange("p (c t) -> p c t", c=2)
            for cc_ in range(2):
                for t_ in range(5):
                    rb("oT%d_%d" % (cc_, t_), 71264, 8256, grp="oT")
            wo_sb = carve(A + 79520, 16384, BF16).rearrange("p (k c) -> p k c", k=KC)
            rb("wo", 79520, 16384)
            wv_sb = carve(A + 95904, 4096, BF16).rearrange("p (k c) -> p k c", k=KC)
            rb("wv", 95904, 4096)
            NPT = len(PT)

            c.dma_group("pool", "mw", [(wv_sb, wv)], writes=[B["wv"]])
            c.op("dve", lambda h: h.memset(Vd[:, :, :, 64:128], 1.0), writes=[B["V%d" % i] for i in range(17)])
            c.op("dve", lambda h: h.memset(kTa[64:128, :], 0.0), writes=[B["kTa"]])
            c.op("dve", lambda h: h.memset(kTb[0:64, :], 0.0), writes=[B["kTb"]])

            def load_k(k):
                i = k % 2
                c.dma_group("pool", "mq%d" % i, [(wq_sb[i], wq[:, :, k * 256:(k + 1) * 256]), (wk_sb[i], wk2[:, :, k, :])],
                            writes=[B["wq%d" % i], B["wk%d" % i]])
            load_k(0)
            load_k(1)
            c.dma_group("pool", "mo", [(wo_sb, wo)], writes=[B["wo"]])
            rmsnorm(gi)
            def vproj(blk):
                ks, nk = (0, 16) if blk == 0 else (16 + 128 * (blk - 1), 128)
                ti = tile_of(ks)
                bk = nxt("r3", 3)

                def mm(h):
                    for kc in range(KC):
                        ins = h.matmul(ps[:nk, bk, 0:256], lhsT=XN[:, kc, ks:ks + nk], rhs=wv_sb[:, kc, :], start=(kc == 0), stop=(kc == KC - 1))
                    return ins
                c.op("pe", mm, reads=[B["wv"]] + [BXN[kc][ti] for kc in range(KC)], writes=[BPS[bk]])
                src = ps[:nk, bk, 0:256].rearrange("p (k d) -> p k d", k=4)
                c.op("act", lambda h: h.activation(out=Vd[:nk, blk, :, 0:64], in_=src, func=AF.Copy),
                     reads=[BPS[bk]], writes=[B["V%d" % blk]])
            for blk in range(17):
                vproj(blk)
            if ATT_SUB <= 1:
                return

            def proj(k, wi, ti, t0, n, which):
                bk = nxt("r3", 3)
                if which < 2:
                    def mm(h):
                        for kc in range(KC):
                            ins = h.matmul(ps[:, bk, :n], lhsT=wq_sb[wi][:, kc, which * 128:(which + 1) * 128], rhs=XN[:, kc, t0:t0 + n],
                                           start=(kc == 0), stop=(kc == KC - 1))
                        return ins
                    wb = B["wq%d" % wi]
                else:
                    def mm(h):
                        for kc in range(KC):
                            ins = h.matmul(ps[:, bk, :n], lhsT=wk_sb[wi][:, kc, :], rhs=XN[:, kc, t0:t0 + n], start=(kc == 0), stop=(kc == KC - 1))
                        return ins
                    wb = B["wk%d" % wi]
                c.op("pe", mm, reads=[wb] + [BXN[kc][ti] for kc in range(KC)], writes=[BPS[bk]])
                sqi = nxt("sqp", 2)
                sbk = (7, 3)[sqi]
                c.op("act", lambda h: h.activation(out=sq[sqi][:, :n], in_=ps[:, bk, :n], func=AF.Square), reads=[BPS[bk]], writes=[Bsq[sqi]])
                c.op("pe", lambda h: h.matmul(ps[:, sbk, :n], lhsT=bd_ones, rhs=sq[sqi][:, :n], start=True, stop=True), reads=[Bsq[sqi], Bcst], writes=[BPS[sbk]])
                r_i = nxt("rs", 2)
                r = rs[r_i]
                if which < 2:
                    c.op("act", lambda h: h.activation(out=r[:, :n], in_=ps[:, sbk, :n], func=AF.Ln, bias=CST[:, 91:92]),
                         reads=[BPS[sbk], Bcst], writes=[Brs[r_i]])
                else:
                    c.op("act", lambda h: h.activation(out=r[:, :n], in_=ps[:, sbk, :n], func=AF.Ln, scale=1.0 / 64.0, bias=CST[:, 90:91]),
                         reads=[BPS[sbk], Bcst], writes=[Brs[r_i]])
                c.op("act", lambda h: h.activation(out=r[:, :n], in_=r[:, :n], func=AF.Exp, scale=-0.5), reads=[Brs[r_i]], writes=[Brs[r_i]])
                if which < 2:
                    c.op("dve", lambda h: h.scalar_tensor_tensor(out=qT[:, which, t0:t0 + n], in0=ps[:, bk, :n], scalar=qkg[:, 0:1],
                                                             in1=r[:, :n], op0=ALU.mult, op1=ALU.mult),
                         reads=[BPS[bk], Brs[r_i], Bcst], writes=[B["qT%d" % which]])
                else:
                    c.op("dve", lambda h: h.scalar_tensor_tensor(out=kTa[0:64, t0:t0 + n], in0=ps[0:64, bk, :n], scalar=qkg[0:64, 1:2],
                                                             in1=r[0:64, :n], op0=ALU.mult, op1=ALU.mult),
                         reads=[BPS[bk], Brs[r_i], Bcst], writes=[B["kTa"]])
                    c.op("dve", lambda h: h.scalar_tensor_tensor(out=kTb[64:128, t0:t0 + n], in0=ps[64:128, bk, :n], scalar=qkg[64:128, 1:2],
                                                             in1=r[64:128, :n], op0=ALU.mult, op1=ALU.mult),
                         reads=[BPS[bk], Brs[r_i], Bcst], writes=[B["kTb"]])

            def kg_scores(k, qs, nq, W, blk, ks, nk, bias_ap, cq):
                bk = nxt("r4", 4)

                def mm(h):
                    for hh in range(2):
                        ins = h.matmul(ps[:nk, bk, hh * 2 * nq:(hh + 1) * 2 * nq].rearrange("p (c n) -> p c n", c=2),
                                       lhsT=(kTa if hh == 0 else kTb)[:, ks:ks + nk],
                                       rhs=qT[:, :, qs:qs + nq], start=True, stop=True)
                    return ins
                c.op("pe", mm, reads=[B["kTa"], B["kTb"], B["qT0"], B["qT1"]], writes=[BPS[bk]])
                tb = nxt("tmp", 2)
                if cq is None:
                    c.op("dve", lambda h: h.tensor_tensor(out=tmp[tb][:nk, :W], in0=ps[:nk, bk, :W], in1=bias_ap, op=ALU.add),
                         reads=[BPS[bk], B["bias"]], writes=[B["tmp%d" % tb]])
                else:
                    c.op("dve", lambda h: h.tensor_tensor(out=tmp[tb][:nk, :W].rearrange("p (g n) -> p g n", g=4),
                                                          in0=ps[:nk, bk, :W].rearrange("p (g n) -> p g n", g=4),
                                                          in1=cbias[:nk, 4 * k:4 * k + 4, cq:cq + 1].to_broadcast([nk, 4, nq]), op=ALU.add),
                         reads=[BPS[bk], Bcst], writes=[B["tmp%d" % tb]])
                    c.op("dve", lambda h: h.tensor_tensor(out=tmp[tb][:nk, :W], in0=tmp[tb][:nk, :W], in1=bias_ap, op=ALU.add),
                         reads=[B["tmp%d" % tb], B["bias"]], writes=[B["tmp%d" % tb]])
                pt = nxt("pt", NPT)
                c.op("act", lambda h: h.activation(out=PT[pt][:nk, :W], in_=tmp[tb][:nk, :W], func=AF.Exp),
                     reads=[B["tmp%d" % tb]], writes=[B["PT%d" % pt]])
                return (blk, nk, pt)

            def qgroup_scores(k, qb, qs, nq):
                W = 4 * nq
                if qb < 0:
                    kgs = [(0, 0, 16, Bmm[0:16, 0:W], None), (1, 16, 128, Bm0[:, 0:W], None)]
                else:
                    kgs = [(0, 0, 16, Bmeta[0:16, :], qb)]
                    for r_ in (-1, 0, 1):
                        if 0 <= qb + r_ <= 15:
                            kgs.append((qb + r_ + 1, 16 + 128 * (qb + r_), 128, Bband[:, r_ + 1, :], None))
                parts = [kg_scores(k, qs, nq, W, blk, ks, nk, bias_ap, cq) for (blk, ks, nk, bias_ap, cq) in kgs]
                return (k, qb, qs, nq, parts)

            def qgroup_pv(state):
                k, qb, qs, nq, parts = state
                W = 4 * nq
                bA = 4 + nxt("pv", 3)
                oti = tile_of(qs)

                def mm2(h):
                    for i, (blk, nk, pt) in enumerate(parts):
                        ins = h.matmul(ps[:, bA, :W], lhsT=Vd[:nk, blk, k, :], rhs=PT[pt][:nk, :W], start=(i == 0), stop=(i == len(parts) - 1))
                    return ins
                c.op("pe", mm2, reads=[B["V%d" % blk] for (blk, nk, pt) in parts] + [B["PT%d" % pt] for (blk, nk, pt) in parts],
                     writes=[BPS[bA]])
                ri = nxt("rd", 3)
                c.op("dve", lambda h: h.tensor_tensor(out=rd[ri][64:128, :W].rearrange("p (g n) -> p g n", g=4),
                                                      in0=ps[64:128, bA, :W].rearrange("p (g n) -> p g n", g=4),
                                                      in1=es3[64:128, 4 * k:4 * k + 4, :].to_broadcast([64, 4, nq]), op=ALU.add),
                     reads=[BPS[bA], Bcst], writes=[B["rd%d" % ri]])
                c.op("act", lambda h: h.activation(out=rd[ri][64:128, :W], in_=rd[ri][64:128, :W], func=AF.Ln), reads=[B["rd%d" % ri]], writes=[B["rd%d" % ri]])
                c.op("act", lambda h: h.activation(out=rd[ri][64:128, :W], in_=rd[ri][64:128, :W], func=AF.Exp, scale=-1.0), reads=[B["rd%d" % ri]], writes=[B["rd%d" % ri]])
                return (qs, nq, W, bA, ri, oti)

            def qgroup_fin(fin):
                qs, nq, W, bA, ri, oti = fin
                for hh in range(2):
                    lo = 64 * hh
                    c.op("dve", lambda h, hh=hh, lo=lo: h.tensor_tensor(
                        out=oT[lo:lo + 64, :, qs:qs + nq],
                        in0=ps[0:64, bA, :W].rearrange("p (h c n) -> p h c n", c=2, h=2)[:, hh, :, :],
                        in1=rd[ri][64:128, :W].rearrange("p (h c n) -> p h c n", c=2, h=2)[:, hh, :, :], op=ALU.mult),
                        reads=[BPS[bA], B["rd%d" % ri]], writes=[B["oT0_%d" % oti], B["oT1_%d" % oti]])

            def oproj(k, ti, t0, n):
                for m in range(KC):
                    bk = nxt("r3", 3)

                    def mm(h, m=m, bk=bk):
                        for cc in range(2):
                            ins = h.matmul(ps[:, bk, :n], lhsT=wo_sb[:, 2 * k + cc, m * 128:(m + 1) * 128], rhs=oT[:, cc, t0:t0 + n], start=(cc == 0), stop=(cc == 1))
                        return ins
                    c.op("pe", mm, reads=[B["wo"], B["oT0_%d" % ti], B["oT1_%d" % ti]], writes=[BPS[bk]])
                    c.op("dve", lambda h, m=m, bk=bk: h.tensor_tensor(out=H[:, m, t0:t0 + n], in0=ps[:, bk, :n], in1=H[:, m, t0:t0 + n], op=ALU.add),
                         reads=[BPS[bk], BH[m][ti]], writes=[BH[m][ti]])
                if k == 3 and next_gi is not None:
                    _norm_tile(next_gi, ti, t0, n)
                if k == 3 and tail is not None:
                    tail(ti)

            def att_k(k):
                wi = k % 2
                c.dma_group("sp", "tb", [(Bband.rearrange("p r x -> p (r x)"), bband[k].rearrange("p r g a -> p (r g a)")),
                                         (Bmeta[0:16, :], bmeta[k].rearrange("p g a -> p (g a)")),
                                         (Bm0, bm0[k].rearrange("p g a -> p (g a)")),
                                         (Bmm[0:16, :], bmm[k].rearrange("p g a -> p (g a)"))], writes=[B["bias"]])
                for ti, (t0, n) in enumerate(TT):
                    for which in range(3):
                        proj(k, wi, ti, t0, n, which)
                if k + 2 < 4:
                    load_k(k + 2)
                if ATT_SUB <= 2:
                    return
                prev = None
                fin = None
                for (qb, qs, nq) in [(-1, 0, 16)] + [(qb, 16 + 128 * qb, 128) for qb in range(16)]:
                    stt = qgroup_scores(k, qb, qs, nq)
                    nfin = qgroup_pv(prev) if prev is not None else None
                    if fin is not None:
                        qgroup_fin(fin)
                    fin = nfin
                    prev = stt
                nfin = qgroup_pv(prev)
                if fin is not None:
                    qgroup_fin(fin)
                qgroup_fin(nfin)
                if ATT_SUB <= 4:
                    return
                for ti, (t0, n) in enumerate(TT):
                    oproj(k, ti, t0, n)

            for k in range(4):
                att_k(k)
            if ATT_SUB >= 9:
                pending["gi"] = next_gi

        def poolmix(gi, next_gi=None, tail=None):
            A = OFF_A
            B = {}
            wpi_sb = carve(A, 16384, BF16).rearrange("p (k c) -> p k c", k=KC)
            B["wpi"] = c.rbuf("pool.wpi", 0, 16384)
            Us = [carve(A + 16384, 8320), carve(A + 94848, 8320)]
            B["U0"] = c.rbuf("pool.U0", 16384, 8320)
            B["U1"] = c.rbuf("pool.U1", 94848, 8320)
            T = [carve(A + 86528, 8320), carve(A + 24704, 8320)]
            B["T0"] = c.rbuf("pool.T0", 86528, 8320)
            B["T1"] = c.rbuf("pool.T1", 24704, 8320)
            wpg_sb = carve(A + 33024, 4096, BF16).rearrange("p (g k c) -> p g k c", g=4, k=2)
            B["wpg"] = c.rbuf("pool.wpg", 33024, 4096)
            wpo_sb = carve(A + 37120, 16384, BF16).rearrange("p (k c) -> p k c", k=KC)
            B["wpo"] = c.rbuf("pool.wpo", 37120, 16384)
            pooled = carve(A + 53504, 33024, BF16).rearrange("p (k t) -> p k t", k=KC)
            for i in range(KC):
                B["pl%d" % i] = c.rbuf("pool.pl%d" % i, 53504 + 4128 * i, 4128)
            mixed = carve(A + 86528, 8192, BF16).rearrange("p (k n) -> p k n", k=KC)
            BT = [B["T0"], B["T1"]]
            c.dma_group("pool", "mw", [(wpi_sb, wpi)], writes=[B["wpi"]])
            c.dma_group("pool", "mo", [(wpg_sb, wpg), (wpo_sb, wpo)], writes=[B["wpg"], B["wpo"]])
            rmsnorm(gi)
            for ui in range(2):
                c.op("dve", lambda h, ui=ui: h.memset(Us[ui][:, 0:8], 0.0), writes=[B["U%d" % ui]])
                c.op("dve", lambda h, ui=ui: h.memset(Us[ui][:, 2072:2080], 0.0), writes=[B["U%d" % ui]])
            def chunk(ch):
                g = ch // 2
                hw = 1 << g
                U = Us[ch % 2]
                BU = B["U%d" % (ch % 2)]
                for ti, (t0, n) in enumerate(TT):
                    bk = nxt("r3", 3)

                    def mm(h, bk=bk, t0=t0, n=n):
                        for kc in range(KC):
                            ins = h.matmul(ps[:, bk, :n], lhsT=wpi_sb[:, kc, ch * 128:(ch + 1) * 128], rhs=XN[:, kc, t0:t0 + n], start=(kc == 0), stop=(kc == KC - 1))
                        return ins
                    c.op("pe", mm, reads=[B["wpi"]] + [BXN[kc][ti] for kc in range(KC)], writes=[BPS[bk]])
                    c.op("act", lambda h, bk=bk, t0=t0, n=n: h.activation(out=U[:, 8 + t0:8 + t0 + n], in_=ps[:, bk, :n], func=AF.Copy),
                         reads=[BPS[bk]], writes=[BU])
                src, bsrc = U, BU
                s_ = 1
                lvl = 0
                while s_ <= hw:
                    ln = 2080 - 2 * s_ + 1
                    dst, bdst = T[lvl % 2], BT[lvl % 2]
                    c.op("dve", lambda h, src=src, dst=dst, s_=s_, ln=ln: h.tensor_tensor(out=dst[:, 0:ln], in0=src[:, 0:ln], in1=src[:, s_:s_ + ln], op=ALU.add),
                         reads=[bsrc], writes=[bdst])
                    src, bsrc = dst, bdst
                    s_ *= 2
                    lvl += 1
                w_ = 2 * hw
                c.op("dve", lambda h: h.scalar_tensor_tensor(out=pooled[:, ch, :], in0=src[:, 8 - hw:8 - hw + L], scalar=1.0 / w_, in1=U[:, 8:8 + L],
                                                         op0=ALU.mult, op1=ALU.subtract),
                     reads=[bsrc, BU], writes=[B["pl%d" % ch]])
                c.op("dve", lambda h: h.tensor_tensor(out=sm[:, 0:hw], in0=src[:, 8 - hw:8], in1=invc[:, g, 0:hw], op=ALU.mult),
                     reads=[bsrc, Bcst], writes=[Bsm])
                c.op("dve", lambda h: h.tensor_tensor(out=pooled[:, ch, 0:hw], in0=sm[:, 0:hw], in1=U[:, 8:8 + hw], op=ALU.subtract),
                     reads=[Bsm, BU], writes=[B["pl%d" % ch]])
                if hw > 1:
                    nr = hw - 1
                    tb_ = L - hw + 1
                    c.op("dve", lambda h: h.tensor_tensor(out=sm[:, 16:16 + nr], in0=src[:, 8 + tb_ - hw:8 + tb_ - hw + nr],
                                                          in1=invc[:, g, 8:8 + nr], op=ALU.mult),
                         reads=[bsrc, Bcst], writes=[Bsm])
                    c.op("dve", lambda h: h.tensor_tensor(out=pooled[:, ch, tb_:tb_ + nr], in0=sm[:, 16:16 + nr], in1=U[:, 8 + tb_:8 + tb_ + nr], op=ALU.subtract),
                         reads=[Bsm, BU], writes=[B["pl%d" % ch]])

            def mixtile(ti, t0, n):
                for g in range(4):
                    for mm_ in range(2):
                        bk = nxt("r3", 3)
                        oc = 2 * g + mm_

                        def mm(h, g=g, mm_=mm_, bk=bk):
                            for kk in range(2):
                                ins = h.matmul(ps[:, bk, :n], lhsT=wpg_sb[:, g, kk, mm_ * 128:(mm_ + 1) * 128], rhs=pooled[:, 2 * g + kk, t0:t0 + n], start=(kk == 0), stop=(kk == 1))
                            return ins
                        c.op("pe", mm, reads=[B["wpg"], B["pl%d" % (2 * g)], B["pl%d" % (2 * g + 1)]], writes=[BPS[bk]])
                        c.op("dve", lambda h, oc=oc, bk=bk: h.tensor_scalar(out=mixed[:, oc, :n], in0=ps[:, bk, :n], scalar1=gains[:, 6, oc:oc + 1], scalar2=None, op0=ALU.mult),
                             reads=[BPS[bk], Bcst], writes=[B["T0"]])
                for m in range(KC):
                    bk = nxt("r3", 3)

                    def mm(h, m=m, bk=bk):
                        for cc in range(KC):
                            ins = h.matmul(ps[:, bk, :n], lhsT=wpo_sb[:, cc, m * 128:(m + 1) * 128], rhs=mixed[:, cc, :n], start=(cc == 0), stop=(cc == KC - 1))
                        return ins
                    c.op("pe", mm, reads=[B["wpo"], B["T0"]], writes=[BPS[bk]])
                    c.op("dve", lambda h, m=m, bk=bk: h.tensor_tensor(out=H[:, m, t0:t0 + n], in0=ps[:, bk, :n], in1=H[:, m, t0:t0 + n], op=ALU.add),
                         reads=[BPS[bk], BH[m][ti]], writes=[BH[m][ti]])
                if next_gi is not None:
                    _norm_tile(next_gi, ti, t0, n)
                if tail is not None:
                    tail(ti)

            for ch in range(KC):
                chunk(ch)
            for ti, (t0, n) in enumerate(TT):
                mixtile(ti, t0, n)
            pending["gi"] = next_gi

        phases_sel = phases
        plist = [("ffn", 0, 0), ("att", 4, 4), ("ffn", 1, 1), ("ffn", 2, 2), ("pool", 5, 5), ("ffn", 3, 3)]
        sel = list(phases_sel if phases_sel is not None else range(min(stage, 6)))

        def load_tile(seq, ti, queue="sp"):
            t0, n = TT[ti]
            items = []
            if t0 < NM:
                items.append((H[:, :, 0:NM], metaT[seq]))
            r0, r1 = max(t0, NM), t0 + n
            items.append((H[:, :, r0:r1], xT[seq, :, KC * (r0 - NM):KC * (r1 - NM)].rearrange("p (k n) -> p k n", k=KC)))
            c.dma_group(queue, ("xin%d" if queue == "sp" else "xinp%d") % ti, items, writes=[BH[kc][ti] for kc in range(KC)])

        def store_tile(seq, ti):
            t0, n = TT[ti]
            r0, r1 = max(t0, NM), t0 + n
            c.dma_group("sp", "out%d" % ti, [(outT[seq, :, KC * (r0 - NM):KC * (r1 - NM)].rearrange("p (k n) -> p k n", k=KC), H[:, :, r0:r1])],
                        reads=[BH[kc][ti] for kc in range(KC)])

        for ti in range(5):
            load_tile(0, ti)
        for seq in range(2):
            if seq == 0:
                pending["gi"] = None

            def tail(ti, seq=seq):
                store_tile(seq, ti)
                if seq == 0:
                    load_tile(1, ti, "pool")
                    if len(sel) > 0:
                        _norm_tile(plist[sel[0]][2], ti, TT[ti][0], TT[ti][1])
            for ii, pi in enumerate(sel):
                kind, arg, g_i = plist[pi]
                last = (ii + 1 == len(sel))
                nx = plist[sel[ii + 1]][2] if not last else None
                tl = tail if last else None
                if kind == "ffn":
                    ffn(arg, nx, tl)
                elif kind == "att":
                    attention(arg, nx, tl)
                else:
                    poolmix(arg, nx, tl)
            if seq == 0 and len(sel) > 0:
                pending["gi"] = plist[sel[0]][2]
        c.wait_all("sp", allH)
        c.replay(block)
    return nc


def _const_tables():
    slopes = np.array([2.0 ** (-8.0 * (h + 1) / 16.0) for h in range(16)], dtype=np.float64).reshape(4, 4)
    a = np.arange(128)
    bband = np.zeros((4, 128, 3, 4, 128), np.float32)
    for r in range(3):
        rel = (r - 1) * 128 + a[:, None] - a[None, :]
        valid = np.abs(rel) <= 128
        for k in range(4):
            for g in range(4):
                bband[k, :, r, g, :] = np.where(valid, -slopes[k, g] * np.abs(rel), NEG)
    m = np.arange(16)
    bmeta = np.zeros((4, 16, 4, 128), np.float32)
    bmm = np.zeros((4, 16, 4, 16), np.float32)
    bm0 = np.zeros((4, 128, 4, 16), np.float32)
    for k in range(4):
        for g in range(4):
            bmeta[k, :, g, :] = -slopes[k, g] * (16 + a[None, :] - m[:, None])
            bmm[k, :, g, :] = -slopes[k, g] * np.abs(m[None, :] - m[:, None])
            dist = 16 + a[:, None] - m[None, :]
            bm0[k, :, g, :] = np.where(dist <= 128, -slopes[k, g] * dist, NEG)
    invc = np.ones((4, 16), np.float32)
    for g in range(4):
        hw = 1 << g
        for t in range(hw):
            invc[g, t] = 1.0 / (t + hw)
        for i in range(hw - 1):
            invc[g, 8 + i] = 1.0 / (2 * hw - 1 - i)
    perm = [0, 2, 1, 3]
    bband = np.ascontiguousarray(bband[:, :, :, perm, :])
    bmeta = np.ascontiguousarray(bmeta[:, :, perm, :])
    bmm = np.ascontiguousarray(bmm[:, :, perm, :])
    bm0 = np.ascontiguousarray(bm0[:, :, perm, :])
    sl_p = slopes[:, perm].reshape(16, 1)
    cb = np.zeros((128, 16, 16), np.float32)
    cb[:] = (-sl_p * 128.0 * np.arange(16)[None, :])[None]
    return bband, bmeta, bm0, bmm, invc, cb.reshape(128, 256)


def _fm(v):
    return np.ascontiguousarray(v.reshape(KC, 128).T)


def _wfm(w):
    return np.ascontiguousarray(w.reshape(KC, 128, -1).transpose(1, 0, 2))


_CACHE = {}


def _real_ranges():
    return [(max(t0, NM) - NM, t0 + n - NM) for (t0, n) in TT]


def _pack_x(xs):
    out = np.empty((2, 128, KC * SEQ), np.float32)
    for (r0, r1) in _real_ranges():
        ln = r1 - r0
        blk = xs[:, r0:r1, :].reshape(2, ln, KC, 128).transpose(0, 3, 2, 1)
        out[:, :, KC * r0:KC * r1] = blk.reshape(2, 128, KC * ln)
    return out


def _unpack_out(o):
    out = np.empty((2, SEQ, D), np.float32)
    for (r0, r1) in _real_ranges():
        ln = r1 - r0
        blk = o[:, :, KC * r0:KC * r1].reshape(2, 128, KC, ln)
        out[:, r0:r1, :] = blk.transpose(0, 3, 2, 1).reshape(2, ln, D)
    return out


def kernel(x, meta_tokens, ffn_norm, w_gate_up, w_down, mixer_norm, w_qkv, q_norm, k_norm,
           sink_logit, w_o, w_pool_in, w_pool_group, pool_scale, w_pool_out):
    f = np.float32
    x = np.asarray(x, f)
    bband, bmeta, bm0, bmm, invc, cbias = _const_tables()
    wg = np.empty((4, 128, NJ, KC, 128), f)
    wu = np.empty((4, 128, NJ, KC, 128), f)
    wd = np.empty((4, 128, NJ, D), f)
    for i in range(2):
        for ff in range(2):
            n = 2 * i + ff
            wgu = np.asarray(w_gate_up[i, ff], f)
            wg[n] = wgu[:, :2816].reshape(KC, 128, NJ, 128).transpose(1, 2, 0, 3)
            wu[n] = wgu[:, 2816:].reshape(KC, 128, NJ, 128).transpose(1, 2, 0, 3)
            wd[n] = np.asarray(w_down[i, ff], f).reshape(NJ, 128, D).transpose(1, 0, 2)
    cst = np.zeros((128, 288), f)
    gl = [ffn_norm[0, 0], ffn_norm[0, 1], ffn_norm[1, 0], ffn_norm[1, 1], mixer_norm[0], mixer_norm[1], pool_scale[0]]
    for gi, v in enumerate(gl):
        cst[:, gi * 8:(gi + 1) * 8] = _fm(np.asarray(v, f))
    cst[:, 56] = np.tile(np.asarray(q_norm[0], f), 2)
    cst[:, 57] = np.tile(np.asarray(k_norm[0], f), 2)
    cst[:, 58:74] = np.asarray(sink_logit[0], f).reshape(4, 4)[:, [0, 2, 1, 3]].reshape(1, 16)
    cst[:, 90] = EPS
    cst[:, 91] = 64.0 * EPS
    cst[:, 224:288] = invc.reshape(1, 64)
    wqkv = np.asarray(w_qkv[0], f)
    wq = _wfm(wqkv[:, :1024])
    wk = wqkv[:, 1024:1280].reshape(KC, 128, 4, 64).transpose(1, 0, 2, 3)
    wk2 = np.ascontiguousarray(np.concatenate([wk, wk], axis=3))
    wv = _wfm(wqkv[:, 1280:1536])
    wo = _wfm(np.asarray(w_o[0], f))
    wpi = _wfm(np.asarray(w_pool_in[0], f))
    wpg = np.ascontiguousarray(np.asarray(w_pool_group[0], f).reshape(4, 2, 128, 256).transpose(2, 0, 1, 3))
    wpo = _wfm(np.asarray(w_pool_out[0], f))
    metaT1 = np.ascontiguousarray(np.asarray(meta_tokens, f).T.reshape(KC, 128, NM).transpose(1, 0, 2))
    metaT = np.ascontiguousarray(np.stack([metaT1, metaT1]))
    shared = dict(metaT=metaT, wg=wg, wu=wu, wd=wd, cst=cst, wq=wq, wk2=wk2, wv=wv, wo=wo, wpi=wpi, wpg=wpg, wpo=wpo,
                  bband=bband, bmeta=bmeta, bm0=bm0, bmm=bmm, cbias=cbias)
    in_maps = []
    for core in range(8):
        xs = x[2 * core:2 * core + 2]
        d = dict(shared)
        d["xT"] = _pack_x(xs)
        in_maps.append(d)
    if STAGE not in _CACHE:
        _CACHE[STAGE] = build_program(STAGE)
    nc = _CACHE[STAGE]
    res = run_bass_kernel_spmd(nc, in_maps, core_ids=list(range(8)))
    out = np.empty((16, SEQ, D), f)
    for core in range(8):
        out[2 * core:2 * core + 2] = _unpack_out(np.asarray(res.results[core]["outT"]))
    return out
```
